# Optimizing a Trainium2 kernel written in Bass

```python
import jax
import jax.numpy as jnp
from jax import lax
import numpy as np

D_MODEL = 1024
BATCH = 32
SEQ = 2048
DEPTH = 4

CTX_LEN = 256
GRID_W = 64
N_MIXERS = 4
MLP_HIDDEN = 4 * D_MODEL
N_MOD = 6
NORM_EPS = 1e-6
NEG_INF = -1e30

ATTN_HEADS = 16
ATTN_KV_HEADS = 4
ATTN_GROUP = ATTN_HEADS // ATTN_KV_HEADS
HEAD_DIM = D_MODEL // ATTN_HEADS
WINDOW = 128
ATTN_BLOCK = 128
ROPE_BASE = 10000.0
ROPE_AXIS_DIM = HEAD_DIM // 2
ROPE_FREQS = ROPE_AXIS_DIM // 2

GLA_HEADS = 4
GLA_KEY_DIM = D_MODEL // 2
GLA_VAL_DIM = D_MODEL
GLA_DK = GLA_KEY_DIM // GLA_HEADS
GLA_DV = GLA_VAL_DIM // GLA_HEADS
GLA_GATE_RANK = 16
GLA_TAU = 16.0
SCAN_CHUNK = 64

RWKV_HEAD_SIZE = 64
RWKV_HEADS = D_MODEL // RWKV_HEAD_SIZE
RWKV_DECAY_RANK = 64
RWKV_AAA_RANK = 64
RWKV_GATE_RANK = 128
RWKV_LN_EPS = 64e-5
L2_EPS = 1e-12

HGRN_EXPAND = 128
HGRN_HEADS = D_MODEL // HGRN_EXPAND
HGRN_FORGET_DIM = HGRN_HEADS * HGRN_EXPAND
HGRN_IN_DIM = D_MODEL // HGRN_HEADS

N_ATTN_LAYERS = (DEPTH + 3) // N_MIXERS
N_GLA_LAYERS = (DEPTH + 2) // N_MIXERS
N_RWKV_LAYERS = (DEPTH + 1) // N_MIXERS
N_HGRN_LAYERS = DEPTH // N_MIXERS

kernel_name = "hybrid_interleaved_flow_backbone"


def rms_norm(x, gain):
    xf = x.astype(jnp.float32)
    y = xf * lax.rsqrt(jnp.mean(xf * xf, axis=-1, keepdims=True) + NORM_EPS)
    return (y * gain.astype(jnp.float32)).astype(x.dtype)


def modulate(x, gain, shift, scale):
    return rms_norm(x, gain) * (1 + scale) + shift


def channel_mlp(h, w_in, w_out):
    return jnp.square(jax.nn.relu(h @ w_in)) @ w_out


def axial_rope_tables(rows):
    inv_freq = ROPE_BASE ** (-jnp.arange(ROPE_FREQS, dtype=jnp.float32) * 2.0 / ROPE_AXIS_DIM)
    pos = jnp.arange(rows * GRID_W)
    row = (pos // GRID_W).astype(jnp.float32)
    col = (pos % GRID_W).astype(jnp.float32)
    ang = jnp.stack([row[:, None] * inv_freq, col[:, None] * inv_freq], axis=1)
    return jnp.cos(ang), jnp.sin(ang)


def apply_axial_rope(x, cos, sin):
    lead = x.shape[:-1]
    xr = x.reshape(lead + (2, 2, ROPE_FREQS))
    x1, x2 = xr[..., 0, :], xr[..., 1, :]
    bshape = (1, x.shape[1]) + (1,) * (x.ndim - 3) + cos.shape[1:]
    cb = cos.reshape(bshape).astype(x.dtype)
    sb = sin.reshape(bshape).astype(x.dtype)
    out = jnp.stack([x1 * cb - x2 * sb, x2 * cb + x1 * sb], axis=-2)
    return out.reshape(x.shape)


def softmax_with_sink(s, sink):
    sk = sink.astype(jnp.float32)[None, :, :, None, None]
    m = jnp.maximum(jnp.max(s, axis=-1, keepdims=True), sk)
    e = jnp.exp(s - m)
    return e / (jnp.sum(e, axis=-1, keepdims=True) + jnp.exp(sk - m))


def windowed_gqa_sink(h_lat, h_ctx, w_qkv, w_o, sink, cos, sin, need_ctx):
    B, L, _ = h_lat.shape
    nb = L // ATTN_BLOCK
    q_cols = ATTN_HEADS * HEAD_DIM
    kv_cols = ATTN_KV_HEADS * HEAD_DIM

    def project(h):
        T = h.shape[1]
        qkv = h @ w_qkv
        q = qkv[..., :q_cols].reshape(B, T, ATTN_KV_HEADS, ATTN_GROUP, HEAD_DIM) * (HEAD_DIM ** -0.5)
        k = qkv[..., q_cols:q_cols + kv_cols].reshape(B, T, ATTN_KV_HEADS, HEAD_DIM)
        v = qkv[..., q_cols + kv_cols:].reshape(B, T, ATTN_KV_HEADS, HEAD_DIM)
        return q, k, v

    q_lat, k_lat, v_lat = project(h_lat)
    q_ctx, k_ctx, v_ctx = project(h_ctx)
    q_lat = apply_axial_rope(q_lat, cos, sin)
    k_lat = apply_axial_rope(k_lat, cos, sin)
    sink_g = sink.reshape(ATTN_KV_HEADS, ATTN_GROUP)

    def band(t):
        tp = jnp.pad(t, ((0, 0), (ATTN_BLOCK, ATTN_BLOCK), (0, 0), (0, 0)))
        tp = tp.reshape(B, nb + 2, ATTN_BLOCK, ATTN_KV_HEADS, HEAD_DIM)
        return jnp.moveaxis(jnp.concatenate([tp[:, :-2], tp[:, 1:-1], tp[:, 2:]], axis=2), 1, 0)

    k_band, v_band = band(k_lat), band(v_lat)
    q_blocks = jnp.moveaxis(q_lat.reshape(B, nb, ATTN_BLOCK, ATTN_KV_HEADS, ATTN_GROUP, HEAD_DIM), 1, 0)
    n_loc = 3 * ATTN_BLOCK
    rel = jnp.arange(n_loc)[None, :] - ATTN_BLOCK - jnp.arange(ATTN_BLOCK)[:, None]
    in_window = jnp.abs(rel) <= WINDOW

    def attend_block(args):
        q, k, v, blk = args
        key_pos = blk * ATTN_BLOCK - ATTN_BLOCK + jnp.arange(n_loc)
        ok = in_window & ((key_pos >= 0) & (key_pos < L))[None, :]
        s_lat = jnp.einsum('bqhgd,bkhd->bhgqk', q, k).astype(jnp.float32)
        s_lat = jnp.where(ok, s_lat, NEG_INF)
        s_ctx = jnp.einsum('bqhgd,bkhd->bhgqk', q, k_ctx).astype(jnp.float32)
        p = softmax_with_sink(jnp.concatenate([s_lat, s_ctx], axis=-1), sink_g).astype(v.dtype)
        return (jnp.einsum('bhgqk,bkhd->bqhgd', p[..., :n_loc], v)
                + jnp.einsum('bhgqk,bkhd->bqhgd', p[..., n_loc:], v_ctx))

    o_lat = lax.map(attend_block, (q_blocks, k_band, v_band, jnp.arange(nb)))
    o_lat = jnp.moveaxis(o_lat, 0, 1).reshape(B, L, q_cols)
    y_lat = o_lat @ w_o
    if not need_ctx:
        return y_lat, None
    s_c = jnp.einsum('bqhgd,bkhd->bhgqk', q_ctx, k_ctx).astype(jnp.float32)
    p_c = softmax_with_sink(s_c, sink_g).astype(v_ctx.dtype)
    o_ctx = jnp.einsum('bhgqk,bkhd->bqhgd', p_c, v_ctx).reshape(B, h_ctx.shape[1], q_cols)
    return y_lat, o_ctx @ w_o


def reverse_segments(t, n_ctx):
    return jnp.concatenate([t[:, :n_ctx][:, ::-1], t[:, n_ctx:][:, ::-1]], axis=1)


def chunk_gated_scan(q, k, v, log_g):
    B, T, H, dk = q.shape
    dv = v.shape[-1]
    n = T // SCAN_CHUNK

    def chunks(t):
        return jnp.moveaxis(t.astype(jnp.float32).reshape(B, n, SCAN_CHUNK, H, t.shape[-1]), 1, 0)

    lower = jnp.tril(jnp.ones((SCAN_CHUNK, SCAN_CHUNK), dtype=bool))

    def step(S, xs):
        qc, kc, vc, gc = xs
        b = jnp.cumsum(gc, axis=1)
        q_dec = qc * jnp.exp(b)
        k_inv = kc * jnp.exp(-b)
        A = jnp.where(lower, jnp.einsum('bqhd,bkhd->bhqk', q_dec, k_inv), 0.0)
        o = jnp.einsum('bhqk,bkhv->bqhv', A, vc) + jnp.einsum('bqhd,bhdv->bqhv', q_dec, S)
        b_last = b[:, -1]
        k_end = kc * jnp.exp(b_last[:, None] - b)
        S = jnp.exp(b_last)[..., None] * S + jnp.einsum('bkhd,bkhv->bhdv', k_end, vc)
        return S, o

    S0 = jnp.zeros((B, H, dk, dv), jnp.float32)
    _, o = lax.scan(step, S0, (chunks(q), chunks(k), chunks(v), chunks(log_g)))
    return jnp.moveaxis(o, 0, 1).reshape(B, T, H, dv).astype(v.dtype)


def bidir_gated_scan(q, v, k_fwd, g_fwd, k_bwd, g_bwd, n_ctx):
    rev = lambda t: reverse_segments(t, n_ctx)
    o_f = chunk_gated_scan(q, k_fwd, v, g_fwd)
    o_b = chunk_gated_scan(rev(q), rev(k_bwd), rev(v), rev(g_bwd))
    return o_f + rev(o_b)


def gla_mixer(h_lat, h_ctx, w_in, w_gate_down, w_gate_up, gate_bias, g_norm, w_o, need_ctx):
    n_ctx = h_ctx.shape[1]
    h = jnp.concatenate([h_ctx, h_lat], axis=1)
    B, T, _ = h.shape
    z = h @ w_in
    q = z[..., :GLA_KEY_DIM].reshape(B, T, GLA_HEADS, GLA_DK) * (GLA_DK ** -0.5)
    k = z[..., GLA_KEY_DIM:2 * GLA_KEY_DIM].reshape(B, T, GLA_HEADS, GLA_DK)
    v = z[..., 2 * GLA_KEY_DIM:2 * GLA_KEY_DIM + GLA_VAL_DIM].reshape(B, T, GLA_HEADS, GLA_DV)
    out_gate = z[..., 2 * GLA_KEY_DIM + GLA_VAL_DIM:]

    def log_decay(d):
        zg = (h @ w_gate_down[d]) @ w_gate_up[d] + gate_bias[d]
        return (jax.nn.log_sigmoid(zg.astype(jnp.float32)) / GLA_TAU).reshape(B, T, GLA_HEADS, GLA_DK)

    o = bidir_gated_scan(q, v, k, log_decay(0), k, log_decay(1), n_ctx)
    o = rms_norm(o, g_norm).reshape(B, T, GLA_VAL_DIM) * jax.nn.silu(out_gate)
    y_lat = o[:, n_ctx:] @ w_o
    y_ctx = o[:, :n_ctx] @ w_o if need_ctx else None
    return y_lat, y_ctx


def centred_shift(h):
    hp = jnp.pad(h, ((0, 0), (1, 1), (0, 0)))
    return 0.5 * (hp[:, :-2] + hp[:, 2:]) - h


def rwkv7_scan(r, decay, kk, a, k, v):
    B, T, H, N = r.shape

    def step(S, xs):
        r_t, w_t, kk_t, a_t, k_t, v_t = xs
        sa = jnp.einsum('bhvk,bhk->bhv', S, -kk_t)
        S = (S * w_t[:, :, None, :] + sa[..., None] * (kk_t * a_t)[:, :, None, :]
             + v_t[..., None] * k_t[:, :, None, :])
        return S, jnp.einsum('bhvk,bhk->bhv', S, r_t)

    tm = lambda t: jnp.moveaxis(t.astype(jnp.float32), 1, 0)
    S0 = jnp.zeros((B, H, N, N), jnp.float32)
    _, y = lax.scan(step, S0, (tm(r), tm(decay), tm(kk), tm(a), tm(k), tm(v)))
    return jnp.moveaxis(y, 0, 1)


def rwkv7_mixer(h_lat, h_ctx, mix, w_rkv, w0, w_down, w_up, a0, a_down, a_up, g_down, g_up,
                k_k, k_a, r_k, ln_w, ln_b, w_o, need_ctx):
    n_ctx = h_ctx.shape[1]
    h = jnp.concatenate([h_ctx, h_lat], axis=1)
    dx = jnp.concatenate([centred_shift(h_ctx), centred_shift(h_lat)], axis=1)
    B, T, D = h.shape
    hm = h[None] + dx[None] * mix[:, None, None, :]
    r, k, v = jnp.einsum('nbtd,nde->nbte', hm[:3], w_rkv)
    hw, ha, hg = hm[3], hm[4], hm[5]
    heads = lambda t: t.reshape(B, T, RWKV_HEADS, RWKV_HEAD_SIZE)
    g = jax.nn.sigmoid(hg @ g_down) @ g_up
    kk = heads((k * k_k).astype(jnp.float32))
    kk = kk / jnp.maximum(jnp.sqrt(jnp.sum(kk * kk, axis=-1, keepdims=True)), L2_EPS)

    def direction(d):
        w_log = -jax.nn.softplus(-(w0[d] + jnp.tanh(hw @ w_down[d]) @ w_up[d]).astype(jnp.float32)) - 0.5
        a = jax.nn.sigmoid((a0[d] + (ha @ a_down[d]) @ a_up[d]).astype(jnp.float32))
        return heads(jnp.exp(-jnp.exp(w_log))), heads(a), heads(k * (1 + (a - 1) * k_a))

    dec_f, a_f, k_f = direction(0)
    dec_b, a_b, k_b = direction(1)
    r_h, v_h = heads(r), heads(v)
    rev = lambda t: reverse_segments(t, n_ctx)
    y = (rwkv7_scan(r_h, dec_f, kk, a_f, k_f, v_h)
         + rev(rwkv7_scan(rev(r_h), rev(dec_b), rev(kk), rev(a_b), rev(k_b), rev(v_h))))
    mu = jnp.mean(y, axis=-1, keepdims=True)
    var = jnp.mean(jnp.square(y - mu), axis=-1, keepdims=True)
    yn = ((y - mu) * lax.rsqrt(var + RWKV_LN_EPS) * ln_w.reshape(RWKV_HEADS, RWKV_HEAD_SIZE)
          + ln_b.reshape(RWKV_HEADS, RWKV_HEAD_SIZE))
    bonus = jnp.sum(r_h * (0.5 * (k_f + k_b)) * r_k, axis=-1, keepdims=True) * v_h
    out = (yn + bonus).reshape(B, T, D).astype(h.dtype) * g
    y_lat = out[:, n_ctx:] @ w_o
    y_ctx = out[:, :n_ctx] @ w_o if need_ctx else None
    return y_lat, y_ctx


def hgrn_lower_bound(lb_param, layer):
    p = jax.nn.softmax(lb_param.astype(jnp.float32), axis=0)
    return (jnp.cumsum(p, axis=0) - p[0])[layer]


def hgrn2_mixer(h_lat, h_ctx, w_in, w_f, lower_bound, g_norm, w_o, need_ctx):
    n_ctx = h_ctx.shape[1]
    h = jnp.concatenate([h_ctx, h_lat], axis=1)
    B, T, _ = h.shape
    z = h @ w_in
    q = jax.nn.silu(z[..., :HGRN_FORGET_DIM]).reshape(B, T, HGRN_HEADS, HGRN_EXPAND)
    i_in = z[..., HGRN_FORGET_DIM:HGRN_FORGET_DIM + D_MODEL].reshape(B, T, HGRN_HEADS, HGRN_IN_DIM)
    out_gate = z[..., HGRN_FORGET_DIM + D_MODEL:]

    def gates(d):
        f = lower_bound + (1 - lower_bound) * jax.nn.sigmoid((h @ w_f[d]).astype(jnp.float32))
        f = f.reshape(B, T, HGRN_HEADS, HGRN_EXPAND)
        return 1 - f, jnp.log(f)

    k_fwd, g_fwd = gates(0)
    k_bwd, g_bwd = gates(1)
    o = bidir_gated_scan(q, i_in, k_fwd, g_fwd, k_bwd, g_bwd, n_ctx)
    o = rms_norm(o, g_norm).reshape(B, T, D_MODEL) * jax.nn.silu(out_gate)
    y_lat = o[:, n_ctx:] @ w_o
    y_ctx = o[:, :n_ctx] @ w_o if need_ctx else None
    return y_lat, y_ctx


def setup_inputs(seed: int = 0) -> dict:
    key = jax.random.key(seed)
    keys = iter(jax.random.split(key, 48))
    D = D_MODEL

    def normal(shape, scale):
        return jax.random.normal(next(keys), shape, jnp.float32) * scale

    def gain(shape):
        return 1.0 + normal(shape, 0.02)

    nA, nB, nC, nH = N_ATTN_LAYERS, N_GLA_LAYERS, N_RWKV_LAYERS, N_HGRN_LAYERS
    return {
        "x": normal((BATCH, SEQ, D), 1.0),
        "c": normal((BATCH, D), 1.0),
        "ctx": normal((BATCH, CTX_LEN, D), 1.0),
        "c_ctx": normal((D,), 1.0),
        "w_mod": normal((DEPTH, D, N_MOD * D), 0.5 * D ** -0.5),
        "b_mod": normal((DEPTH, N_MOD * D), 0.01),
        "g_pre_mix": gain((DEPTH, D)),
        "g_post_mix": gain((DEPTH, D)),
        "g_pre_mlp": gain((DEPTH, D)),
        "g_post_mlp": gain((DEPTH, D)),
        "w_mlp_in": normal((DEPTH, D, MLP_HIDDEN), D ** -0.5),
        "w_mlp_out": normal((DEPTH, MLP_HIDDEN, D), MLP_HIDDEN ** -0.5),
        "attn_w_qkv": normal((nA, D, (ATTN_HEADS + 2 * ATTN_KV_HEADS) * HEAD_DIM), D ** -0.5),
        "attn_w_o": normal((nA, ATTN_HEADS * HEAD_DIM, D), (ATTN_HEADS * HEAD_DIM) ** -0.5),
        "attn_sink": normal((nA, ATTN_HEADS), 0.5),
        "gla_w_in": normal((nB, D, 2 * GLA_KEY_DIM + 2 * GLA_VAL_DIM), D ** -0.5),
        "gla_w_gate_down": normal((nB, 2, D, GLA_GATE_RANK), D ** -0.5),
        "gla_w_gate_up": normal((nB, 2, GLA_GATE_RANK, GLA_KEY_DIM), GLA_GATE_RANK ** -0.5),
        "gla_gate_bias": normal((nB, 2, GLA_KEY_DIM), 0.1),
        "gla_g_norm": gain((nB, GLA_DV)),
        "gla_w_o": normal((nB, GLA_VAL_DIM, D), GLA_VAL_DIM ** -0.5),
        "rwkv_mix": jax.random.uniform(next(keys), (nC, 6, D), jnp.float32),
        "rwkv_w_rkv": normal((nC, 3, D, D), D ** -0.5),
        "rwkv_w0": -1.5 + normal((nC, 2, D), 0.5),
        "rwkv_w_down": normal((nC, 2, D, RWKV_DECAY_RANK), D ** -0.5),
        "rwkv_w_up": normal((nC, 2, RWKV_DECAY_RANK, D), 0.1 * RWKV_DECAY_RANK ** -0.5),
        "rwkv_a0": normal((nC, 2, D), 0.1),
        "rwkv_a_down": normal((nC, 2, D, RWKV_AAA_RANK), D ** -0.5),
        "rwkv_a_up": normal((nC, 2, RWKV_AAA_RANK, D), RWKV_AAA_RANK ** -0.5),
        "rwkv_g_down": normal((nC, D, RWKV_GATE_RANK), D ** -0.5),
        "rwkv_g_up": normal((nC, RWKV_GATE_RANK, D), RWKV_GATE_RANK ** -0.5),
        "rwkv_k_k": 0.85 + normal((nC, D), 0.02),
        "rwkv_k_a": gain((nC, D)),
        "rwkv_r_k": normal((nC, RWKV_HEADS, RWKV_HEAD_SIZE), 0.1),
        "rwkv_ln_w": gain((nC, D)),
        "rwkv_ln_b": normal((nC, D), 0.02),
        "rwkv_w_o": normal((nC, D, D), D ** -0.5),
        "hgrn_w_in": normal((nH, D, HGRN_FORGET_DIM + 2 * D), D ** -0.5),
        "hgrn_w_f": normal((nH, 2, D, HGRN_FORGET_DIM), D ** -0.5),
        "hgrn_lb": normal((DEPTH, HGRN_FORGET_DIM), 0.1),
        "hgrn_g_norm": gain((nH, HGRN_IN_DIM)),
        "hgrn_w_o": normal((nH, D, D), D ** -0.5),
    }


def reference(x, c, ctx, c_ctx, w_mod, b_mod, g_pre_mix, g_post_mix, g_pre_mlp, g_post_mlp,
              w_mlp_in, w_mlp_out, attn_w_qkv, attn_w_o, attn_sink,
              gla_w_in, gla_w_gate_down, gla_w_gate_up, gla_gate_bias, gla_g_norm, gla_w_o,
              rwkv_mix, rwkv_w_rkv, rwkv_w0, rwkv_w_down, rwkv_w_up, rwkv_a0, rwkv_a_down,
              rwkv_a_up, rwkv_g_down, rwkv_g_up, rwkv_k_k, rwkv_k_a, rwkv_r_k, rwkv_ln_w,
              rwkv_ln_b, rwkv_w_o, hgrn_w_in, hgrn_w_f, hgrn_lb, hgrn_g_norm, hgrn_w_o):
    n_latent = x.shape[1]
    rows = n_latent // GRID_W
    cos, sin = axial_rope_tables(rows)
    c_lat = jax.nn.silu(c)
    c_con = jax.nn.silu(c_ctx)
    x_lat, x_ctx = x, ctx
    for i in range(DEPTH):
        kind, j = i % N_MIXERS, i // N_MIXERS
        need_ctx = i < DEPTH - 1
        m_lat = jnp.split((c_lat @ w_mod[i] + b_mod[i])[:, None, :], N_MOD, axis=-1)
        m_ctx = jnp.split(c_con @ w_mod[i] + b_mod[i], N_MOD, axis=-1)
        h_lat = modulate(x_lat, g_pre_mix[i], m_lat[0], m_lat[1])
        h_ctx = modulate(x_ctx, g_pre_mix[i], m_ctx[0], m_ctx[1])
        if kind == 0:
            y_lat, y_ctx = windowed_gqa_sink(h_lat, h_ctx, attn_w_qkv[j], attn_w_o[j], attn_sink[j],
                                             cos, sin, need_ctx)
        elif kind == 1:
            y_lat, y_ctx = gla_mixer(h_lat, h_ctx, gla_w_in[j], gla_w_gate_down[j], gla_w_gate_up[j],
                                     gla_gate_bias[j], gla_g_norm[j], gla_w_o[j], need_ctx)
        elif kind == 2:
            y_lat, y_ctx = rwkv7_mixer(h_lat, h_ctx, rwkv_mix[j], rwkv_w_rkv[j], rwkv_w0[j],
                                       rwkv_w_down[j], rwkv_w_up[j], rwkv_a0[j], rwkv_a_down[j],
                                       rwkv_a_up[j], rwkv_g_down[j], rwkv_g_up[j], rwkv_k_k[j],
                                       rwkv_k_a[j], rwkv_r_k[j], rwkv_ln_w[j], rwkv_ln_b[j],
                                       rwkv_w_o[j], need_ctx)
        else:
            y_lat, y_ctx = hgrn2_mixer(h_lat, h_ctx, hgrn_w_in[j], hgrn_w_f[j],
                                       hgrn_lower_bound(hgrn_lb, i), hgrn_g_norm[j], hgrn_w_o[j],
                                       need_ctx)
        x_lat = x_lat + m_lat[2] * rms_norm(y_lat, g_post_mix[i])
        f_lat = channel_mlp(modulate(x_lat, g_pre_mlp[i], m_lat[3], m_lat[4]), w_mlp_in[i], w_mlp_out[i])
        x_lat = x_lat + m_lat[5] * rms_norm(f_lat, g_post_mlp[i])
        if need_ctx:
            x_ctx = x_ctx + m_ctx[2] * rms_norm(y_ctx, g_post_mix[i])
            f_ctx = channel_mlp(modulate(x_ctx, g_pre_mlp[i], m_ctx[3], m_ctx[4]), w_mlp_in[i], w_mlp_out[i])
            x_ctx = x_ctx + m_ctx[5] * rms_norm(f_ctx, g_post_mlp[i])
    return x_lat
```

```python
import numpy as np
from contextlib import ExitStack
import concourse.bass as bass
import concourse.mybir as mybir
from concourse.bass_utils import run_bass_kernel_spmd

F32 = mybir.dt.float32
BF16 = mybir.dt.bfloat16
AF = mybir.ActivationFunctionType
ALU = mybir.AluOpType
AX = mybir.AxisListType

D = 1024
TC = 256
TL = 2048
T = TC + TL
HID = 4096
NCORES = 8
EPS = 1e-6

EPOCH = 30000
COMPUTE = ("pe", "act", "dve", "pool")
DEBUG = False
import os
ATT_STOP = int(os.environ.get("ATT_STOP", "9"))
ATT_SUB = os.environ.get("ATT_SUB", "d")


class Res:
    __slots__ = ("name", "last_w", "readers")

    def __init__(self, name=""):
        self.name = name
        self.last_w = None
        self.readers = {}


class Tile:
    def __init__(self, ap, name=""):
        self.t = ap
        self.res = Res(name)

    def __getitem__(self, idx):
        return self.t[idx]


class Sched:
    def __init__(self, nc, es):
        self.nc = nc
        self.engs = {"pe": nc.tensor, "act": nc.scalar, "dve": nc.vector, "pool": nc.gpsimd, "sp": nc.sync}
        self.count = {e: 0 for e in COMPUTE}
        self.esems = {e: [] for e in COMPUTE}
        self.es = es
        self.known = {e: {} for e in self.engs}
        self.lists = {e: [] for e in self.engs}
        self.ndma = 8
        self.dsems, self.dcnt, self.dlast = {}, {}, {}
        self.drot = {"sp": 0, "pool": 0, "act": 0}
        for q in ("sp", "pool", "act"):
            for i in range(self.ndma):
                self.dsems[(q, i)] = es.enter_context(nc.semaphore(f"d_{q}_{i}"))
                self.dcnt[(q, i)] = 0
                self.dlast[(q, i)] = None
        self.n_inst = 0
        self._psi = 0

    def _esem(self, e, epoch):
        lst = self.esems[e]
        while len(lst) <= epoch:
            lst.append(self.es.enter_context(self.nc.semaphore(f"e_{e}_{len(lst)}")))
        return lst[epoch]

    def _wait(self, e, toks):
        kn = self.known[e]
        best = {}
        for tk in toks:
            if tk is None:
                continue
            if tk[0] == "eng":
                _, e2, idx = tk
                if e == "pe" and e2 == "pe":
                    continue
                k = ("eng", e2)
                v = idx
            else:
                k = ("dma", tk[1])
                v = tk[2]
            if kn.get(k, 0) >= v:
                continue
            if best.get(k, 0) < v:
                best[k] = v
        for k, v in best.items():
            kn[k] = v
            if k[0] == "eng":
                ep = (v - 1) // EPOCH
                self.lists[e].append(("wait", self._esem(k[1], ep), v - ep * EPOCH))
            else:
                self.lists[e].append(("wait", self.dsems[k[1]], v))

    @staticmethod
    def _r(x):
        return x.res if isinstance(x, Tile) else x

    def _deps(self, reads, writes):
        deps = []
        for r in reads:
            r = self._r(r)
            if r.last_w is not None:
                deps.append(r.last_w)
        for w in writes:
            w = self._r(w)
            if w.last_w is not None:
                deps.append(w.last_w)
            deps.extend(w.readers.values())
        return deps

    def _mark(self, tok, reads, writes):
        for r in reads:
            self._r(r).readers[(tok[0], tok[1])] = tok
        for w in writes:
            w = self._r(w)
            w.last_w = tok
            w.readers = {}

    def op(self, e, fn, reads=(), writes=()):
        self._wait(e, self._deps(reads, writes))
        self.count[e] += 1
        idx = self.count[e]
        ep = (idx - 1) // EPOCH
        self.lists[e].append(("op", fn, self._esem(e, ep), 1))
        tok = ("eng", e, idx)
        self._mark(tok, reads, writes)
        self.n_inst += 1
        return tok

    def dma(self, q, fn, reads=(), writes=()):
        i = self.drot[q]
        self.drot[q] = (i + 1) % self.ndma
        key = (q, i)
        deps = self._deps(reads, writes)
        if self.dlast[key] is not None:
            deps.append(self.dlast[key])
        self._wait(q, deps)
        self.dcnt[key] += 16
        tok = ("dma", key, self.dcnt[key])
        self.dlast[key] = tok
        self.lists[q].append(("op", fn, self.dsems[key], 16))
        self._mark(tok, reads, writes)
        self.n_inst += 1
        return tok

    def barrier(self):
        toks = [("eng", e, self.count[e]) for e in COMPUTE if self.count[e] > 0]
        toks += [tk for tk in self.dlast.values() if tk is not None]
        for e in self.engs:
            self._wait(e, toks)

    def flush(self):
        lists = self.lists
        self.lists = {e: [] for e in self.engs}

        def replay(eng, items):
            for it in items:
                if it[0] == "wait":
                    eng.wait_ge(it[1], it[2])
                else:
                    it[1](eng).then_inc(it[2], it[3])

        with self.nc.Block() as block:
            @block.tensor
            def _(eng):
                replay(eng, lists["pe"])

            @block.scalar
            def _(eng):
                replay(eng, lists["act"])

            @block.vector
            def _(eng):
                replay(eng, lists["dve"])

            @block.gpsimd
            def _(eng):
                replay(eng, lists["pool"])

            @block.sync
            def _(eng):
                replay(eng, lists["sp"])


def fm_vec(v):
    v = np.asarray(v, np.float32).reshape(-1, 128)
    return np.ascontiguousarray(v.T)


def wlay(w):
    K, N = w.shape
    return np.ascontiguousarray(w.reshape(K // 128, 128, N).transpose(1, 0, 2))


class VecPack:
    def __init__(self):
        self.cols = []
        self.off = {}
        self.n = 0

    def add(self, name, arr2d):
        arr2d = np.asarray(arr2d, np.float32)
        assert arr2d.shape[0] == 128
        self.off[name] = (self.n, arr2d.shape[1])
        self.cols.append(arr2d)
        self.n += arr2d.shape[1]

    def pack(self):
        return np.ascontiguousarray(np.concatenate(self.cols, axis=1))


class Prog:
    def __init__(self, nb, layers, voff, nv, last_layer=3, do_mlp=True):
        self.nb = nb
        self.layers = layers
        self.voff = voff
        self.nv = nv
        self.last_layer = last_layer
        self.do_mlp = do_mlp
        self.nc = bass.Bass("TRN2", target_bir_lowering=False)
        self.dram = {}
        self.dbg_names = set()
        self.debug_on = DEBUG

    def dbg(self, name, tile, ap, shape):
        if not getattr(self, "debug_on", False) or name in self.dbg_names:
            return
        self.dbg_names.add(name)
        d = self.nc.dram_tensor("dbg_" + name, list(shape), F32, kind="ExternalOutput").ap()
        self.S.dma("pool", lambda e: e.dma_start(out=d, in_=ap), reads=[tile])

    def din(self, name, shape, dt=F32):
        self.dram[name] = self.nc.dram_tensor(name, list(shape), dt, kind="ExternalInput").ap()
        return self.dram[name]

    def dscratch(self, name, shape, dt=F32):
        return self.nc.dram_tensor(name, list(shape), dt, kind="Internal").ap()

    def sb(self, es, name, shape, dt):
        self._nm = getattr(self, "_nm", 0) + 1
        return Tile(es.enter_context(self.nc.sbuf_tensor(f"{name}_{self._nm}", list(shape), dt)), name)

    def ps(self):
        S = self.S
        t = self.PS[S._psi % len(self.PS)]
        S._psi += 1
        return t

    def ps_ex(self, excl):
        while True:
            t = self.ps()
            if all(t is not x for x in excl):
                return t

    def vcol(self, name, c=0, n=1):
        o, w = self.voff[name]
        return self.vecs[:, o + c:o + c + n]

    def rstd_of(self, src, W, rstd, nch=8, scale=1.0 / D, eps=EPS, ones=None, sq_dt=BF16, lo=0, c0=0):
        S = self.S
        ones = ones or self.ones_b
        p = self.ps()
        for c in range(nch):
            sq = self.sqb[c % 2]
            eng = "act" if c % 2 == 0 else "pool"
            if eng == "act":
                S.op("act", lambda e, sq=sq, c=c: e.activation(out=sq[:, :W], in_=src[:, c0 + c, lo:lo + W], func=AF.Square), reads=[src], writes=[sq])
            else:
                S.op("pool", lambda e, sq=sq, c=c: e.tensor_tensor(out=sq[:, :W], in0=src[:, c0 + c, lo:lo + W], in1=src[:, c0 + c, lo:lo + W], op=ALU.mult), reads=[src], writes=[sq])
            S.op("pe", lambda e, sq=sq, c=c, p=p: e.matmul(p[:, :W], lhsT=ones[:, :], rhs=sq[:, :W], start=(c == 0), stop=(c == nch - 1)), reads=[sq, ones], writes=[p])
        S.op("act", lambda e, p=p: e.activation(out=rstd[:, :W], in_=p[:, :W], func=AF.Sqrt, bias=self.epsb[:, 0:1] if eps == EPS else self.epsb2[:, 0:1], scale=scale),
             reads=[p, self.epsb], writes=[rstd])
        S.op("dve", lambda e: e.reciprocal(out=rstd[:, :W], in_=rstd[:, :W]), reads=[rstd], writes=[rstd])

    def modulate(self, xt, W, rstd, layer, kind_g, kind_s, col, hT, lo=0, hlo=0):
        S = self.S
        for c in range(8):
            tmp = self.tmpf[c % 2]
            g = self.modv[:, layer, kind_g, c, col:col + 1]
            sh = self.modv[:, layer, kind_s, c, col:col + 1]
            S.op("dve", lambda e, tmp=tmp, c=c, g=g: e.scalar_tensor_tensor(out=tmp[:, :W], in0=xt[:, c, lo:lo + W], scalar=g, in1=rstd[:, :W], op0=ALU.mult, op1=ALU.mult),
                 reads=[xt, rstd, self.modv], writes=[tmp])
            if c % 2 == 0:
                S.op("act", lambda e, tmp=tmp, c=c, sh=sh: e.activation(out=hT[:, c, hlo:hlo + W], in_=tmp[:, :W], func=AF.Identity, bias=sh, scale=1.0),
                     reads=[tmp, self.modv], writes=[hT])
            else:
                S.op("pool", lambda e, tmp=tmp, c=c, sh=sh: e.tensor_scalar(out=hT[:, c, hlo:hlo + W], in0=tmp[:, :W], scalar1=sh, scalar2=None, op0=ALU.add),
                     reads=[tmp, self.modv], writes=[hT])

    def resid(self, xt, W, y, rstd, layer, kind_g, col):
        S = self.S
        for c in range(8):
            tmp = self.tmpf[c % 2]
            g = self.modv[:, layer, kind_g, c, col:col + 1]
            S.op("dve", lambda e, tmp=tmp, c=c, g=g: e.scalar_tensor_tensor(out=tmp[:, :W], in0=y[:, c, :W], scalar=g, in1=rstd[:, :W], op0=ALU.mult, op1=ALU.mult),
                 reads=[y, rstd, self.modv], writes=[tmp])
            S.op("pool", lambda e, tmp=tmp, c=c: e.tensor_tensor(out=xt[:, c, :W], in0=xt[:, c, :W], in1=tmp[:, :W], op=ALU.add), reads=[tmp, xt], writes=[xt])

    def load_x(self, xt, b, t0, W):
        XT = self.XT
        self.S.dma("sp", lambda e: e.dma_start(out=xt[:, :, :W], in_=XT.t[b, :, :, t0:t0 + W].rearrange("c p t -> p c t")), reads=[XT], writes=[xt])

    def store_x(self, xt, b, t0, W):
        XT = self.XT
        self.S.dma("sp", lambda e: e.dma_start(out=XT.t[b, :, :, t0:t0 + W].rearrange("c p t -> p c t"), in_=xt[:, :, :W]), reads=[xt], writes=[XT])

    def load_w(self, dst, dst_ap, src_ap, cap=4096):
        shp = tuple(dst_ap.shape)
        assert tuple(src_ap.shape) == shp, (shp, tuple(src_ap.shape))
        pieces = []
        if len(shp) == 2:
            n = shp[1]
            for c0 in range(0, n, cap):
                c1 = min(n, c0 + cap)
                pieces.append((dst_ap[:, c0:c1], src_ap[:, c0:c1]))
        else:
            assert len(shp) == 3
            k, n = shp[1], shp[2]
            if n >= cap:
                for j in range(k):
                    for c0 in range(0, n, cap):
                        c1 = min(n, c0 + cap)
                        pieces.append((dst_ap[:, j, c0:c1], src_ap[:, j, c0:c1]))
            else:
                kk = max(1, cap // n)
                for j0 in range(0, k, kk):
                    j1 = min(k, j0 + kk)
                    pieces.append((dst_ap[:, j0:j1, :], src_ap[:, j0:j1, :]))
        for (da, sa) in pieces:
            self.S.dma("pool", lambda e, da=da, sa=sa: e.dma_start(out=da, in_=sa), writes=[dst])

    def evac(self, i, out_ap, in_ap, reads, writes):
        if i % 2 == 0:
            self.S.op("act", lambda e: e.activation(out=out_ap, in_=in_ap, func=AF.Copy), reads=reads, writes=writes)
        else:
            self.S.op("dve", lambda e: e.tensor_copy(out=out_ap, in_=in_ap), reads=reads, writes=writes)

    def build(self):
        nc = self.nc
        nb = self.nb
        x_d = self.din("x", [nb, TL, D])
        ctx_d = self.din("ctx", [nb, TC, D])
        cT_d = self.din("cT", [128, 8, 5])
        vecs_d = self.din("vecs", [128, self.nv])
        wmod_d = self.din("w_mod", [4, 128, 8, 6 * D])
        self.w1_d = self.din("w_mlp_in", [4, 128, 8, HID])
        self.w2_d = self.din("w_mlp_out", [4, 128, 32, D])
        out_d = self.nc.dram_tensor("out", [nb, TL, D], F32, kind="ExternalOutput").ap()
        self.XT = Tile(self.dscratch("XT", [nb, 8, 128, T]), "XT")
        self.declare_mixer_inputs()

        with ExitStack() as es:
            S = self.S = Sched(nc, es)
            self.PS = [Tile(es.enter_context(nc.psum_tensor(f"ps{i}", [128, 512], F32)), f"ps{i}") for i in range(7)]
            self.PSB = Tile(es.enter_context(nc.psum_tensor("psb", [128, 1024], BF16)), "psb")
            self.vecs = self.sb(es, "vecs", [128, self.nv], F32)
            S.dma("sp", lambda e: e.dma_start(out=self.vecs[:], in_=vecs_d[:, :]), writes=[self.vecs])
            self.ident_f = self.sb(es, "identf", [128, 128], F32)
            self.ident_b = self.sb(es, "identb", [128, 128], BF16)
            self.ones_b = self.sb(es, "onesb", [128, 128], BF16)
            self.ones_f = self.sb(es, "onesf", [128, 128], F32)
            self.epsb = self.sb(es, "epsb", [128, 1], F32)
            self.epsb2 = self.sb(es, "epsb2", [128, 1], F32)
            S.op("pool", lambda e: e.memset(self.ident_f[:], 0.0), writes=[self.ident_f])
            S.op("pool", lambda e: e.affine_select(out=self.ident_f[:], in_=self.ident_f[:], pattern=[[-1, 128]], compare_op=ALU.not_equal, fill=1.0, base=0, channel_multiplier=1),
                 reads=[self.ident_f], writes=[self.ident_f])
            S.op("dve", lambda e: e.tensor_copy(out=self.ident_b[:], in_=self.ident_f[:]), reads=[self.ident_f], writes=[self.ident_b])
            S.op("pool", lambda e: e.memset(self.ones_b[:], 1.0), writes=[self.ones_b])
            S.op("pool", lambda e: e.memset(self.ones_f[:], 1.0), writes=[self.ones_f])
            S.op("pool", lambda e: e.memset(self.epsb[:], EPS), writes=[self.epsb])
            S.op("pool", lambda e: e.memset(self.epsb2[:], 64e-5), writes=[self.epsb2])
            self.sqb = [self.sb(es, f"sqb{i}", [128, 512], BF16) for i in range(2)]
            self.tmpf = [self.sb(es, f"tmpf{i}", [128, 512], F32) for i in range(2)]
            self.modv = self.sb(es, "modv", [128, 4, 6, 8, 5], F32)

            self.stage_prologue(cT_d, wmod_d)
            self.stage_in(x_d, ctx_d)
            for layer in self.layers:
                self.stage_mixer(layer)
                if self.do_mlp:
                    self.stage_mlp(layer)
            self.stage_out(out_d)
            S.barrier()
            S.flush()
        return nc

    def declare_mixer_inputs(self):
        pass

    def stage_mixer(self, layer):
        pass

    def stage_prologue(self, cT_d, wmod_d):
        S = self.S
        with ExitStack() as es:
            cT = self.sb(es, "cT", [128, 8, 5], F32)
            csil = self.sb(es, "csil", [128, 8, 5], BF16)
            mods = self.sb(es, "mods", [128, 48, 5], F32)
            wm = [self.sb(es, f"wm{i}", [128, 8, 1536], BF16) for i in range(2)]
            S.dma("sp", lambda e: e.dma_start(out=cT[:], in_=cT_d[:, :, :]), writes=[cT])
            S.op("act", lambda e: e.activation(out=csil[:], in_=cT[:], func=AF.Silu), reads=[cT], writes=[csil])
            k = 0
            for layer in range(4):
                p = self.ps()
                for piece in range(4):
                    w = wm[k % 2]
                    k += 1
                    for kc in range(8):
                        self.load_w(w, w[:, kc, :], wmod_d[layer, :, kc, piece * 1536:(piece + 1) * 1536])
                    for jj in range(12):
                        j = piece * 12 + jj
                        for kc in range(8):
                            S.op("pe", lambda e, w=w, kc=kc, jj=jj, j=j, p=p: e.matmul(p[:, j * 8:j * 8 + 5], lhsT=w[:, kc, jj * 128:(jj + 1) * 128], rhs=csil[:, kc, :],
                                                                                   start=(kc == 0), stop=(kc == 7)), reads=[w, csil], writes=[p])
                bo = self.voff["b_mod"][0] + layer * 48
                for col in range(5):
                    S.op("dve", lambda e, p=p, col=col, bo=bo: e.tensor_tensor(out=mods[:, :, col], in0=p[:, 0:384].rearrange("p (j e) -> p j e", e=8)[:, :, col],
                                                                            in1=self.vecs[:, bo:bo + 48], op=ALU.add), reads=[p, self.vecs], writes=[mods])
                gofs = {n: self.voff[n][0] + layer * 8 for n in ("g_pre_mix", "g_post_mix", "g_pre_mlp", "g_post_mlp")}
                for col in range(5):
                    def g(n):
                        return self.vecs[:, gofs[n]:gofs[n] + 8]
                    mv = self.modv
                    S.op("dve", lambda e, col=col, layer=layer, gg=g("g_pre_mix"): e.scalar_tensor_tensor(out=mv[:, layer, 0, :, col], in0=mods[:, 8:16, col], scalar=1.0, in1=gg, op0=ALU.add, op1=ALU.mult),
                         reads=[mods, self.vecs], writes=[mv])
                    S.op("dve", lambda e, col=col, layer=layer: e.tensor_copy(out=mv[:, layer, 1, :, col], in_=mods[:, 0:8, col]), reads=[mods], writes=[mv])
                    S.op("dve", lambda e, col=col, layer=layer, gg=g("g_post_mix"): e.tensor_tensor(out=mv[:, layer, 2, :, col], in0=mods[:, 16:24, col], in1=gg, op=ALU.mult),
                         reads=[mods, self.vecs], writes=[mv])
                    S.op("dve", lambda e, col=col, layer=layer, gg=g("g_pre_mlp"): e.scalar_tensor_tensor(out=mv[:, layer, 3, :, col], in0=mods[:, 32:40, col], scalar=1.0, in1=gg, op0=ALU.add, op1=ALU.mult),
                         reads=[mods, self.vecs], writes=[mv])
                    S.op("dve", lambda e, col=col, layer=layer: e.tensor_copy(out=mv[:, layer, 4, :, col], in_=mods[:, 24:32, col]), reads=[mods], writes=[mv])
                    S.op("dve", lambda e, col=col, layer=layer, gg=g("g_post_mlp"): e.tensor_tensor(out=mv[:, layer, 5, :, col], in0=mods[:, 40:48, col], in1=gg, op=ALU.mult),
                         reads=[mods, self.vecs], writes=[mv])
            S.barrier()
            S.flush()

    def stage_in(self, x_d, ctx_d):
        S = self.S
        with ExitStack() as es:
            xin = [self.sb(es, f"xin{i}", [128, D], F32) for i in range(2)]
            xts = [self.sb(es, f"xts{i}", [128, 8, 128], F32) for i in range(2)]
            k = 0
            for b in range(self.nb):
                for tt in range(T // 128):
                    xi, xo = xin[k % 2], xts[k % 2]
                    k += 1
                    src = ctx_d[b, tt * 128:(tt + 1) * 128, :] if tt < 2 else x_d[b, (tt - 2) * 128:(tt - 1) * 128, :]
                    S.dma("sp", lambda e, xi=xi, src=src: e.dma_start(out=xi[:], in_=src), writes=[xi])
                    for half in range(2):
                        p = self.ps()
                        for j in range(4):
                            c = half * 4 + j
                            S.op("pe", lambda e, p=p, xi=xi, c=c, j=j: e.transpose(out=p[:, j * 128:(j + 1) * 128], in_=xi[:, c * 128:(c + 1) * 128], identity=self.ident_f[:]),
                                 reads=[xi, self.ident_f], writes=[p])
                        self.evac(half, xo[:, half * 4:(half + 1) * 4, :], p[:].rearrange("p (j t) -> p j t", j=4), [p], [xo])
                    XT = self.XT
                    S.dma("sp", lambda e, xo=xo, b=b, tt=tt: e.dma_start(out=XT.t[b, :, :, tt * 128:(tt + 1) * 128].rearrange("c p t -> p c t"), in_=xo[:]), reads=[xo], writes=[XT])
            S.barrier()
            S.flush()

    def stage_out(self, out_d):
        S = self.S
        with ExitStack() as es:
            xin = [self.sb(es, f"xout{i}", [128, D], F32) for i in range(2)]
            xts = [self.sb(es, f"xtso{i}", [128, 8, 128], F32) for i in range(2)]
            k = 0
            for b in range(self.nb):
                for tt in range(2, T // 128):
                    xi, xo = xin[k % 2], xts[k % 2]
                    k += 1
                    self.load_x(xo, b, tt * 128, 128)
                    for half in range(2):
                        p = self.ps()
                        for j in range(4):
                            c = half * 4 + j
                            S.op("pe", lambda e, p=p, xo=xo, c=c, j=j: e.transpose(out=p[:, j * 128:(j + 1) * 128], in_=xo[:, c, :], identity=self.ident_f[:]),
                                 reads=[xo, self.ident_f], writes=[p])
                        self.evac(half, xi[:, half * 512:(half + 1) * 512], p[:], [p], [xi])
                    S.dma("sp", lambda e, xi=xi, b=b, tt=tt: e.dma_start(out=out_d[b, (tt - 2) * 128:(tt - 1) * 128, :], in_=xi[:]), reads=[xi])
            S.barrier()
            S.flush()

    def stage_mlp(self, layer):
        S = self.S
        W = 256
        with ExitStack() as es:
            w1 = self.sb(es, "w1", [128, 8, HID], BF16)
            w2 = self.sb(es, "w2", [128, 32, D], BF16)
            for kc in range(8):
                self.load_w(w1, w1[:, kc, :], self.w1_d[layer, :, kc, :])
            for j4 in range(8):
                self.load_w(w2, w2[:, j4 * 4:(j4 + 1) * 4, :], self.w2_d[layer, :, j4 * 4:(j4 + 1) * 4, :])
            xts = [self.sb(es, f"mx{i}", [128, 8, W], F32) for i in range(2)]
            hT = self.sb(es, "mh", [128, 8, W], BF16)
            u = self.sb(es, "mu", [128, 32, W], BF16)
            f = self.sb(es, "mf", [128, 8, W], F32)
            rstd = self.sb(es, "mrstd", [128, W], F32)
            rl = [self.sb(es, f"mrl{i}", [128, W], F32) for i in range(2)]
            k = 0
            for b in range(self.nb):
                for t0 in range(0, T, W):
                    if t0 < TC and layer == self.last_layer:
                        continue
                    col = 4 if t0 < TC else b
                    xt = xts[k % 2]
                    k += 1
                    self.load_x(xt, b, t0, W)
                    self.rstd_of(xt, W, rstd)
                    self.modulate(xt, W, rstd, layer, 3, 4, col, hT)
                    for j in range(32):
                        p = self.ps()
                        for kc in range(8):
                            S.op("pe", lambda e, p=p, kc=kc, j=j: e.matmul(p[:, :W], lhsT=w1[:, kc, j * 128:(j + 1) * 128], rhs=hT[:, kc, :], start=(kc == 0), stop=(kc == 7)),
                                 reads=[w1, hT], writes=[p])
                        r = rl[j % 2]
                        S.op("act", lambda e, p=p, r=r: e.activation(out=r[:], in_=p[:, :W], func=AF.Relu), reads=[p], writes=[r])
                        S.op("dve" if j % 2 else "pool", lambda e, r=r, j=j: e.tensor_tensor(out=u[:, j, :], in0=r[:], in1=r[:], op=ALU.mult), reads=[r], writes=[u])
                    for c in range(8):
                        p = self.ps()
                        for j in range(32):
                            S.op("pe", lambda e, p=p, c=c, j=j: e.matmul(p[:, :W], lhsT=w2[:, j, c * 128:(c + 1) * 128], rhs=u[:, j, :], start=(j == 0), stop=(j == 31)),
                                 reads=[w2, u], writes=[p])
                        self.evac(c, f[:, c, :], p[:, :W], [p], [f])
                    self.rstd_of(f, W, rstd)
                    self.resid(xt, W, f, rstd, layer, 5, col)
                    self.store_x(xt, b, t0, W)
            S.barrier()
            S.flush()


def host_common(inp):
    vp = VecPack()
    vp.add("b_mod", np.concatenate([fm_vec(inp["b_mod"][l]) for l in range(4)], axis=1))
    for n in ("g_pre_mix", "g_post_mix", "g_pre_mlp", "g_post_mlp"):
        vp.add(n, np.concatenate([fm_vec(inp[n][l]) for l in range(4)], axis=1))
    com = {
        "w_mod": np.stack([wlay(np.asarray(inp["w_mod"][l], np.float32)) for l in range(4)]),
        "w_mlp_in": np.stack([wlay(np.asarray(inp["w_mlp_in"][l], np.float32)) for l in range(4)]),
        "w_mlp_out": np.stack([wlay(np.asarray(inp["w_mlp_out"][l], np.float32)) for l in range(4)]),
    }
    host_mixers(inp, vp, com)
    com["vecs"] = vp.pack()
    return com, vp.off, vp.n


def host_mixers(inp, vp, com):
    pass


def host_core(inp, b0, nb):
    c = np.asarray(inp["c"], np.float32)[b0:b0 + nb]
    cc = np.zeros((5, D), np.float32)
    cc[:nb] = c
    cc[4] = np.asarray(inp["c_ctx"], np.float32)
    cT = np.ascontiguousarray(cc.reshape(5, 8, 128).transpose(2, 1, 0))
    return {
        "x": np.ascontiguousarray(np.asarray(inp["x"], np.float32)[b0:b0 + nb]),
        "ctx": np.ascontiguousarray(np.asarray(inp["ctx"], np.float32)[b0:b0 + nb]),
        "cT": cT,
    }


def run(inp, nb, ncores, layers=(0, 1, 2, 3), prog_cls=None, last_layer=3, do_mlp=True, trace=False):
    com, voff, nv = host_common(inp)
    prog = (prog_cls or Prog)(nb, list(layers), voff, nv, last_layer=last_layer, do_mlp=do_mlp)
    nc = prog.build()
    in_maps = []
    for i in range(ncores):
        d = dict(com)
        d.update(host_core(inp, i * nb, nb))
        d = {k: v for k, v in d.items() if k in prog.dram}
        in_maps.append(d)
    res = run_bass_kernel_spmd(nc, in_maps, core_ids=list(range(ncores)), trace=trace)
    out = np.concatenate([np.asarray(r["out"]) for r in res.results], axis=0)
    return out.astype(np.float32), res, prog


def kernel(**inputs):
    out, _, _ = run(inputs, 4, NCORES)
    return out


def _partner():
    d = np.arange(64)
    axis, half, f = d // 32, (d % 32) // 16, d % 16
    return axis * 32 + (1 - half) * 16 + f


def host_attn(inp, vp, com):
    wqkv = np.asarray(inp["attn_w_qkv"][0], np.float32)
    wo = np.asarray(inp["attn_w_o"][0], np.float32)
    pr = _partner()
    qcols = np.arange(1024)
    qperm = (qcols // 64) * 64 + pr[qcols % 64]
    com["attn_wq"] = wlay(wqkv[:, :1024])
    com["attn_wqp"] = wlay(wqkv[:, qperm])
    kd = np.concatenate([1024 + g * 64 + np.concatenate([np.arange(64), np.arange(64)]) for g in range(4)])
    kdp = np.concatenate([1024 + g * 64 + np.concatenate([pr, pr]) for g in range(4)])
    com["attn_wk"] = wlay(wqkv[:, kd])
    com["attn_wkp"] = wlay(wqkv[:, kdp])
    com["attn_wv"] = wlay(wqkv[:, 1280:1536])
    com["attn_wo"] = np.ascontiguousarray(wo.reshape(16, 64, 1024).transpose(1, 0, 2))
    inv_freq = (np.float32(10000.0) ** (-np.arange(16, dtype=np.float32) * np.float32(2.0) / np.float32(32))).astype(np.float32)
    pos = np.arange(TL)
    row = (pos // 64).astype(np.float32)
    colp = (pos % 64).astype(np.float32)
    p = np.arange(128) % 64
    axis, half, f = p // 32, (p % 32) // 16, p % 16
    base = np.where(axis[:, None] == 0, row[None, :], colp[None, :]).astype(np.float32)
    ang = (base * inv_freq[f][:, None]).astype(np.float32)
    sgn = np.where(half == 0, -1.0, 1.0).astype(np.float32)[:, None]
    com["attn_cos"] = np.cos(ang).astype(np.float32)
    com["attn_sin"] = (np.sin(ang) * sgn).astype(np.float32)
    com["attn_sinkrow"] = np.ascontiguousarray(np.repeat(np.asarray(inp["attn_sink"][0], np.float32), 128)[None, :])
    kk = np.arange(128)[:, None]
    qq = np.arange(128)[None, :]
    com["attn_maskL"] = np.tile((qq <= kk).astype(np.float32), (1, 4))
    com["attn_maskU"] = np.tile((kk <= qq).astype(np.float32), (1, 4))


class FullProg(Prog):
    def declare_mixer_inputs(self):
        if 0 in self.layers:
            self.din("attn_wq", [128, 8, 1024])
            self.din("attn_wqp", [128, 8, 1024])
            self.din("attn_wk", [128, 8, 512])
            self.din("attn_wkp", [128, 8, 512])
            self.din("attn_wv", [128, 8, 256])
            self.din("attn_wo", [64, 16, 1024])
            self.din("attn_cos", [128, TL])
            self.din("attn_sin", [128, TL])
            self.din("attn_sinkrow", [1, 2048])
            self.din("attn_maskL", [128, 512])
            self.din("attn_maskU", [128, 512])

    def stage_mixer(self, layer):
        if layer == 0:
            self.stage_attn(layer)

    def stage_attn(self, layer):
        S = self.S
        dr = self.dram
        W = 256
        need_ctx = layer != self.last_layer
        with ExitStack() as es:
            wq = self.sb(es, "wq", [128, 8, 1024], BF16)
            wqp = self.sb(es, "wqp", [128, 8, 1024], BF16)
            wk = self.sb(es, "wk", [128, 8, 512], BF16)
            wkp = self.sb(es, "wkp", [128, 8, 512], BF16)
            wv = self.sb(es, "wv", [128, 8, 256], BF16)
            wo = self.sb(es, "wo", [64, 16, 1024], BF16)
            cos = self.sb(es, "cos", [128, TL], F32)
            sin = self.sb(es, "sin", [128, TL], F32)
            esink = self.sb(es, "esink", [1, 2048], BF16)
            sinkf = self.sb(es, "sinkf", [1, 2048], F32)
            mL = self.sb(es, "mL", [128, 512], BF16)
            mU = self.sb(es, "mU", [128, 512], BF16)
            for w, n in ((wq, "attn_wq"), (wqp, "attn_wqp"), (wk, "attn_wk"), (wkp, "attn_wkp"), (wv, "attn_wv"), (wo, "attn_wo"), (mL, "attn_maskL"), (mU, "attn_maskU")):
                self.load_w(w, w[:], dr[n])
            S.dma("sp", lambda e: e.dma_start(out=cos[:], in_=dr["attn_cos"][:, :]), writes=[cos])
            S.dma("sp", lambda e: e.dma_start(out=sin[:], in_=dr["attn_sin"][:, :]), writes=[sin])
            S.dma("sp", lambda e: e.dma_start(out=sinkf[:], in_=dr["attn_sinkrow"][:, :]), writes=[sinkf])
            S.op("act", lambda e: e.activation(out=esink[:], in_=sinkf[:], func=AF.Exp), reads=[sinkf], writes=[esink])
            kT = self.sb(es, "kT", [128, 4, T], BF16)
            V = self.sb(es, "V", [128, T // 128, 256], BF16)
            xts = [self.sb(es, f"ax{i}", [128, 8, W], F32) for i in range(2)]
            hT = self.sb(es, "ah", [128, 8, W], BF16)
            qT = self.sb(es, "aq", [128, 8, W], BF16)
            OT = self.sb(es, "aO", [64, 16, W], BF16)
            y = self.sb(es, "ay", [128, 8, W], F32)
            rstd = self.sb(es, "arstd", [128, W], F32)
            ta = [self.sb(es, f"ata{i}", [128, W], F32) for i in range(2)]
            tb = [self.sb(es, f"atb{i}", [128, W], F32) for i in range(2)]
            Et = [self.sb(es, f"aE{i}", [128, 512], BF16) for i in range(3)]
            rden = [self.sb(es, f"ard{i}", [64, 512], F32) for i in range(2)]
            kx = 0
            ei = 0
            ri = 0

            def rope(i, p1, p2, tl, out_ap, outtile):
                a, b2 = ta[i % 2], tb[i % 2]
                S.op("dve", lambda e: e.tensor_tensor(out=a[:], in0=p1[:, :W], in1=cos[:, tl:tl + W], op=ALU.mult), reads=[p1, cos], writes=[a])
                S.op("dve", lambda e: e.tensor_tensor(out=b2[:], in0=p2[:, :W], in1=sin[:, tl:tl + W], op=ALU.mult), reads=[p2, sin], writes=[b2])
                S.op("pool", lambda e: e.tensor_tensor(out=out_ap, in0=a[:], in1=b2[:], op=ALU.add), reads=[a, b2], writes=[outtile])

            for b in range(self.nb):
                for t0 in range(0, T, W):
                    if ATT_STOP <= 0:
                        continue
                    lat = t0 >= TC
                    col = b if lat else 4
                    xt = xts[kx % 2]
                    kx += 1
                    self.load_x(xt, b, t0, W)
                    self.rstd_of(xt, W, rstd)
                    self.modulate(xt, W, rstd, layer, 0, 1, col, hT)
                    for g in range(4):
                        p1 = self.ps()
                        for kc in range(8):
                            S.op("pe", lambda e, p1=p1, kc=kc, g=g: e.matmul(p1[:, :W], lhsT=wk[:, kc, g * 128:(g + 1) * 128], rhs=hT[:, kc, :], start=(kc == 0), stop=(kc == 7)),
                                 reads=[wk, hT], writes=[p1])
                        if lat:
                            p2 = self.ps()
                            for kc in range(8):
                                S.op("pe", lambda e, p2=p2, kc=kc, g=g: e.matmul(p2[:, :W], lhsT=wkp[:, kc, g * 128:(g + 1) * 128], rhs=hT[:, kc, :], start=(kc == 0), stop=(kc == 7)),
                                     reads=[wkp, hT], writes=[p2])
                            rope(g, p1, p2, t0 - TC, kT[:, g, t0:t0 + W], kT)
                        else:
                            self.evac(g, kT[:, g, t0:t0 + W], p1[:, :W], [p1], [kT])
                    for tt in range(W // 128):
                        p = self.ps()
                        for kc in range(8):
                            S.op("pe", lambda e, p=p, kc=kc, tt=tt: e.matmul(p[:, :256], lhsT=hT[:, kc, tt * 128:(tt + 1) * 128], rhs=wv[:, kc, :], start=(kc == 0), stop=(kc == 7)),
                                 reads=[wv, hT], writes=[p])
                        self.evac(tt, V[:, t0 // 128 + tt, :], p[:, :256], [p], [V])
                for t0 in range(0, T, W):
                    lat = t0 >= TC
                    if not lat and not need_ctx:
                        continue
                    if ATT_STOP <= 1:
                        continue
                    col = b if lat else 4
                    xt = xts[kx % 2]
                    kx += 1
                    self.load_x(xt, b, t0, W)
                    self.rstd_of(xt, W, rstd)
                    self.modulate(xt, W, rstd, layer, 0, 1, col, hT)
                    for c in range(8):
                        p1 = self.ps()
                        for kc in range(8):
                            S.op("pe", lambda e, p1=p1, kc=kc, c=c: e.matmul(p1[:, :W], lhsT=wq[:, kc, c * 128:(c + 1) * 128], rhs=hT[:, kc, :], start=(kc == 0), stop=(kc == 7)),
                                 reads=[wq, hT], writes=[p1])
                        if lat:
                            p2 = self.ps()
                            for kc in range(8):
                                S.op("pe", lambda e, p2=p2, kc=kc, c=c: e.matmul(p2[:, :W], lhsT=wqp[:, kc, c * 128:(c + 1) * 128], rhs=hT[:, kc, :], start=(kc == 0), stop=(kc == 7)),
                                     reads=[wqp, hT], writes=[p2])
                            rope(c, p1, p2, t0 - TC, qT[:, c, :], qT)
                        else:
                            self.evac(c, qT[:, c, :], p1[:, :W], [p1], [qT])
                    for qb in range(W // 128):
                        if ATT_STOP <= 2:
                            continue
                        q0 = qb * 128
                        if lat:
                            bi = (t0 - TC) // 128 + qb
                            chunks = []
                            if bi > 0:
                                chunks.append((2 + bi - 1, mL))
                            chunks.append((2 + bi, None))
                            if bi < 15:
                                chunks.append((2 + bi + 1, mU))
                            chunks += [(0, None), (1, None)]
                        else:
                            chunks = [(0, None), (1, None)]
                        for g in range(4):
                            pO = self.ps()
                            pD = self.ps()
                            for ci, (kt, mask) in enumerate(chunks):
                                pSs = [self.ps_ex((pO, pD)), self.ps_ex((pO, pD))]
                                for hh in range(4):
                                    h = 4 * g + hh
                                    c, s = h // 2, h % 2
                                    pS = pSs[s]
                                    S.op("pe", lambda e, pS=pS, hh=hh, c=c, s=s, kt=kt, g=g, q0=q0: e.matmul(
                                        pS[:, (hh // 2) * 128:(hh // 2 + 1) * 128], lhsT=kT[s * 64:(s + 1) * 64, g, kt * 128:(kt + 1) * 128], rhs=qT[s * 64:(s + 1) * 64, c, q0:q0 + 128], start=True, stop=True),
                                        reads=[kT, qT], writes=[pS])
                                E = Et[ei % 3]
                                ei += 1
                                for s in range(2):
                                    S.op("act", lambda e, E=E, pS=pSs[s], s=s: e.activation(out=E[:].rearrange("p (c s q) -> p c s q", c=2, s=2)[:, :, s, :], in_=pS[:, 0:256].rearrange("p (c q) -> p c q", c=2),
                                                                                          func=AF.Exp, scale=0.125), reads=[pSs[s]], writes=[E])
                                if mask is not None:
                                    S.op("pool", lambda e, E=E, mask=mask: e.tensor_tensor(out=E[:], in0=E[:], in1=mask[:], op=ALU.mult), reads=[E, mask], writes=[E])
                                last = ci == len(chunks) - 1
                                if ATT_SUB == "a":
                                    continue
                                S.op("pe", lambda e, pO=pO, E=E, kt=kt, g=g, ci=ci, last=last: e.matmul(pO[0:64, :], lhsT=V[:, kt, g * 64:(g + 1) * 64], rhs=E[:], start=(ci == 0), stop=last),
                                     reads=[V, E], writes=[pO])
                                S.op("pe", lambda e, pD=pD, E=E, ci=ci, last=last: e.matmul(pD[0:64, :], lhsT=self.ones_b[:, 0:64], rhs=E[:], start=(ci == 0), stop=(last and ATT_SUB == "b")),
                                     reads=[self.ones_b, E], writes=[pD])
                            if ATT_SUB == "a":
                                continue
                            if ATT_SUB != "b":
                                S.op("pe", lambda e, pD=pD, g=g: e.matmul(pD[0:64, :], lhsT=self.ones_b[0:1, 0:64], rhs=esink[0:1, g * 512:(g + 1) * 512], start=False, stop=True),
                                     reads=[self.ones_b, esink], writes=[pD])
                            if ATT_SUB in ("b", "c"):
                                continue
                            rd = rden[ri % 2]
                            ri += 1
                            S.op("dve", lambda e, rd=rd, pD=pD: e.reciprocal(out=rd[:], in_=pD[0:64, :]), reads=[pD], writes=[rd])
                            S.op("dve", lambda e, rd=rd, pO=pO, g=g, q0=q0: e.tensor_tensor(out=OT[:, 4 * g:4 * g + 4, q0:q0 + 128], in0=pO[0:64, :].rearrange("p (h q) -> p h q", h=4),
                                                                                          in1=rd[:].rearrange("p (h q) -> p h q", h=4), op=ALU.mult), reads=[pO, rd], writes=[OT])
                    if ATT_STOP <= 3:
                        continue
                    for c in range(8):
                        p = self.ps()
                        for h in range(16):
                            S.op("pe", lambda e, p=p, h=h, c=c: e.matmul(p[:, :W], lhsT=wo[:, h, c * 128:(c + 1) * 128], rhs=OT[:, h, :], start=(h == 0), stop=(h == 15)),
                                 reads=[wo, OT], writes=[p])
                        self.evac(c, y[:, c, :], p[:, :W], [p], [y])
                    self.rstd_of(y, W, rstd)
                    self.resid(xt, W, y, rstd, layer, 2, col)
                    self.store_x(xt, b, t0, W)
            S.barrier()
            S.flush()


def host_mixers(inp, vp, com):
    host_attn(inp, vp, com)


def scan_consts(gscale):
    s = np.arange(128)[:, None]
    t = np.arange(128)[None, :]
    same = (s // 64) == (t // 64)
    triF = (same & (s <= t)).astype(np.float32)
    triB = (same & (s >= t)).astype(np.float32)
    trisF = (same & (s > t)).astype(np.float32)
    trisB = (same & (s < t)).astype(np.float32)
    return np.ascontiguousarray(np.concatenate([triF * gscale, triB * gscale, trisF * gscale, trisB * gscale, np.tile(triF, (1, 4)), np.tile(triB, (1, 4))], axis=1).astype(np.float32))


def host_gla(inp, vp, com):
    com["gla_win"] = wlay(np.asarray(inp["gla_w_in"][0], np.float32))
    wd = np.asarray(inp["gla_w_gate_down"][0], np.float32)
    com["gla_wgd"] = wlay(np.concatenate([wd[0], wd[1]], axis=1))
    com["gla_wgu"] = np.ascontiguousarray(np.asarray(inp["gla_w_gate_up"][0], np.float32).transpose(1, 0, 2))
    com["gla_gbias"] = np.ascontiguousarray(np.asarray(inp["gla_gate_bias"][0], np.float32).reshape(1, 1024))
    com["gla_wo"] = wlay(np.asarray(inp["gla_w_o"][0], np.float32))
    com["gla_consts"] = scan_consts(1.0 / 16.0)
    vp.add("gla_gn", fm_vec(np.tile(np.asarray(inp["gla_g_norm"][0], np.float32), 4)))


def host_hgrn(inp, vp, com):
    com["hgrn_win"] = wlay(np.asarray(inp["hgrn_w_in"][0], np.float32))
    wf = np.asarray(inp["hgrn_w_f"][0], np.float32)
    com["hgrn_wf"] = np.stack([wlay(wf[0]), wlay(wf[1])])
    com["hgrn_wo"] = wlay(np.asarray(inp["hgrn_w_o"][0], np.float32))
    com["hgrn_consts"] = scan_consts(1.0)
    lb = np.asarray(inp["hgrn_lb"], np.float32)
    vp.add("hgrn_lb", np.concatenate([fm_vec(lb[l]) for l in range(4)], axis=1))
    com["hgrn_lbrow"] = np.ascontiguousarray(np.broadcast_to(lb.reshape(1, 4096), (128, 4096)))
    vp.add("hgrn_gn", fm_vec(np.tile(np.asarray(inp["hgrn_g_norm"][0], np.float32), 8)))


class ScanMixin:
    def scan_setup(self, es, consts_d, H, dv):
        S = self.S
        c = self.sb(es, "sconst", [128, 4 * 128 + 2 * 512], F32)
        S.dma("sp", lambda e: e.dma_start(out=c[:], in_=consts_d[:, :]), writes=[c])
        self.sc = c
        self.sH, self.sdv = H, dv
        G = H // 4
        self.sG = G
        mk = lambda n, shp, dt: self.sb(es, n, shp, dt)
        self.s_eb = [mk(f"s_eb{g}", [128, 4, 128], F32) for g in range(G)]
        self.s_emb = mk("s_emb", [128, 4, 128], F32)
        self.s_qd = [mk(f"s_qd{g}", [128, 4, 128], BF16) for g in range(G)]
        self.s_ki = mk("s_ki", [128, 4, 128], BF16)
        self.s_esuf = mk("s_esuf", [128, 512], F32)
        self.s_kend = [mk(f"s_kend{g}", [128, 512], BF16) for g in range(G)]
        self.s_AmT = [mk(f"s_AmT{g}", [128, 4, 128], BF16) for g in range(G)]
        self.s_S = mk("s_S", [128, H, dv], F32)
        self.s_Sb = mk("s_Sb", [128, H, dv], BF16)
        self.s_ob = [mk(f"s_ob{i}", [128, 8, 128], F32) for i in range(2)]
        self.s_ol = mk("s_ol", [128, 8, 128], F32)
        self._obi = 0

    def scan_reset(self):
        self.S.op("pool", lambda e: e.memset(self.s_S[:], 0.0), writes=[self.s_S])
        self.S.op("pool", lambda e: e.memset(self.s_Sb[:], 0.0), writes=[self.s_Sb])

    def scan_block(self, d, qf, kf, ktm, vtm, gtm, OACC, blk, first_dir):
        S = self.S
        sc = self.sc
        H, dv, G = self.sH, self.sdv, self.sG
        dvc = dv // 128
        tri = sc[:, d * 128:(d + 1) * 128]
        tris = sc[:, 256 + d * 128:256 + (d + 1) * 128]
        mask4 = sc[:, 512 + d * 512:512 + (d + 1) * 512]
        for g in range(G):
            eb, emb, qd, ki, kend, AmT = self.s_eb[g], self.s_emb, self.s_qd[g], self.s_ki, self.s_kend[g], self.s_AmT[g]
            pb = self.ps()
            for hh in range(4):
                h = g * 4 + hh
                S.op("pe", lambda e, pb=pb, hh=hh, h=h: e.matmul(pb[:, hh * 128:(hh + 1) * 128], lhsT=gtm[:, h * 128:(h + 1) * 128], rhs=tri, start=True, stop=True),
                     reads=[gtm, sc], writes=[pb])
            S.op("act", lambda e, pb=pb, eb=eb: e.activation(out=eb[:].rearrange("p h t -> p (h t)"), in_=pb[:], func=AF.Exp), reads=[pb], writes=[eb])
            S.op("act", lambda e, pb=pb, emb=emb: e.activation(out=emb[:].rearrange("p h t -> p (h t)"), in_=pb[:], func=AF.Exp, scale=-1.0), reads=[pb], writes=[emb])
            S.op("dve", lambda e, g=g, qd=qd, eb=eb: e.tensor_tensor(out=qd[:], in0=qf[:, g * 4:(g + 1) * 4, :], in1=eb[:], op=ALU.mult), reads=[qf, eb], writes=[qd])
            S.op("pool", lambda e, g=g, ki=ki, emb=emb: e.tensor_tensor(out=ki[:], in0=kf[:, g * 4:(g + 1) * 4, :], in1=emb[:], op=ALU.mult), reads=[kf, emb], writes=[ki])
            psf = self.ps()
            S.op("pe", lambda e, psf=psf, g=g: e.matmul(psf[:], lhsT=tris, rhs=gtm[:, g * 512:(g + 1) * 512], start=True, stop=True), reads=[gtm, sc], writes=[psf])
            S.op("act", lambda e, psf=psf: e.activation(out=self.s_esuf[:], in_=psf[:], func=AF.Exp), reads=[psf], writes=[self.s_esuf])
            S.op("dve", lambda e, kend=kend, g=g: e.tensor_tensor(out=kend[:], in0=ktm[:, g * 512:(g + 1) * 512], in1=self.s_esuf[:], op=ALU.mult), reads=[ktm, self.s_esuf], writes=[kend])
            pA = self.ps()
            for hh in range(4):
                S.op("pe", lambda e, pA=pA, hh=hh, ki=ki, qd=qd: e.matmul(pA[:, hh * 128:(hh + 1) * 128], lhsT=ki[:, hh, :], rhs=qd[:, hh, :], start=True, stop=True), reads=[ki, qd], writes=[pA])
            S.op("dve", lambda e, pA=pA, AmT=AmT: e.tensor_tensor(out=AmT[:].rearrange("p h t -> p (h t)"), in0=pA[:], in1=mask4, op=ALU.mult), reads=[pA, sc], writes=[AmT])
        ob = self.s_ob[self._obi % 2]
        self._obi += 1
        po = [self.ps(), self.ps()]

        def po_ap(oc, lo, hi):
            return po[oc // 4][:, (oc % 4) * 128 + lo:(oc % 4) * 128 + hi]

        for h in range(H):
            for vc in range(dvc):
                oc = h * dvc + vc
                S.op("pe", lambda e, h=h, vc=vc, oc=oc: e.matmul(po_ap(oc, 0, 128), lhsT=vtm[:, h * dv + vc * 128:h * dv + (vc + 1) * 128], rhs=self.s_AmT[h // 4][:, h % 4, :], start=(oc % 4 == 0), stop=False, skip_group_check=True),
                     reads=[vtm, self.s_AmT[h // 4]], writes=[po[oc // 4]])
        order = (0, 1) if d == 0 else (1, 0)
        for cc in order:
            lo, hi = cc * 64, (cc + 1) * 64
            for h in range(H):
                for vc in range(dvc):
                    oc = h * dvc + vc
                    S.op("pe", lambda e, h=h, vc=vc, oc=oc, lo=lo, hi=hi: e.matmul(po_ap(oc, lo, hi), lhsT=self.s_Sb[:, h, vc * 128:(vc + 1) * 128], rhs=self.s_qd[h // 4][:, h % 4, lo:hi], start=False, stop=True, skip_group_check=True),
                         reads=[self.s_Sb, self.s_qd[h // 4]], writes=[po[oc // 4]])
            colb = (63 if cc == 0 else 127) if d == 0 else (0 if cc == 0 else 64)
            hpb = 512 // dv
            for h0 in range(0, H, hpb):
                pS = self.ps()
                for h in range(h0, h0 + hpb):
                    S.op("pe", lambda e, pS=pS, h=h, h0=h0, lo=lo, hi=hi: e.matmul(pS[:, (h - h0) * dv:(h - h0 + 1) * dv], lhsT=self.s_kend[h // 4][lo:hi, (h % 4) * 128:(h % 4 + 1) * 128],
                                                                                rhs=vtm[lo:hi, h * dv:(h + 1) * dv], start=True, stop=True), reads=[self.s_kend[h // 4], vtm], writes=[pS])
                for h in range(h0, h0 + hpb):
                    S.op("dve", lambda e, pS=pS, h=h, h0=h0, colb=colb: e.scalar_tensor_tensor(out=self.s_S[:, h, :], in0=self.s_S[:, h, :], scalar=self.s_eb[h // 4][:, h % 4, colb:colb + 1],
                                                                                             in1=pS[:, (h - h0) * dv:(h - h0 + 1) * dv], op0=ALU.mult, op1=ALU.add),
                         reads=[pS, self.s_eb[h // 4], self.s_S], writes=[self.s_S])
            S.op("pool", lambda e: e.tensor_copy(out=self.s_Sb[:], in_=self.s_S[:]), reads=[self.s_S], writes=[self.s_Sb])
        for i in range(2):
            self.evac(i, ob[:, i * 4:(i + 1) * 4, :], po[i][:].rearrange("p (c t) -> p c t", c=4), [po[i]], [ob])
        t0 = blk * 128
        if not first_dir:
            ol = self.s_ol
            S.dma("sp", lambda e: e.dma_start(out=ol[:], in_=OACC.t[:, :, t0:t0 + 128].rearrange("c p t -> p c t")), reads=[OACC], writes=[ol])
            S.op("pool", lambda e: e.tensor_tensor(out=ob[:], in0=ob[:], in1=ol[:], op=ALU.add), reads=[ob, ol], writes=[ob])
        S.dma("sp", lambda e: e.dma_start(out=OACC.t[:, :, t0:t0 + 128].rearrange("c p t -> p c t"), in_=ob[:]), reads=[ob], writes=[OACC])

    def fill_hTall(self, hTall, xts, rstd, b, layer, W=256):
        k = 0
        for t0 in range(0, T, W):
            col = b if t0 >= TC else 4
            xt = xts[k % 2]
            k += 1
            self.load_x(xt, b, t0, W)
            self.rstd_of(xt, W, rstd)
            self.modulate(xt, W, rstd, layer, 0, 1, col, hTall, hlo=t0)

    def phase_out(self, layer, b, OACC, hTall, w_gate, gate_c0, wo, gn_name, H, xts, rstd, og, y, oin, sg):
        S = self.S
        W = 256
        need_ctx = layer != self.last_layer
        cph = 8 // H if H <= 8 else 1
        k = 0
        for t0 in range(0, T, W):
            lat = t0 >= TC
            if not lat and not need_ctx:
                continue
            col = b if lat else 4
            xt = xts[k % 2]
            k += 1
            self.load_x(xt, b, t0, W)
            S.dma("sp", lambda e, t0=t0: e.dma_start(out=oin[:], in_=OACC.t[:, :, t0:t0 + W].rearrange("c p t -> p c t")), reads=[OACC], writes=[oin])
            for c in range(8):
                p = self.ps()
                for kc in range(8):
                    S.op("pe", lambda e, p=p, kc=kc, c=c, t0=t0: e.matmul(p[:, :W], lhsT=w_gate[:, kc, gate_c0 + c * 128:gate_c0 + (c + 1) * 128], rhs=hTall[:, kc, t0:t0 + W], start=(kc == 0), stop=(kc == 7)),
                         reads=[w_gate, hTall], writes=[p])
                S.op("act", lambda e, p=p, c=c: e.activation(out=sg[:, c, :], in_=p[:, :W], func=AF.Silu), reads=[p], writes=[sg])
            nchh = 8 // H
            for h in range(H):
                self.rstd_of(oin, W, rstd, nch=nchh, scale=1.0 / (128 * nchh), c0=h * nchh)
                for cc in range(nchh):
                    c = h * nchh + cc
                    tmp = self.tmpf[c % 2]
                    S.op("dve", lambda e, tmp=tmp, c=c: e.scalar_tensor_tensor(out=tmp[:, :W], in0=oin[:, c, :], scalar=self.vcol(gn_name, c), in1=rstd[:, :W], op0=ALU.mult, op1=ALU.mult),
                         reads=[oin, rstd, self.vecs], writes=[tmp])
                    S.op("pool", lambda e, tmp=tmp, c=c: e.tensor_tensor(out=og[:, c, :], in0=tmp[:, :W], in1=sg[:, c, :], op=ALU.mult), reads=[tmp, sg], writes=[og])
            for c in range(8):
                p = self.ps()
                for kc in range(8):
                    S.op("pe", lambda e, p=p, kc=kc, c=c: e.matmul(p[:, :W], lhsT=wo[:, kc, c * 128:(c + 1) * 128], rhs=og[:, kc, :], start=(kc == 0), stop=(kc == 7)), reads=[wo, og], writes=[p])
                self.evac(c, y[:, c, :], p[:, :W], [p], [y])
            self.rstd_of(y, W, rstd)
            self.resid(xt, W, y, rstd, layer, 2, col)
            self.store_x(xt, b, t0, W)


SCAN_ORDER = {0: list(range(18)), 1: [1, 0] + list(range(17, 1, -1))}


class FullProg2(ScanMixin, FullProg):
    def declare_mixer_inputs(self):
        FullProg.declare_mixer_inputs(self)
        if 1 in self.layers:
            self.din("gla_win", [128, 8, 3072])
            self.din("gla_wgd", [128, 8, 32])
            self.din("gla_wgu", [16, 2, 512])
            self.din("gla_gbias", [1, 1024])
            self.din("gla_wo", [128, 8, 1024])
            self.din("gla_consts", [128, 1536])
        if 3 in self.layers:
            self.din("hgrn_win", [128, 8, 3072])
            self.din("hgrn_wf", [2, 128, 8, 1024])
            self.din("hgrn_wo", [128, 8, 1024])
            self.din("hgrn_consts", [128, 1536])
            self.din("hgrn_lbrow", [128, 4096])
        if 1 in self.layers or 3 in self.layers or 2 in self.layers:
            self.OACC = [Tile(self.dscratch(f"OACC{b}", [8, 128, T]), f"OACC{b}") for b in range(self.nb)]

    def stage_mixer(self, layer):
        if layer == 0:
            self.stage_attn(layer)
        elif layer == 1:
            self.stage_gla(layer)
        elif layer == 3:
            self.stage_hgrn(layer)
        elif layer == 2:
            self.stage_rwkv(layer)

    def stage_out_part(self, layer, w_gate_d, gate_c0, wo_d, gn_name, H):
        S = self.S
        W = 256
        with ExitStack() as es:
            wg = self.sb(es, "po_wg", [128, 8, 1024], BF16)
            wo = self.sb(es, "po_wo", [128, 8, 1024], BF16)
            self.load_w(wg, wg[:], w_gate_d[:, :, gate_c0:gate_c0 + 1024])
            self.load_w(wo, wo[:], wo_d)
            xts = [self.sb(es, f"po_x{i}", [128, 8, W], F32) for i in range(2)]
            rstd = self.sb(es, "po_rstd", [128, W], F32)
            hT = self.sb(es, "po_h", [128, 8, W], BF16)
            og = self.sb(es, "po_og", [128, 8, W], BF16)
            y = self.sb(es, "po_y", [128, 8, W], F32)
            oin = self.sb(es, "po_oin", [128, 8, W], F32)
            sg = self.sb(es, "po_sg", [128, 8, W], F32)
            need_ctx = layer != self.last_layer
            nchh = 8 // H
            k = 0
            for b in range(self.nb):
                OACC = self.OACC[b]
                for t0 in range(0, T, W):
                    lat = t0 >= TC
                    if not lat and not need_ctx:
                        continue
                    col = b if lat else 4
                    xt = xts[k % 2]
                    k += 1
                    self.load_x(xt, b, t0, W)
                    self.rstd_of(xt, W, rstd)
                    self.modulate(xt, W, rstd, layer, 0, 1, col, hT)
                    S.dma("sp", lambda e, t0=t0, OACC=OACC: e.dma_start(out=oin[:], in_=OACC.t[:, :, t0:t0 + W].rearrange("c p t -> p c t")), reads=[OACC], writes=[oin])
                    for c in range(8):
                        p = self.ps()
                        for kc in range(8):
                            S.op("pe", lambda e, p=p, kc=kc, c=c: e.matmul(p[:, :W], lhsT=wg[:, kc, c * 128:(c + 1) * 128], rhs=hT[:, kc, :], start=(kc == 0), stop=(kc == 7)), reads=[wg, hT], writes=[p])
                        S.op("act", lambda e, p=p, c=c: e.activation(out=sg[:, c, :], in_=p[:, :W], func=AF.Silu), reads=[p], writes=[sg])
                    for h in range(H):
                        self.rstd_of(oin, W, rstd, nch=nchh, scale=1.0 / (128 * nchh), c0=h * nchh)
                        for cc in range(nchh):
                            c = h * nchh + cc
                            tmp = self.tmpf[c % 2]
                            S.op("dve", lambda e, tmp=tmp, c=c: e.scalar_tensor_tensor(out=tmp[:, :W], in0=oin[:, c, :], scalar=self.vcol(gn_name, c), in1=rstd[:, :W], op0=ALU.mult, op1=ALU.mult),
                                 reads=[oin, rstd, self.vecs], writes=[tmp])
                            S.op("pool", lambda e, tmp=tmp, c=c: e.tensor_tensor(out=og[:, c, :], in0=tmp[:, :W], in1=sg[:, c, :], op=ALU.mult), reads=[tmp, sg], writes=[og])
                    for c in range(8):
                        p = self.ps()
                        for kc in range(8):
                            S.op("pe", lambda e, p=p, kc=kc, c=c: e.matmul(p[:, :W], lhsT=wo[:, kc, c * 128:(c + 1) * 128], rhs=og[:, kc, :], start=(kc == 0), stop=(kc == 7)), reads=[wo, og], writes=[p])
                        self.evac(c, y[:, c, :], p[:, :W], [p], [y])
                    self.rstd_of(y, W, rstd)
                    self.resid(xt, W, y, rstd, layer, 2, col)
                    self.store_x(xt, b, t0, W)
            S.barrier()
            S.flush()

    def stage_gla(self, layer):
        S = self.S
        dr = self.dram
        with ExitStack() as es:
            win = self.sb(es, "g_win", [128, 8, 2048], BF16)
            wgd = self.sb(es, "g_wgd", [128, 8, 32], BF16)
            wgu = self.sb(es, "g_wgu", [16, 2, 512], BF16)
            gbias = self.sb(es, "g_gb", [1, 1024], BF16)
            self.load_w(win, win[:], dr["gla_win"][:, :, 0:2048])
            self.load_w(wgd, wgd[:], dr["gla_wgd"])
            self.load_w(wgu, wgu[:], dr["gla_wgu"])
            self.load_w(gbias, gbias[:], dr["gla_gbias"])
            self.scan_setup(es, dr["gla_consts"], 4, 256)
            hTall = self.sb(es, "g_hT", [128, 8, T], BF16)
            xts = [self.sb(es, f"g_x{i}", [128, 8, 256], F32) for i in range(2)]
            rstd = self.sb(es, "g_rstd", [128, 256], F32)
            qf = self.sb(es, "g_qf", [128, 4, 128], F32)
            kf = self.sb(es, "g_kf", [128, 4, 128], F32)
            ktm = self.sb(es, "g_ktm", [128, 512], BF16)
            vtm = self.sb(es, "g_vtm", [128, 1024], BF16)
            gtm = self.sb(es, "g_gtm", [128, 512], F32)
            sig = self.sb(es, "g_sig", [128, 512], F32)
            hdT = self.sb(es, "g_hdT", [16, 128], BF16)
            for b in range(self.nb):
                self.fill_hTall(hTall, xts, rstd, b, layer)
                for d in (0, 1):
                    self.scan_reset()
                    for blk in SCAN_ORDER[d]:
                        ts = blk * 128
                        pq = self.ps()
                        pk = self.ps()
                        for hh in range(4):
                            for kc in range(8):
                                S.op("pe", lambda e, pq=pq, hh=hh, kc=kc, ts=ts: e.matmul(pq[:, hh * 128:(hh + 1) * 128], lhsT=win[:, kc, hh * 128:(hh + 1) * 128], rhs=hTall[:, kc, ts:ts + 128], start=(kc == 0), stop=(kc == 7)),
                                     reads=[win, hTall], writes=[pq])
                        for hh in range(4):
                            for kc in range(8):
                                S.op("pe", lambda e, pk=pk, hh=hh, kc=kc, ts=ts: e.matmul(pk[:, hh * 128:(hh + 1) * 128], lhsT=win[:, kc, 512 + hh * 128:512 + (hh + 1) * 128], rhs=hTall[:, kc, ts:ts + 128], start=(kc == 0), stop=(kc == 7)),
                                     reads=[win, hTall], writes=[pk])
                        S.op("act", lambda e, pq=pq: e.activation(out=qf[:].rearrange("p h t -> p (h t)"), in_=pq[:], func=AF.Copy, scale=float(128 ** -0.5)), reads=[pq], writes=[qf])
                        S.op("dve", lambda e, pk=pk: e.tensor_copy(out=kf[:].rearrange("p h t -> p (h t)"), in_=pk[:]), reads=[pk], writes=[kf])
                        pkt = self.ps()
                        for kc in range(8):
                            S.op("pe", lambda e, pkt=pkt, kc=kc, ts=ts: e.matmul(pkt[:], lhsT=hTall[:, kc, ts:ts + 128], rhs=win[:, kc, 512:1024], start=(kc == 0), stop=(kc == 7)), reads=[win, hTall], writes=[pkt])
                        S.op("act", lambda e, pkt=pkt: e.activation(out=ktm[:], in_=pkt[:], func=AF.Copy), reads=[pkt], writes=[ktm])
                        for half in range(2):
                            pv = self.ps()
                            for kc in range(8):
                                S.op("pe", lambda e, pv=pv, kc=kc, ts=ts, half=half: e.matmul(pv[:], lhsT=hTall[:, kc, ts:ts + 128], rhs=win[:, kc, 1024 + half * 512:1024 + (half + 1) * 512], start=(kc == 0), stop=(kc == 7)),
                                     reads=[win, hTall], writes=[pv])
                            self.evac(half, vtm[:, half * 512:(half + 1) * 512], pv[:], [pv], [vtm])
                        phd = self.ps()
                        for kc in range(8):
                            S.op("pe", lambda e, phd=phd, kc=kc, ts=ts, d=d: e.matmul(phd[0:16, 0:128], lhsT=wgd[:, kc, d * 16:(d + 1) * 16], rhs=hTall[:, kc, ts:ts + 128], start=(kc == 0), stop=(kc == 7)),
                                 reads=[wgd, hTall], writes=[phd])
                        S.op("dve", lambda e, phd=phd: e.tensor_copy(out=hdT[:], in_=phd[0:16, 0:128]), reads=[phd], writes=[hdT])
                        pz = self.ps()
                        S.op("pe", lambda e, pz=pz, d=d: e.matmul(pz[:], lhsT=hdT[:, :], rhs=wgu[:, d, :], start=True, stop=False), reads=[hdT, wgu], writes=[pz])
                        S.op("pe", lambda e, pz=pz, d=d: e.matmul(pz[:], lhsT=self.ones_b[0:1, :], rhs=gbias[0:1, d * 512:(d + 1) * 512], start=False, stop=True), reads=[self.ones_b, gbias], writes=[pz])
                        S.op("act", lambda e, pz=pz: e.activation(out=sig[:], in_=pz[:], func=AF.Sigmoid), reads=[pz], writes=[sig])
                        S.op("act", lambda e: e.activation(out=gtm[:], in_=sig[:], func=AF.Ln), reads=[sig], writes=[gtm])
                        self.scan_block(d, qf, kf, ktm, vtm, gtm, self.OACC[b], blk, d == 0)
            S.barrier()
            S.flush()
        self.stage_out_part(layer, dr["gla_win"], 2048, dr["gla_wo"], "gla_gn", 4)

    def stage_hgrn(self, layer):
        S = self.S
        dr = self.dram
        with ExitStack() as es:
            win = self.sb(es, "h_win", [128, 8, 2048], BF16)
            wf = [self.sb(es, f"h_wf{d}", [128, 8, 1024], BF16) for d in range(2)]
            self.load_w(win, win[:], dr["hgrn_win"][:, :, 0:2048])
            for d in range(2):
                self.load_w(wf[d], wf[d][:], dr["hgrn_wf"][d])
            self.scan_setup(es, dr["hgrn_consts"], 8, 128)
            lb_row = self.sb(es, "h_lbrow", [128, 1024], F32)
            oml_row = self.sb(es, "h_omlrow", [128, 1024], F32)
            lb_f = self.sb(es, "h_lbfm", [128, 8], F32)
            oml_f = self.sb(es, "h_omlfm", [128, 8], F32)
            es_outer = es
            es = es_tmp = ExitStack()
            lbr = self.sb(es, "h_lbr", [128, 4096], F32)
            S.dma("sp", lambda e: e.dma_start(out=lbr[:], in_=dr["hgrn_lbrow"][:, :]), writes=[lbr])
            S.op("act", lambda e: e.activation(out=lbr[:], in_=lbr[:], func=AF.Exp), reads=[lbr], writes=[lbr])
            lbf = self.sb(es, "h_lbf", [128, 32], F32)
            o = self.voff["hgrn_lb"][0]
            S.op("act", lambda e: e.activation(out=lbf[:], in_=self.vecs[:, o:o + 32], func=AF.Exp), reads=[self.vecs], writes=[lbf])
            den_row = self.sb(es, "h_denrow", [128, 1024], F32)
            den_f = self.sb(es, "h_denfm", [128, 8], F32)
            for (src, n, num, oml, den) in ((lbr, 1024, lb_row, oml_row, den_row), (lbf, 8, lb_f, oml_f, den_f)):
                if layer == 0:
                    S.op("dve", lambda e, num=num: e.memset(num[:], 0.0), writes=[num])
                else:
                    S.op("dve", lambda e, num=num, src=src, n=n: e.tensor_copy(out=num[:], in_=src[:, n:2 * n]), reads=[src], writes=[num])
                    for j in range(2, layer + 1):
                        S.op("dve", lambda e, num=num, src=src, n=n, j=j: e.tensor_tensor(out=num[:], in0=num[:], in1=src[:, j * n:(j + 1) * n], op=ALU.add), reads=[src, num], writes=[num])
                S.op("dve", lambda e, num=num, src=src, n=n, den=den: e.tensor_tensor(out=den[:], in0=num[:], in1=src[:, 0:n], op=ALU.add), reads=[src, num], writes=[den])
                S.op("dve", lambda e, den=den: e.reciprocal(out=den[:], in_=den[:]), reads=[den], writes=[den])
                S.op("dve", lambda e, num=num, den=den: e.tensor_tensor(out=num[:], in0=num[:], in1=den[:], op=ALU.mult), reads=[num, den], writes=[num])
                S.op("dve", lambda e, oml=oml, src=src, n=n, den=den: e.tensor_tensor(out=oml[:], in0=src[:, 0:n], in1=den[:], op=ALU.mult), reads=[src, den], writes=[oml])
            S.barrier()
            S.flush()
            es_tmp.close()
            es = es_outer
            hTall = self.sb(es, "h_hT", [128, 8, T], BF16)
            xts = [self.sb(es, f"h_x{i}", [128, 8, 256], F32) for i in range(2)]
            rstd = self.sb(es, "h_rstd", [128, 256], F32)
            qf = self.sb(es, "h_qf", [128, 8, 128], F32)
            kf = self.sb(es, "h_kf", [128, 8, 128], F32)
            sgn = self.sb(es, "h_sgn", [128, 8, 128], F32)
            ktm = self.sb(es, "h_ktm", [128, 1024], BF16)
            vtm = self.sb(es, "h_vtm", [128, 1024], BF16)
            gtm = self.sb(es, "h_gtm", [128, 1024], F32)
            ftm = self.sb(es, "h_ftm", [128, 1024], F32)
            for b in range(self.nb):
                self.fill_hTall(hTall, xts, rstd, b, layer)
                for d in (0, 1):
                    self.scan_reset()
                    for blk in SCAN_ORDER[d]:
                        ts = blk * 128
                        for g in range(2):
                            pq = self.ps()
                            pz = self.ps()
                            for hh in range(4):
                                c = g * 4 + hh
                                for kc in range(8):
                                    S.op("pe", lambda e, pq=pq, hh=hh, c=c, kc=kc, ts=ts: e.matmul(pq[:, hh * 128:(hh + 1) * 128], lhsT=win[:, kc, c * 128:(c + 1) * 128], rhs=hTall[:, kc, ts:ts + 128], start=(kc == 0), stop=(kc == 7)),
                                         reads=[win, hTall], writes=[pq])
                            for hh in range(4):
                                c = g * 4 + hh
                                for kc in range(8):
                                    S.op("pe", lambda e, pz=pz, hh=hh, c=c, kc=kc, ts=ts, d=d: e.matmul(pz[:, hh * 128:(hh + 1) * 128], lhsT=wf[d][:, kc, c * 128:(c + 1) * 128], rhs=hTall[:, kc, ts:ts + 128], start=(kc == 0), stop=(kc == 7)),
                                         reads=[wf[d], hTall], writes=[pz])
                            S.op("act", lambda e, pq=pq, g=g: e.activation(out=qf[:, g * 4:(g + 1) * 4, :].rearrange("p h t -> p (h t)"), in_=pq[:], func=AF.Silu), reads=[pq], writes=[qf])
                            S.op("act", lambda e, pz=pz, g=g: e.activation(out=sgn[:, g * 4:(g + 1) * 4, :].rearrange("p h t -> p (h t)"), in_=pz[:], func=AF.Sigmoid, scale=-1.0), reads=[pz], writes=[sgn])
                            for hh in range(4):
                                c = g * 4 + hh
                                S.op("pool", lambda e, c=c: e.tensor_scalar(out=kf[:, c, :], in0=sgn[:, c, :], scalar1=oml_f[:, c:c + 1], scalar2=None, op0=ALU.mult), reads=[sgn, oml_f], writes=[kf])
                        for half in range(2):
                            pf = self.ps()
                            for kc in range(8):
                                S.op("pe", lambda e, pf=pf, kc=kc, ts=ts, half=half, d=d: e.matmul(pf[:], lhsT=hTall[:, kc, ts:ts + 128], rhs=wf[d][:, kc, half * 512:(half + 1) * 512], start=(kc == 0), stop=(kc == 7)),
                                     reads=[wf[d], hTall], writes=[pf])
                            S.op("act", lambda e, pf=pf, half=half: e.activation(out=ftm[:, half * 512:(half + 1) * 512], in_=pf[:], func=AF.Sigmoid), reads=[pf], writes=[ftm])
                            pv = self.ps()
                            for kc in range(8):
                                S.op("pe", lambda e, pv=pv, kc=kc, ts=ts, half=half: e.matmul(pv[:], lhsT=hTall[:, kc, ts:ts + 128], rhs=win[:, kc, 1024 + half * 512:1024 + (half + 1) * 512], start=(kc == 0), stop=(kc == 7)),
                                     reads=[win, hTall], writes=[pv])
                            self.evac(half + 1, vtm[:, half * 512:(half + 1) * 512], pv[:], [pv], [vtm])
                        S.op("dve", lambda e: e.tensor_tensor(out=ftm[:], in0=ftm[:], in1=oml_row[:], op=ALU.mult), reads=[ftm, oml_row], writes=[ftm])
                        S.op("dve", lambda e: e.tensor_tensor(out=ftm[:], in0=ftm[:], in1=lb_row[:], op=ALU.add), reads=[ftm, lb_row], writes=[ftm])
                        S.op("act", lambda e: e.activation(out=gtm[:], in_=ftm[:], func=AF.Ln), reads=[ftm], writes=[gtm])
                        S.op("pool", lambda e: e.tensor_scalar(out=ktm[:], in0=ftm[:], scalar1=-1.0, scalar2=1.0, op0=ALU.mult, op1=ALU.add), reads=[ftm], writes=[ktm])
                        self.scan_block(d, qf, kf, ktm, vtm, gtm, self.OACC[b], blk, d == 0)
            S.barrier()
            S.flush()
        self.stage_out_part(layer, dr["hgrn_win"], 2048, dr["hgrn_wo"], "hgrn_gn", 8)


def host_mixers(inp, vp, com):
    host_attn(inp, vp, com)
    host_gla(inp, vp, com)
    host_hgrn(inp, vp, com)


def host_rwkv(inp, vp, com):
    g = lambda n: np.asarray(inp[n][0], np.float32)
    wrkv = g("rwkv_w_rkv")
    com["rw_wrkv"] = np.concatenate([wlay(wrkv[i]) for i in range(3)], axis=2)
    com["rw_wo"] = wlay(g("rwkv_w_o"))
    com["rw_wdn"] = wlay(np.concatenate([g("rwkv_w_down")[0], g("rwkv_w_down")[1], g("rwkv_a_down")[0], g("rwkv_a_down")[1]], axis=1))
    com["rw_gdn"] = wlay(g("rwkv_g_down"))
    com["rw_wup"] = np.ascontiguousarray(np.concatenate([g("rwkv_w_up").transpose(1, 0, 2), g("rwkv_w0")[None, :, :]], axis=0))
    com["rw_aup"] = np.ascontiguousarray(g("rwkv_a_up").transpose(1, 0, 2))
    com["rw_gup"] = g("rwkv_g_up")
    com["rw_consts"] = scan_consts(1.0)
    blk = np.zeros((128, 128), np.float32)
    blk[:64, :64] = 1.0
    blk[64:, 64:] = 1.0
    com["rw_blk"] = blk
    mix = g("rwkv_mix")
    vp.add("rw_mix", np.concatenate([fm_vec(mix[n]) for n in range(6)], axis=1))
    vp.add("rw_a0", np.concatenate([fm_vec(g("rwkv_a0")[d]) for d in range(2)], axis=1))
    vp.add("rw_kk", fm_vec(g("rwkv_k_k")))
    vp.add("rw_ka", fm_vec(g("rwkv_k_a")))
    vp.add("rw_rk", fm_vec(g("rwkv_r_k").reshape(-1)))
    vp.add("rw_lnw", fm_vec(g("rwkv_ln_w")))
    vp.add("rw_lnb", fm_vec(g("rwkv_ln_b")))


RW_DIRS = (0, 1)


class FullProg3(FullProg2):
    def declare_mixer_inputs(self):
        FullProg2.declare_mixer_inputs(self)
        if 2 in self.layers:
            self.din("rw_wrkv", [128, 8, 3072])
            self.din("rw_wo", [128, 8, 1024])
            self.din("rw_wdn", [128, 8, 256])
            self.din("rw_gdn", [128, 8, 128])
            self.din("rw_wup", [65, 2, 1024])
            self.din("rw_aup", [64, 2, 1024])
            self.din("rw_gup", [128, 1024])
            self.din("rw_consts", [128, 1536])
            self.din("rw_blk", [128, 128])

    def mm(self, out_ap, lhsT, rhs, reads, writes, start=True, stop=True, skip=False):
        if skip:
            self.S.op("pe", lambda e: e.matmul(out_ap, lhsT=lhsT, rhs=rhs, start=start, stop=stop, skip_group_check=True), reads=reads, writes=writes)
        else:
            self.S.op("pe", lambda e: e.matmul(out_ap, lhsT=lhsT, rhs=rhs, start=start, stop=stop), reads=reads, writes=writes)

    def stage_rwkv(self, layer):
        S = self.S
        dr = self.dram
        need_ctx = layer != self.last_layer
        f3 = lambda t: t[:].rearrange("p c t -> p (c t)")
        with ExitStack() as es:
            mk = lambda n, shp, dt: self.sb(es, "r_" + n, shp, dt)
            wrkv = mk("wrkv", [128, 8, 3072], BF16)
            wo = mk("wo", [128, 8, 1024], BF16)
            wdn = mk("wdn", [128, 8, 256], BF16)
            gdn = mk("gdn", [128, 8, 128], BF16)
            wup = mk("wup", [65, 2, 1024], BF16)
            aup = mk("aup", [64, 2, 1024], BF16)
            gup = mk("gup", [128, 1024], BF16)
            for kc in range(8):
                self.load_w(wrkv, wrkv[:, kc, :], dr["rw_wrkv"][:, kc, :])
            for w, n in ((wo, "rw_wo"), (wdn, "rw_wdn"), (gdn, "rw_gdn"), (wup, "rw_wup"), (aup, "rw_aup"), (gup, "rw_gup")):
                self.load_w(w, w[:], dr[n])
            sc = mk("sc", [128, 512], F32)
            blkf = mk("blkf", [128, 128], F32)
            blkb = mk("blkb", [128, 128], BF16)
            S.dma("sp", lambda e: e.dma_start(out=sc[:], in_=dr["rw_consts"][:, 0:512]), writes=[sc])
            S.dma("sp", lambda e: e.dma_start(out=blkf[:], in_=dr["rw_blk"][:, :]), writes=[blkf])
            S.op("dve", lambda e: e.tensor_copy(out=blkb[:], in_=blkf[:]), reads=[blkf], writes=[blkb])
            hTall = mk("hT", [128, 8, T], BF16)
            xts = [mk(f"x{i}", [128, 8, 128], F32) for i in range(2)]
            rstd = mk("rstd", [128, 128], F32)
            F = lambda n: mk(n, [128, 8, 128], F32)
            B = lambda n: mk(n, [128, 8, 128], BF16)
            dx = B("dx")
            hm0 = B("hm0")
            hm = [hm0, hm0, hm0]
            rT, kT, kk, ap_, bb, kd = B("rT"), B("kT"), B("kk"), B("ap"), B("bb"), B("kd")
            t1 = mk("t1", [65, 128], BF16)
            S.op("pool", lambda e: e.memset(t1[64:65, :], 1.0), writes=[t1])
            vtm = mk("vtm", [128, 1024], BF16)
            lwt = mk("lwt", [128, 1024], F32)
            e1, e2 = F("e1"), F("e2")
            ARt = mk("ARt", [128, 8, 256], BF16)
            Rt = B("Rt")
            Bt, Kt, Bet, Ket, Atb = B("Bt"), B("Kt"), B("Bet"), B("Ket"), B("Atb")
            gC = mk("gC", [128, 8, 2], F32)
            Betm, Ketm, Atm = mk("Betm", [128, 1024], BF16), mk("Ketm", [128, 1024], BF16), mk("Atm", [128, 1024], BF16)
            pads = {n: [mk(f"{n}pad{p}", [128, 128], BF16) for p in range(2)] for n in ("Be", "Ke", "V", "A", "U")}
            for n in pads:
                for p in range(2):
                    S.op("pool", lambda e, t=pads[n][p]: e.memset(t[:], 0.0), writes=[pads[n][p]])
            sq = lambda n: [mk(f"{n}{i}", [128, 128], BF16) for i in range(2)]
            sqf = lambda n: [mk(f"{n}{i}", [128, 128], F32) for i in range(2)]
            Pm, Qm, Wm = sqf("Pm"), sqf("Qm"), sqf("Wm")
            MrbT, LakT, MrkT = sq("MrbT"), sq("LakT"), sq("MrkT")
            AX = sqf("AX")
            PhiT = mk("PhiT", [128, 128], BF16)
            RhT = mk("RhT", [128, 128], BF16)
            Yh = mk("Yh", [128, 128], F32)
            Sb = mk("Sb", [128, 8, 128], BF16)
            obs = [F("ob0")] * 2
            ol = e2
            mix = lambda n, c: self.vcol("rw_mix", n * 8 + c)

            def make_hm(blk, n, dst):
                ts = blk * 128
                for c in range(8):
                    S.op("dve", lambda e, c=c: e.scalar_tensor_tensor(out=dst[:, c, :], in0=dx[:, c, :], scalar=mix(n, c), in1=hTall[:, c, ts:ts + 128], op0=ALU.mult, op1=ALU.add),
                         reads=[dx, hTall, self.vecs], writes=[dst])

            def make_dx(blk):
                ts = blk * 128
                first = blk in (0, 2)
                last = blk in (1, 17)
                lo = 1 if first else 0
                hi = 127 if last else 128
                S.op("pool", lambda e: e.tensor_tensor(out=dx[:, :, lo:hi], in0=hTall[:, :, ts + lo - 1:ts + hi - 1], in1=hTall[:, :, ts + lo + 1:ts + hi + 1], op=ALU.add), reads=[hTall], writes=[dx])
                if first:
                    S.op("pool", lambda e: e.tensor_copy(out=dx[:, :, 0:1], in_=hTall[:, :, ts + 1:ts + 2]), reads=[hTall], writes=[dx])
                if last:
                    S.op("pool", lambda e: e.tensor_copy(out=dx[:, :, 127:128], in_=hTall[:, :, ts + 126:ts + 127]), reads=[hTall], writes=[dx])
                S.op("dve", lambda e: e.scalar_tensor_tensor(out=dx[:], in0=dx[:], scalar=0.5, in1=hTall[:, :, ts:ts + 128], op0=ALU.mult, op1=ALU.subtract), reads=[dx, hTall], writes=[dx])

            def proj_fm(src, col0, dst, func=AF.Copy):
                for g in range(2):
                    p = self.ps()
                    for hh in range(4):
                        c = g * 4 + hh
                        for kc in range(8):
                            self.mm(p[:, hh * 128:(hh + 1) * 128], wrkv[:, kc, col0 + c * 128:col0 + (c + 1) * 128], src[:, kc, :], [wrkv, src], [p], start=(kc == 0), stop=(kc == 7))
                    self.evac(g, dst[:, g * 4:(g + 1) * 4, :].rearrange("p c t -> p (c t)"), p[:], [p], [dst])

            def lora_dn(src, col0, dst, func):
                p = self.ps()
                for kc in range(8):
                    self.mm(p[0:64, 0:128], wdn[:, kc, col0:col0 + 64], src[:, kc, :], [wdn, src], [p], start=(kc == 0), stop=(kc == 7))
                S.op("act", lambda e: e.activation(out=dst[0:64, :], in_=p[0:64, 0:128], func=func), reads=[p], writes=[dst])

            def a_prime(blk, d, dst):
                make_hm(blk, 4, hm[2])
                lora_dn(hm[2], 128 + d * 64, t1, AF.Copy)
                for g in range(2):
                    p = self.ps()
                    for hh in range(4):
                        c = g * 4 + hh
                        self.mm(p[:, hh * 128:(hh + 1) * 128], aup[:, d, c * 128:(c + 1) * 128], t1[0:64, :], [aup, t1], [p])
                    for hh in range(4):
                        c = g * 4 + hh
                        S.op("act", lambda e, c=c, hh=hh, p=p: e.activation(out=dst[:, c, :], in_=p[:, hh * 128:(hh + 1) * 128], func=AF.Sigmoid, bias=self.vcol("rw_a0", d * 8 + c), scale=1.0),
                             reads=[p, self.vecs], writes=[dst])

            def blocksum(src, dst_ps_tiles, fp32=True):
                for g in range(2):
                    p = dst_ps_tiles[g]
                    for hh in range(4):
                        c = g * 4 + hh
                        self.mm(p[:, hh * 128:(hh + 1) * 128], blkf[:] if fp32 else blkb[:], src[:, c, :], [blkf, blkb, src], [p])

            def common_inputs(blk):
                make_dx(blk)
                make_hm(blk, 0, hm[0])
                proj_fm(hm[0], 0, rT)
                make_hm(blk, 1, hm[1])
                proj_fm(hm[1], 1024, kT)
                for c in range(8):
                    S.op("dve" if c % 2 else "pool", lambda e, c=c: e.tensor_scalar(out=kk[:, c, :], in0=kT[:, c, :], scalar1=self.vcol("rw_kk", c), scalar2=None, op0=ALU.mult), reads=[kT, self.vecs], writes=[kk])
                S.op("pool", lambda e: e.tensor_tensor(out=e1[:], in0=kk[:], in1=kk[:], op=ALU.mult), reads=[kk], writes=[e1])
                pp = [self.ps(), self.ps()]
                blocksum(e1, pp)
                for g in range(2):
                    S.op("act", lambda e, g=g: e.activation(out=e2[:, g * 4:(g + 1) * 4, :].rearrange("p c t -> p (c t)"), in_=pp[g][:], func=AF.Sqrt), reads=[pp[g]], writes=[e2])
                S.op("dve", lambda e: e.tensor_scalar(out=e2[:], in0=e2[:], scalar1=1e-12, scalar2=None, op0=ALU.max), reads=[e2], writes=[e2])
                S.op("dve", lambda e: e.reciprocal(out=e2[:], in_=e2[:]), reads=[e2], writes=[e2])
                S.op("pool", lambda e: e.tensor_tensor(out=kk[:], in0=kk[:], in1=e2[:], op=ALU.mult), reads=[kk, e2], writes=[kk])

            def kfac(asrc, dst):
                for c in range(8):
                    S.op("dve", lambda e, c=c: e.tensor_scalar(out=dst[:, c, :], in0=asrc[:, c, :], scalar1=self.vcol("rw_ka", c), scalar2=oka[:, c:c + 1], op0=ALU.mult, op1=ALU.add),
                         reads=[asrc, self.vecs, oka], writes=[dst])
                S.op("pool", lambda e: e.tensor_tensor(out=dst[:], in0=dst[:], in1=kT[:], op=ALU.mult), reads=[dst, kT], writes=[dst])

            oka = mk("oka", [128, 8], F32)
            o_ka = self.voff["rw_ka"][0]
            S.op("dve", lambda e: e.tensor_scalar(out=oka[:], in0=self.vecs[:, o_ka:o_ka + 8], scalar1=-1.0, scalar2=1.0, op0=ALU.mult, op1=ALU.add), reads=[self.vecs], writes=[oka])

            obi = 0
            for b in range(self.nb):
                self.fill_hTall(hTall, xts, rstd, b, layer, W=128)
                OACC = self.OACC[b]
                for d in RW_DIRS:
                    S.op("pool", lambda e: e.memset(Sb[:], 0.0), writes=[Sb])
                    M_incl = sc[:, d * 128:(d + 1) * 128]
                    M_st = sc[:, 384:512] if d == 0 else sc[:, 256:384]
                    M_ts = sc[:, 256:384] if d == 0 else sc[:, 384:512]
                    for blk in SCAN_ORDER[d]:
                        ts = blk * 128
                        common_inputs(blk)
                        DB = (b == 0 and d == 0 and blk == 0)
                        if DB:
                            self.dbg("hT", hTall, hTall[:, :, 0:128], [128, 8, 128])
                            self.dbg("dx", dx, dx[:], [128, 8, 128])
                            self.dbg("rT", rT, rT[:], [128, 8, 128])
                            self.dbg("kT", kT, kT[:], [128, 8, 128])
                            self.dbg("kk", kk, kk[:], [128, 8, 128])
                        make_hm(blk, 2, hm[2])
                        for half in range(2):
                            p = self.ps()
                            for kc in range(8):
                                self.mm(p[:], hm[2][:, kc, :], wrkv[:, kc, 2048 + half * 512:2048 + (half + 1) * 512], [wrkv, hm[2]], [p], start=(kc == 0), stop=(kc == 7))
                            self.evac(half, vtm[:, half * 512:(half + 1) * 512], p[:], [p], [vtm])
                        make_hm(blk, 3, hm[2])
                        lora_dn(hm[2], d * 64, t1, AF.Tanh)
                        for half in range(2):
                            p = self.ps()
                            self.mm(p[:], t1[:], wup[:, d, half * 512:(half + 1) * 512], [t1, wup], [p])
                            S.op("act", lambda e, p=p, half=half: e.activation(out=lwt[:, half * 512:(half + 1) * 512], in_=p[:], func=AF.Sigmoid), reads=[p], writes=[lwt])
                        S.op("dve", lambda e: e.tensor_scalar(out=lwt[:], in0=lwt[:], scalar1=-0.6065306597126334, scalar2=None, op0=ALU.mult), reads=[lwt], writes=[lwt])
                        a_prime(blk, d, ap_)
                        kfac(ap_, kd)
                        S.op("pool", lambda e: e.tensor_tensor(out=bb[:], in0=kk[:], in1=ap_[:], op=ALU.mult), reads=[kk, ap_], writes=[bb])
                        if DB:
                            self.dbg("vtm", vtm, vtm[:], [128, 1024])
                            self.dbg("lwt", lwt, lwt[:], [128, 1024])
                            self.dbg("ap", ap_, ap_[:], [128, 8, 128])
                            self.dbg("kd", kd, kd[:], [128, 8, 128])
                            self.dbg("bb", bb, bb[:], [128, 8, 128])
                        pi = [self.ps(), self.ps()]
                        for c in range(8):
                            self.mm(pi[c // 4][:, (c % 4) * 128:(c % 4 + 1) * 128], lwt[:, c * 128:(c + 1) * 128], M_incl, [lwt, sc], [pi[c // 4]])
                        for g in range(2):
                            S.op("act", lambda e, g=g, pi=pi: e.activation(out=e1[:, g * 4:(g + 1) * 4, :].rearrange("p c t -> p (c t)"), in_=pi[g][:], func=AF.Exp), reads=[pi[g]], writes=[e1])
                            S.op("act", lambda e, g=g, pi=pi: e.activation(out=e2[:, g * 4:(g + 1) * 4, :].rearrange("p c t -> p (c t)"), in_=pi[g][:], func=AF.Exp, scale=-1.0), reads=[pi[g]], writes=[e2])
                        cols = (63, 127) if d == 0 else (0, 64)
                        for cc in range(2):
                            S.op("dve", lambda e, cc=cc, cols=cols: e.tensor_copy(out=gC[:, :, cc], in_=e1[:, :, cols[cc]]), reads=[e1], writes=[gC])
                        S.op("dve", lambda e: e.tensor_tensor(out=Rt[:], in0=rT[:], in1=e1[:], op=ALU.mult), reads=[rT, e1], writes=[Rt])
                        S.op("pool", lambda e: e.tensor_copy(out=ARt[:, :, 128:256], in_=Rt[:]), reads=[Rt], writes=[ARt])
                        S.op("dve", lambda e: e.tensor_tensor(out=Bt[:], in0=bb[:], in1=e2[:], op=ALU.mult), reads=[bb, e2], writes=[Bt])
                        S.op("pool", lambda e: e.tensor_tensor(out=Kt[:], in0=kd[:], in1=e2[:], op=ALU.mult), reads=[kd, e2], writes=[Kt])
                        pi = [self.ps(), self.ps()]
                        for c in range(8):
                            self.mm(pi[c // 4][:, (c % 4) * 128:(c % 4 + 1) * 128], lwt[:, c * 128:(c + 1) * 128], M_st, [lwt, sc], [pi[c // 4]])
                        for g in range(2):
                            S.op("act", lambda e, g=g, pi=pi: e.activation(out=e1[:, g * 4:(g + 1) * 4, :].rearrange("p c t -> p (c t)"), in_=pi[g][:], func=AF.Exp), reads=[pi[g]], writes=[e1])
                        S.op("dve", lambda e: e.scalar_tensor_tensor(out=Atb[:], in0=kk[:], scalar=-1.0, in1=e1[:], op0=ALU.mult, op1=ALU.mult), reads=[kk, e1], writes=[Atb])
                        S.op("pool", lambda e: e.tensor_copy(out=ARt[:, :, 0:128], in_=Atb[:]), reads=[Atb], writes=[ARt])
                        pi = [self.ps(), self.ps()]
                        for c in range(8):
                            self.mm(pi[c // 4][:, (c % 4) * 128:(c % 4 + 1) * 128], lwt[:, c * 128:(c + 1) * 128], M_ts, [lwt, sc], [pi[c // 4]])
                        for g in range(2):
                            S.op("act", lambda e, g=g, pi=pi: e.activation(out=e1[:, g * 4:(g + 1) * 4, :].rearrange("p c t -> p (c t)"), in_=pi[g][:], func=AF.Exp), reads=[pi[g]], writes=[e1])
                        S.op("dve", lambda e: e.tensor_tensor(out=Bet[:], in0=bb[:], in1=e1[:], op=ALU.mult), reads=[bb, e1], writes=[Bet])
                        S.op("pool", lambda e: e.tensor_tensor(out=Ket[:], in0=kd[:], in1=e1[:], op=ALU.mult), reads=[kd, e1], writes=[Ket])
                        for (src, dst) in ((Bet, Betm), (Ket, Ketm), (Atb, Atm)):
                            for c in range(8):
                                S.op("pe", lambda e, src=src, c=c: e.transpose(out=self.PSB[:, c * 128:(c + 1) * 128], in_=src[:, c, :], identity=self.ident_b[:]), reads=[src, self.ident_b], writes=[self.PSB])
                            S.op("act", lambda e, dst=dst: e.activation(out=dst[:], in_=self.PSB[:], func=AF.Copy), reads=[self.PSB], writes=[dst])
                        ob = obs[obi % 2]
                        obi += 1
                        for c in range(8):
                            for par in range(2):
                                hp = par * 64
                                for (nm, src) in (("Be", Betm), ("Ke", Ketm), ("V", vtm)):
                                    S.op("pool", lambda e, nm=nm, src=src, par=par, hp=hp, c=c: e.tensor_copy(out=pads[nm][par][:, hp:hp + 64], in_=src[:, c * 128 + hp:c * 128 + hp + 64]),
                                         reads=[src], writes=[pads[nm][par]])
                            for par in range(2):
                                hp = par * 64
                                i2 = par
                                p1, p2, p3 = self.ps(), self.ps(), self.ps()
                                self.mm(p1[:, 0:256], Bt[hp:hp + 64, c, :], ARt[hp:hp + 64, c, :], [Bt, ARt], [p1])
                                self.mm(p2[:, 0:256], Kt[hp:hp + 64, c, :], ARt[hp:hp + 64, c, :], [Kt, ARt], [p2])
                                self.mm(p3[:, 0:128], ARt[hp:hp + 64, c, 0:128], Bt[hp:hp + 64, c, :], [Bt, ARt], [p3])
                                P, Q, Wc = Pm[0], Qm[0], Wm[0]
                                S.op("dve", lambda e, p1=p1, P=P, M_st=M_st: e.tensor_tensor(out=P[:], in0=p1[:, 0:128], in1=M_st, op=ALU.mult), reads=[p1, sc], writes=[P])
                                S.op("dve", lambda e, p1=p1, t=MrbT[i2], M_incl=M_incl: e.tensor_tensor(out=t[:], in0=p1[:, 128:256], in1=M_incl, op=ALU.mult), reads=[p1, sc], writes=[MrbT[i2]])
                                S.op("dve", lambda e, p2=p2, t=LakT[i2], M_st=M_st: e.tensor_tensor(out=t[:], in0=p2[:, 0:128], in1=M_st, op=ALU.mult), reads=[p2, sc], writes=[LakT[i2]])
                                S.op("dve", lambda e, p2=p2, t=MrkT[i2], M_incl=M_incl: e.tensor_tensor(out=t[:], in0=p2[:, 128:256], in1=M_incl, op=ALU.mult), reads=[p2, sc], writes=[MrkT[i2]])
                                S.op("dve", lambda e, p3=p3, Q=Q, M_ts=M_ts: e.tensor_tensor(out=Q[:], in0=p3[:, 0:128], in1=M_ts, op=ALU.mult), reads=[p3, sc], writes=[Q])
                                S.op("pool", lambda e, Wc=Wc, P=P: e.tensor_tensor(out=Wc[:], in0=self.ident_f[:], in1=P[:], op=ALU.add), reads=[self.ident_f, P], writes=[Wc])
                                for lv in range(1, 6):
                                    Pn, Qn, Wn = Pm[lv % 2], Qm[lv % 2], Wm[lv % 2]
                                    pq = self.ps()
                                    self.mm(pq[:, 0:128], P[:], Q[:], [P, Q], [pq])
                                    if lv < 5:
                                        pp_ = self.ps()
                                        self.mm(pp_[:, 0:128], Q[:], P[:], [P, Q], [pp_])
                                        S.op("act", lambda e, pp_=pp_, Pn=Pn: e.activation(out=Pn[:], in_=pp_[:, 0:128], func=AF.Copy), reads=[pp_], writes=[Pn])
                                    S.op("dve", lambda e, pq=pq, Qn=Qn: e.tensor_copy(out=Qn[:], in_=pq[:, 0:128]), reads=[pq], writes=[Qn])
                                    pw = self.ps()
                                    self.mm(pw[:, 0:128], Qn[:], Wc[:], [Qn, Wc], [pw])
                                    S.op("dve", lambda e, pw=pw, Wn=Wn, Wc=Wc: e.tensor_tensor(out=Wn[:], in0=pw[:, 0:128], in1=Wc[:], op=ALU.add), reads=[pw, Wc], writes=[Wn])
                                    P, Q, Wc = Pn, Qn, Wn
                                ax = AX[par]
                                px = self.ps()
                                self.mm(px[:, 0:64], LakT[i2][:], pads["V"][par][:, hp:hp + 64], [LakT[i2], pads["V"][par]], [px])
                                S.op("act", lambda e, px=px, ax=ax: e.activation(out=ax[:, 64:128], in_=px[:, 0:64], func=AF.Copy), reads=[px], writes=[ax])
                                S.op("pool", lambda e, ax=ax, c=c, hp=hp: e.tensor_copy(out=ax[:, 0:64], in_=Atm[:, c * 128 + hp:c * 128 + hp + 64]), reads=[Atm], writes=[ax])
                                pau = self.ps()
                                self.mm(pau[:, 0:128], Wc[:], ax[:], [Wc, ax], [pau])
                                S.op("act", lambda e, pau=pau, par=par, c=c, hp=hp: e.activation(out=pads["A"][par][:, hp:hp + 64], in_=pau[:, 0:64], func=AF.Copy), reads=[pau], writes=[pads["A"][par]])
                                S.op("dve", lambda e, pau=pau, par=par, c=c, hp=hp: e.tensor_copy(out=pads["U"][par][:, hp:hp + 64], in_=pau[:, 64:128]), reads=[pau], writes=[pads["U"][par]])
                            pR = self.ps()
                            pYh = self.ps()
                            for par in range(2):
                                self.mm(pR[:, 0:128], pads["A"][par][:, :], MrbT[par][:], [pads["A"][par], MrbT[par]], [pR], start=(par == 0), stop=(par == 1))
                            for par in range(2):
                                self.mm(pYh[:, 0:128], pads["U"][par][:, :], MrbT[par][:], [pads["U"][par], MrbT[par]], [pYh], start=(par == 0), stop=False)
                                self.mm(pYh[:, 0:128], pads["V"][par][:, :], MrkT[par][:], [pads["V"][par], MrkT[par]], [pYh], start=False, stop=(par == 1))
                            S.op("dve", lambda e, pR=pR, c=c: e.tensor_tensor(out=RhT[:], in0=pR[:, 0:128], in1=Rt[:, c, :], op=ALU.add), reads=[pR, Rt], writes=[RhT])
                            S.op("act", lambda e, pYh=pYh: e.activation(out=Yh[:], in_=pYh[:, 0:128], func=AF.Copy), reads=[pYh], writes=[Yh])
                            pY = self.ps()
                            order = (0, 1) if d == 0 else (1, 0)
                            for cc in order:
                                lo, hi = cc * 64, (cc + 1) * 64
                                self.mm(pY[:, lo:hi], Sb[:, c, :], RhT[:, lo:hi], [Sb, RhT], [pY])
                                pPhi = self.ps()
                                for par in range(2):
                                    self.mm(pPhi[:, 0:128], pads["A"][par][lo:hi, :], pads["Be"][par][lo:hi, :], [pads["A"][par], pads["Be"][par]], [pPhi], start=(par == 0), stop=(par == 1))
                                S.op("dve", lambda e, pPhi=pPhi, c=c, cc=cc: e.scalar_tensor_tensor(out=PhiT[:], in0=self.ident_f[:], scalar=gC[:, c, cc:cc + 1], in1=pPhi[:, 0:128], op0=ALU.mult, op1=ALU.add),
                                     reads=[pPhi, gC, self.ident_f], writes=[PhiT])
                                pS = self.ps()
                                for par in range(2):
                                    self.mm(pS[:, 0:128], pads["Be"][par][lo:hi, :], pads["U"][par][lo:hi, :], [pads["Be"][par], pads["U"][par]], [pS], start=(par == 0), stop=False)
                                    self.mm(pS[:, 0:128], pads["Ke"][par][lo:hi, :], pads["V"][par][lo:hi, :], [pads["Ke"][par], pads["V"][par]], [pS], start=False, stop=False)
                                self.mm(pS[:, 0:128], PhiT[:], Sb[:, c, :], [PhiT, Sb], [pS], start=False, stop=True)
                                S.op("act", lambda e, pS=pS, c=c: e.activation(out=Sb[:, c, :], in_=pS[:, 0:128], func=AF.Copy), reads=[pS], writes=[Sb])
                            S.op("dve", lambda e, pY=pY, c=c, ob=ob: e.tensor_tensor(out=ob[:, c, :], in0=pY[:, 0:128], in1=Yh[:], op=ALU.add), reads=[pY, Yh], writes=[ob])
                        if DB:
                            self.dbg("ob", ob, ob[:], [128, 8, 128])
                            self.dbg("Rt", Rt, Rt[:], [128, 8, 128])
                            self.dbg("Bt", Bt, Bt[:], [128, 8, 128])
                            self.dbg("Kt", Kt, Kt[:], [128, 8, 128])
                            self.dbg("Atb", Atb, Atb[:], [128, 8, 128])
                            self.dbg("Bet", Bet, Bet[:], [128, 8, 128])
                            self.dbg("Sb", Sb, Sb[:], [128, 8, 128])
                        if d == 1 and len(RW_DIRS) == 2:
                            S.dma("sp", lambda e, ts=ts, OACC=OACC: e.dma_start(out=ol[:], in_=OACC.t[:, :, ts:ts + 128].rearrange("c p t -> p c t")), reads=[OACC], writes=[ol])
                            S.op("pool", lambda e, ob=ob: e.tensor_tensor(out=ob[:], in0=ob[:], in1=ol[:], op=ALU.add), reads=[ob, ol], writes=[ob])
                        S.dma("sp", lambda e, ts=ts, ob=ob, OACC=OACC: e.dma_start(out=OACC.t[:, :, ts:ts + 128].rearrange("c p t -> p c t"), in_=ob[:]), reads=[ob], writes=[OACC])
                for blk in range(18):
                    lat = blk >= 2
                    if not lat and not need_ctx:
                        continue
                    col = b if lat else 4
                    ts = blk * 128
                    xt = xts[blk % 2]
                    self.load_x(xt, b, ts, 128)
                    common_inputs(blk)
                    a_prime(blk, 0, ap_)
                    a_prime(blk, 1, bb)
                    S.op("dve", lambda e: e.tensor_tensor(out=ap_[:], in0=ap_[:], in1=bb[:], op=ALU.add), reads=[ap_, bb], writes=[ap_])
                    S.op("dve", lambda e: e.tensor_scalar(out=ap_[:], in0=ap_[:], scalar1=0.5, scalar2=None, op0=ALU.mult), reads=[ap_], writes=[ap_])
                    kfac(ap_, kd)
                    S.op("pool", lambda e: e.tensor_tensor(out=kd[:], in0=kd[:], in1=rT[:], op=ALU.mult), reads=[kd, rT], writes=[kd])
                    for c in range(8):
                        S.op("dve", lambda e, c=c: e.tensor_scalar(out=kd[:, c, :], in0=kd[:, c, :], scalar1=self.vcol("rw_rk", c), scalar2=None, op0=ALU.mult), reads=[kd, self.vecs], writes=[kd])
                    pp = [self.ps(), self.ps()]
                    blocksum(kd, pp, fp32=False)
                    make_hm(blk, 2, hm[2])
                    proj_fm(hm[2], 2048, Rt)
                    for g in range(2):
                        S.op("dve", lambda e, g=g, pp=pp: e.tensor_tensor(out=bb[:, g * 4:(g + 1) * 4, :].rearrange("p c t -> p (c t)"), in0=pp[g][:], in1=Rt[:, g * 4:(g + 1) * 4, :].rearrange("p c t -> p (c t)"), op=ALU.mult),
                             reads=[pp[g], Rt], writes=[bb])
                    S.dma("sp", lambda e, ts=ts, OACC=OACC: e.dma_start(out=ol[:], in_=OACC.t[:, :, ts:ts + 128].rearrange("c p t -> p c t")), reads=[OACC], writes=[ol])
                    pm = [self.ps(), self.ps()]
                    blocksum(ol, pm)
                    for g in range(2):
                        S.op("dve", lambda e, g=g, pm=pm: e.scalar_tensor_tensor(out=e1[:, g * 4:(g + 1) * 4, :].rearrange("p c t -> p (c t)"), in0=pm[g][:], scalar=-1.0 / 64.0, in1=ol[:, g * 4:(g + 1) * 4, :].rearrange("p c t -> p (c t)"),
                                                                         op0=ALU.mult, op1=ALU.add), reads=[pm[g], ol], writes=[e1])
                    S.op("pool", lambda e: e.tensor_tensor(out=e2[:], in0=e1[:], in1=e1[:], op=ALU.mult), reads=[e1], writes=[e2])
                    pv = [self.ps(), self.ps()]
                    blocksum(e2, pv)
                    for g in range(2):
                        S.op("act", lambda e, g=g, pv=pv: e.activation(out=e2[:, g * 4:(g + 1) * 4, :].rearrange("p c t -> p (c t)"), in_=pv[g][:], func=AF.Sqrt, bias=self.epsb2[:, 0:1], scale=1.0 / 64.0), reads=[pv[g], self.epsb2], writes=[e2])
                    S.op("dve", lambda e: e.reciprocal(out=e2[:], in_=e2[:]), reads=[e2], writes=[e2])
                    S.op("pool", lambda e: e.tensor_tensor(out=e1[:], in0=e1[:], in1=e2[:], op=ALU.mult), reads=[e1, e2], writes=[e1])
                    for c in range(8):
                        S.op("dve", lambda e, c=c: e.tensor_scalar(out=e1[:, c, :], in0=e1[:, c, :], scalar1=self.vcol("rw_lnw", c), scalar2=self.vcol("rw_lnb", c), op0=ALU.mult, op1=ALU.add), reads=[e1, self.vecs], writes=[e1])
                    S.op("pool", lambda e: e.tensor_tensor(out=e1[:], in0=e1[:], in1=bb[:], op=ALU.add), reads=[e1, bb], writes=[e1])
                    make_hm(blk, 5, hm[2])
                    pg = self.ps()
                    for kc in range(8):
                        self.mm(pg[:, 0:128], gdn[:, kc, :], hm[2][:, kc, :], [gdn, hm[2]], [pg], start=(kc == 0), stop=(kc == 7))
                    S.op("act", lambda e, pg=pg: e.activation(out=PhiT[:], in_=pg[:, 0:128], func=AF.Sigmoid), reads=[pg], writes=[PhiT])
                    for g in range(2):
                        p = self.ps()
                        for hh in range(4):
                            c = g * 4 + hh
                            self.mm(p[:, hh * 128:(hh + 1) * 128], gup[:, c * 128:(c + 1) * 128], PhiT[:], [gup, PhiT], [p])
                        S.op("dve", lambda e, g=g, p=p: e.tensor_tensor(out=Bt[:, g * 4:(g + 1) * 4, :].rearrange("p c t -> p (c t)"), in0=p[:], in1=e1[:, g * 4:(g + 1) * 4, :].rearrange("p c t -> p (c t)"), op=ALU.mult),
                             reads=[p, e1], writes=[Bt])
                    for g in range(2):
                        p = self.ps()
                        for hh in range(4):
                            c = g * 4 + hh
                            for kc in range(8):
                                self.mm(p[:, hh * 128:(hh + 1) * 128], wo[:, kc, c * 128:(c + 1) * 128], Bt[:, kc, :], [wo, Bt], [p], start=(kc == 0), stop=(kc == 7))
                        self.evac(g, e2[:, g * 4:(g + 1) * 4, :].rearrange("p c t -> p (c t)"), p[:], [p], [e2])
                    self.rstd_of(e2, 128, rstd)
                    self.resid(xt, 128, e2, rstd, layer, 2, col)
                    self.store_x(xt, b, ts, 128)
            S.barrier()
            S.flush()


def host_mixers(inp, vp, com):
    host_attn(inp, vp, com)
    host_gla(inp, vp, com)
    host_hgrn(inp, vp, com)
    host_rwkv(inp, vp, com)


def kernel(**inputs):
    out, _, _ = run(inputs, 4, NCORES, prog_cls=FullProg3)
    return out
```

```python
import numpy as np
from contextlib import ExitStack
import concourse.bass as bass
import concourse.mybir as mybir
from concourse.bass_utils import run_bass_kernel_spmd

F32 = mybir.dt.float32
BF16 = mybir.dt.bfloat16
AF = mybir.ActivationFunctionType
ALU = mybir.AluOpType
AX = mybir.AxisListType

D = 1024
TC = 256
TL = 2048
T = TC + TL
HID = 4096
NCORES = 8
EPS = 1e-6

EPOCH = 30000
COMPUTE = ("pe", "act", "dve", "pool")
DEBUG = False
import os
ATT_STOP = int(os.environ.get("ATT_STOP", "9"))
ATT_SUB = os.environ.get("ATT_SUB", "d")


class Res:
    __slots__ = ("name", "last_w", "readers")

    def __init__(self, name=""):
        self.name = name
        self.last_w = None
        self.readers = {}


class Tile:
    def __init__(self, ap, name=""):
        self.t = ap
        self.res = Res(name)

    def __getitem__(self, idx):
        return self.t[idx]


class Sched:
    def __init__(self, nc, es):
        self.nc = nc
        self.engs = {"pe": nc.tensor, "act": nc.scalar, "dve": nc.vector, "pool": nc.gpsimd, "sp": nc.sync}
        self.count = {e: 0 for e in COMPUTE}
        self.esems = {e: [] for e in COMPUTE}
        self.es = es
        self.known = {e: {} for e in self.engs}
        self.lists = {e: [] for e in self.engs}
        self.ndma = 8
        self.dsems, self.dcnt, self.dlast = {}, {}, {}
        self.drot = {"sp": 0, "pool": 0, "act": 0}
        for q in ("sp", "pool", "act"):
            for i in range(self.ndma):
                self.dsems[(q, i)] = es.enter_context(nc.semaphore(f"d_{q}_{i}"))
                self.dcnt[(q, i)] = 0
                self.dlast[(q, i)] = None
        self.n_inst = 0
        self._psi = 0

    def _esem(self, e, epoch):
        lst = self.esems[e]
        while len(lst) <= epoch:
            lst.append(self.es.enter_context(self.nc.semaphore(f"e_{e}_{len(lst)}")))
        return lst[epoch]

    def _wait(self, e, toks):
        kn = self.known[e]
        best = {}
        for tk in toks:
            if tk is None:
                continue
            if tk[0] == "eng":
                _, e2, idx = tk
                if e == "pe" and e2 == "pe":
                    continue
                k = ("eng", e2)
                v = idx
            else:
                k = ("dma", tk[1])
                v = tk[2]
            if kn.get(k, 0) >= v:
                continue
            if best.get(k, 0) < v:
                best[k] = v
        for k, v in best.items():
            kn[k] = v
            if k[0] == "eng":
                ep = (v - 1) // EPOCH
                self.lists[e].append(("wait", self._esem(k[1], ep), v - ep * EPOCH))
            else:
                self.lists[e].append(("wait", self.dsems[k[1]], v))

    @staticmethod
    def _r(x):
        return x.res if isinstance(x, Tile) else x

    def _deps(self, reads, writes):
        deps = []
        for r in reads:
            r = self._r(r)
            if r.last_w is not None:
                deps.append(r.last_w)
        for w in writes:
            w = self._r(w)
            if w.last_w is not None:
                deps.append(w.last_w)
            deps.extend(w.readers.values())
        return deps

    def _mark(self, tok, reads, writes):
        for r in reads:
            self._r(r).readers[(tok[0], tok[1])] = tok
        for w in writes:
            w = self._r(w)
            w.last_w = tok
            w.readers = {}

    def op(self, e, fn, reads=(), writes=()):
        self._wait(e, self._deps(reads, writes))
        self.count[e] += 1
        idx = self.count[e]
        ep = (idx - 1) // EPOCH
        self.lists[e].append(("op", fn, self._esem(e, ep), 1))
        tok = ("eng", e, idx)
        self._mark(tok, reads, writes)
        self.n_inst += 1
        return tok

    def dma(self, q, fn, reads=(), writes=()):
        i = self.drot[q]
        self.drot[q] = (i + 1) % self.ndma
        key = (q, i)
        deps = self._deps(reads, writes)
        if self.dlast[key] is not None:
            deps.append(self.dlast[key])
        self._wait(q, deps)
        self.dcnt[key] += 16
        tok = ("dma", key, self.dcnt[key])
        self.dlast[key] = tok
        self.lists[q].append(("op", fn, self.dsems[key], 16))
        self._mark(tok, reads, writes)
        self.n_inst += 1
        return tok

    def barrier(self):
        toks = [("eng", e, self.count[e]) for e in COMPUTE if self.count[e] > 0]
        toks += [tk for tk in self.dlast.values() if tk is not None]
        for e in self.engs:
            self._wait(e, toks)

    def flush(self):
        lists = self.lists
        self.lists = {e: [] for e in self.engs}

        def replay(eng, items):
            for it in items:
                if it[0] == "wait":
                    eng.wait_ge(it[1], it[2])
                else:
                    it[1](eng).then_inc(it[2], it[3])

        with self.nc.Block() as block:
            @block.tensor
            def _(eng):
                replay(eng, lists["pe"])

            @block.scalar
            def _(eng):
                replay(eng, lists["act"])

            @block.vector
            def _(eng):
                replay(eng, lists["dve"])

            @block.gpsimd
            def _(eng):
                replay(eng, lists["pool"])

            @block.sync
            def _(eng):
                replay(eng, lists["sp"])


def fm_vec(v):
    v = np.asarray(v, np.float32).reshape(-1, 128)
    return np.ascontiguousarray(v.T)


def wlay(w):
    K, N = w.shape
    return np.ascontiguousarray(w.reshape(K // 128, 128, N).transpose(1, 0, 2))


class VecPack:
    def __init__(self):
        self.cols = []
        self.off = {}
        self.n = 0

    def add(self, name, arr2d):
        arr2d = np.asarray(arr2d, np.float32)
        assert arr2d.shape[0] == 128
        self.off[name] = (self.n, arr2d.shape[1])
        self.cols.append(arr2d)
        self.n += arr2d.shape[1]

    def pack(self):
        return np.ascontiguousarray(np.concatenate(self.cols, axis=1))


class Prog:
    def __init__(self, nb, layers, voff, nv, last_layer=3, do_mlp=True):
        self.nb = nb
        self.layers = layers
        self.voff = voff
        self.nv = nv
        self.last_layer = last_layer
        self.do_mlp = do_mlp
        self.nc = bass.Bass("TRN2", target_bir_lowering=False)
        self.dram = {}
        self.dbg_names = set()
        self.debug_on = DEBUG

    def dbg(self, name, tile, ap, shape):
        if not getattr(self, "debug_on", False) or name in self.dbg_names:
            return
        self.dbg_names.add(name)
        d = self.nc.dram_tensor("dbg_" + name, list(shape), F32, kind="ExternalOutput").ap()
        self.S.dma("pool", lambda e: e.dma_start(out=d, in_=ap), reads=[tile])

    def din(self, name, shape, dt=F32):
        self.dram[name] = self.nc.dram_tensor(name, list(shape), dt, kind="ExternalInput").ap()
        return self.dram[name]

    def dscratch(self, name, shape, dt=F32):
        return self.nc.dram_tensor(name, list(shape), dt, kind="Internal").ap()

    def sb(self, es, name, shape, dt):
        self._nm = getattr(self, "_nm", 0) + 1
        return Tile(es.enter_context(self.nc.sbuf_tensor(f"{name}_{self._nm}", list(shape), dt)), name)

    def ps(self):
        S = self.S
        t = self.PS[S._psi % len(self.PS)]
        S._psi += 1
        return t

    def ps_ex(self, excl):
        while True:
            t = self.ps()
            if all(t is not x for x in excl):
                return t

    def vcol(self, name, c=0, n=1):
        o, w = self.voff[name]
        return self.vecs[:, o + c:o + c + n]

    def rstd_of(self, src, W, rstd, nch=8, scale=1.0 / D, eps=EPS, ones=None, sq_dt=BF16, lo=0, c0=0):
        S = self.S
        ones = ones or self.ones_b
        p = self.ps()
        for c in range(nch):
            sq = self.sqb[c % 2]
            eng = "act" if c % 2 == 0 else "pool"
            if eng == "act":
                S.op("act", lambda e, sq=sq, c=c: e.activation(out=sq[:, :W], in_=src[:, c0 + c, lo:lo + W], func=AF.Square), reads=[src], writes=[sq])
            else:
                S.op("pool", lambda e, sq=sq, c=c: e.tensor_tensor(out=sq[:, :W], in0=src[:, c0 + c, lo:lo + W], in1=src[:, c0 + c, lo:lo + W], op=ALU.mult), reads=[src], writes=[sq])
            S.op("pe", lambda e, sq=sq, c=c, p=p: e.matmul(p[:, :W], lhsT=ones[:, :], rhs=sq[:, :W], start=(c == 0), stop=(c == nch - 1)), reads=[sq, ones], writes=[p])
        S.op("act", lambda e, p=p: e.activation(out=rstd[:, :W], in_=p[:, :W], func=AF.Sqrt, bias=self.epsb[:, 0:1] if eps == EPS else self.epsb2[:, 0:1], scale=scale),
             reads=[p, self.epsb], writes=[rstd])
        S.op("dve", lambda e: e.reciprocal(out=rstd[:, :W], in_=rstd[:, :W]), reads=[rstd], writes=[rstd])

    def modulate(self, xt, W, rstd, layer, kind_g, kind_s, col, hT, lo=0, hlo=0):
        S = self.S
        for c in range(8):
            tmp = self.tmpf[c % 2]
            g = self.modv[:, layer, kind_g, c, col:col + 1]
            sh = self.modv[:, layer, kind_s, c, col:col + 1]
            S.op("dve", lambda e, tmp=tmp, c=c, g=g: e.scalar_tensor_tensor(out=tmp[:, :W], in0=xt[:, c, lo:lo + W], scalar=g, in1=rstd[:, :W], op0=ALU.mult, op1=ALU.mult),
                 reads=[xt, rstd, self.modv], writes=[tmp])
            if c % 2 == 0:
                S.op("act", lambda e, tmp=tmp, c=c, sh=sh: e.activation(out=hT[:, c, hlo:hlo + W], in_=tmp[:, :W], func=AF.Identity, bias=sh, scale=1.0),
                     reads=[tmp, self.modv], writes=[hT])
            else:
                S.op("pool", lambda e, tmp=tmp, c=c, sh=sh: e.tensor_scalar(out=hT[:, c, hlo:hlo + W], in0=tmp[:, :W], scalar1=sh, scalar2=None, op0=ALU.add),
                     reads=[tmp, self.modv], writes=[hT])

    def resid(self, xt, W, y, rstd, layer, kind_g, col):
        S = self.S
        for c in range(8):
            tmp = self.tmpf[c % 2]
            g = self.modv[:, layer, kind_g, c, col:col + 1]
            S.op("dve", lambda e, tmp=tmp, c=c, g=g: e.scalar_tensor_tensor(out=tmp[:, :W], in0=y[:, c, :W], scalar=g, in1=rstd[:, :W], op0=ALU.mult, op1=ALU.mult),
                 reads=[y, rstd, self.modv], writes=[tmp])
            S.op("pool", lambda e, tmp=tmp, c=c: e.tensor_tensor(out=xt[:, c, :W], in0=xt[:, c, :W], in1=tmp[:, :W], op=ALU.add), reads=[tmp, xt], writes=[xt])

    def xt_dram(self, b, t0, W):
        g, o = divmod(t0, 256)
        assert o + W <= 256
        if W == 256:
            return self.XT.t[b, g]
        return self.XT.t[b, g][:, :, o:o + W]

    def load_x(self, xt, b, t0, W):
        XT = self.XT
        src = self.xt_dram(b, t0, W)
        self.S.dma("sp", lambda e: e.dma_start(out=xt[:, :, :W], in_=src), reads=[XT], writes=[xt])

    def store_x(self, xt, b, t0, W):
        XT = self.XT
        dst = self.xt_dram(b, t0, W)
        self.S.dma("sp", lambda e: e.dma_start(out=dst, in_=xt[:, :, :W]), reads=[xt], writes=[XT])

    def load_w(self, dst, dst_ap, src_ap, cap=4096):
        shp = tuple(dst_ap.shape)
        assert tuple(src_ap.shape) == shp, (shp, tuple(src_ap.shape))
        pieces = []
        if len(shp) == 2:
            n = shp[1]
            for c0 in range(0, n, cap):
                c1 = min(n, c0 + cap)
                pieces.append((dst_ap[:, c0:c1], src_ap[:, c0:c1]))
        else:
            assert len(shp) == 3
            k, n = shp[1], shp[2]
            if n >= cap:
                for j in range(k):
                    for c0 in range(0, n, cap):
                        c1 = min(n, c0 + cap)
                        pieces.append((dst_ap[:, j, c0:c1], src_ap[:, j, c0:c1]))
            else:
                kk = max(1, cap // n)
                for j0 in range(0, k, kk):
                    j1 = min(k, j0 + kk)
                    pieces.append((dst_ap[:, j0:j1, :], src_ap[:, j0:j1, :]))
        for (da, sa) in pieces:
            self.S.dma("pool", lambda e, da=da, sa=sa: e.dma_start(out=da, in_=sa), writes=[dst])

    def evac(self, i, out_ap, in_ap, reads, writes):
        if i % 2 == 0:
            self.S.op("act", lambda e: e.activation(out=out_ap, in_=in_ap, func=AF.Copy), reads=reads, writes=writes)
        else:
            self.S.op("dve", lambda e: e.tensor_copy(out=out_ap, in_=in_ap), reads=reads, writes=writes)

    def build(self):
        nc = self.nc
        nb = self.nb
        x_d = self.din("x", [nb, TL, D])
        ctx_d = self.din("ctx", [nb, TC, D])
        cT_d = self.din("cT", [128, 8, 5])
        vecs_d = self.din("vecs", [128, self.nv])
        wmod_d = self.din("w_mod", [4, 128, 8, 6 * D])
        self.w1_d = self.din("w_mlp_in", [4, 128, 8, HID])
        self.w2_d = self.din("w_mlp_out", [4, 128, 32, D])
        out_d = self.nc.dram_tensor("out", [nb, TL, D], F32, kind="ExternalOutput").ap()
        self.XT = Tile(self.dscratch("XT", [nb, T // 256, 128, 8, 256]), "XT")
        self.declare_mixer_inputs()

        with ExitStack() as es:
            S = self.S = Sched(nc, es)
            self.PS = [Tile(es.enter_context(nc.psum_tensor(f"ps{i}", [128, 512], F32)), f"ps{i}") for i in range(7)]
            self.PSB = Tile(es.enter_context(nc.psum_tensor("psb", [128, 1024], BF16)), "psb")
            self.vecs = self.sb(es, "vecs", [128, self.nv], F32)
            S.dma("sp", lambda e: e.dma_start(out=self.vecs[:], in_=vecs_d[:, :]), writes=[self.vecs])
            self.ident_f = self.sb(es, "identf", [128, 128], F32)
            self.ident_b = self.sb(es, "identb", [128, 128], BF16)
            self.ones_b = self.sb(es, "onesb", [128, 128], BF16)
            self.ones_f = self.sb(es, "onesf", [128, 128], F32)
            self.epsb = self.sb(es, "epsb", [128, 1], F32)
            self.epsb2 = self.sb(es, "epsb2", [128, 1], F32)
            S.op("pool", lambda e: e.memset(self.ident_f[:], 0.0), writes=[self.ident_f])
            S.op("pool", lambda e: e.affine_select(out=self.ident_f[:], in_=self.ident_f[:], pattern=[[-1, 128]], compare_op=ALU.not_equal, fill=1.0, base=0, channel_multiplier=1),
                 reads=[self.ident_f], writes=[self.ident_f])
            S.op("dve", lambda e: e.tensor_copy(out=self.ident_b[:], in_=self.ident_f[:]), reads=[self.ident_f], writes=[self.ident_b])
            S.op("pool", lambda e: e.memset(self.ones_b[:], 1.0), writes=[self.ones_b])
            S.op("pool", lambda e: e.memset(self.ones_f[:], 1.0), writes=[self.ones_f])
            S.op("pool", lambda e: e.memset(self.epsb[:], EPS), writes=[self.epsb])
            S.op("pool", lambda e: e.memset(self.epsb2[:], 64e-5), writes=[self.epsb2])
            self.sqb = [self.sb(es, f"sqb{i}", [128, 512], BF16) for i in range(2)]
            self.tmpf = [self.sb(es, f"tmpf{i}", [128, 512], F32) for i in range(2)]
            self.modv = self.sb(es, "modv", [128, 4, 6, 8, 5], F32)

            self.stage_prologue(cT_d, wmod_d)
            self.stage_in(x_d, ctx_d)
            for layer in self.layers:
                self.stage_mixer(layer)
                if self.do_mlp:
                    self.stage_mlp(layer)
            self.stage_out(out_d)
            S.barrier()
            S.flush()
        return nc

    def declare_mixer_inputs(self):
        pass

    def stage_mixer(self, layer):
        pass

    def stage_prologue(self, cT_d, wmod_d):
        S = self.S
        with ExitStack() as es:
            cT = self.sb(es, "cT", [128, 8, 5], F32)
            csil = self.sb(es, "csil", [128, 8, 5], BF16)
            mods = self.sb(es, "mods", [128, 48, 5], F32)
            wm = [self.sb(es, f"wm{i}", [128, 8, 1536], BF16) for i in range(2)]
            S.dma("sp", lambda e: e.dma_start(out=cT[:], in_=cT_d[:, :, :]), writes=[cT])
            S.op("act", lambda e: e.activation(out=csil[:], in_=cT[:], func=AF.Silu), reads=[cT], writes=[csil])
            k = 0
            for layer in range(4):
                p = self.ps()
                for piece in range(4):
                    w = wm[k % 2]
                    k += 1
                    for kc in range(8):
                        self.load_w(w, w[:, kc, :], wmod_d[layer, :, kc, piece * 1536:(piece + 1) * 1536])
                    for jj in range(12):
                        j = piece * 12 + jj
                        for kc in range(8):
                            S.op("pe", lambda e, w=w, kc=kc, jj=jj, j=j, p=p: e.matmul(p[:, j * 8:j * 8 + 5], lhsT=w[:, kc, jj * 128:(jj + 1) * 128], rhs=csil[:, kc, :],
                                                                                   start=(kc == 0), stop=(kc == 7)), reads=[w, csil], writes=[p])
                bo = self.voff["b_mod"][0] + layer * 48
                for col in range(5):
                    S.op("dve", lambda e, p=p, col=col, bo=bo: e.tensor_tensor(out=mods[:, :, col], in0=p[:, 0:384].rearrange("p (j e) -> p j e", e=8)[:, :, col],
                                                                            in1=self.vecs[:, bo:bo + 48], op=ALU.add), reads=[p, self.vecs], writes=[mods])
                gofs = {n: self.voff[n][0] + layer * 8 for n in ("g_pre_mix", "g_post_mix", "g_pre_mlp", "g_post_mlp")}
                for col in range(5):
                    def g(n):
                        return self.vecs[:, gofs[n]:gofs[n] + 8]
                    mv = self.modv
                    S.op("dve", lambda e, col=col, layer=layer, gg=g("g_pre_mix"): e.scalar_tensor_tensor(out=mv[:, layer, 0, :, col], in0=mods[:, 8:16, col], scalar=1.0, in1=gg, op0=ALU.add, op1=ALU.mult),
                         reads=[mods, self.vecs], writes=[mv])
                    S.op("dve", lambda e, col=col, layer=layer: e.tensor_copy(out=mv[:, layer, 1, :, col], in_=mods[:, 0:8, col]), reads=[mods], writes=[mv])
                    S.op("dve", lambda e, col=col, layer=layer, gg=g("g_post_mix"): e.tensor_tensor(out=mv[:, layer, 2, :, col], in0=mods[:, 16:24, col], in1=gg, op=ALU.mult),
                         reads=[mods, self.vecs], writes=[mv])
                    S.op("dve", lambda e, col=col, layer=layer, gg=g("g_pre_mlp"): e.scalar_tensor_tensor(out=mv[:, layer, 3, :, col], in0=mods[:, 32:40, col], scalar=1.0, in1=gg, op0=ALU.add, op1=ALU.mult),
                         reads=[mods, self.vecs], writes=[mv])
                    S.op("dve", lambda e, col=col, layer=layer: e.tensor_copy(out=mv[:, layer, 4, :, col], in_=mods[:, 24:32, col]), reads=[mods], writes=[mv])
                    S.op("dve", lambda e, col=col, layer=layer, gg=g("g_post_mlp"): e.tensor_tensor(out=mv[:, layer, 5, :, col], in0=mods[:, 40:48, col], in1=gg, op=ALU.mult),
                         reads=[mods, self.vecs], writes=[mv])
            S.barrier()
            S.flush()

    def stage_in(self, x_d, ctx_d):
        S = self.S
        with ExitStack() as es:
            xin = [self.sb(es, f"xin{i}", [128, D], F32) for i in range(2)]
            xts = [self.sb(es, f"xts{i}", [128, 8, 256], F32) for i in range(2)]
            k = 0
            for b in range(self.nb):
                for tt in range(T // 128):
                    xi, xo = xin[k % 2], xts[(tt // 2) % 2]
                    k += 1
                    jo = (tt % 2) * 128
                    src = ctx_d[b, tt * 128:(tt + 1) * 128, :] if tt < 2 else x_d[b, (tt - 2) * 128:(tt - 1) * 128, :]
                    S.dma("sp", lambda e, xi=xi, src=src: e.dma_start(out=xi[:], in_=src), writes=[xi])
                    for half in range(2):
                        p = self.ps()
                        for j in range(4):
                            c = half * 4 + j
                            S.op("pe", lambda e, p=p, xi=xi, c=c, j=j: e.transpose(out=p[:, j * 128:(j + 1) * 128], in_=xi[:, c * 128:(c + 1) * 128], identity=self.ident_f[:]),
                                 reads=[xi, self.ident_f], writes=[p])
                        self.evac(half, xo[:, half * 4:(half + 1) * 4, jo:jo + 128], p[:].rearrange("p (j t) -> p j t", j=4), [p], [xo])
                    XT = self.XT
                    if tt % 2 == 1:
                        S.dma("sp", lambda e, xo=xo, dst=self.xt_dram(b, (tt - 1) * 128, 256): e.dma_start(out=dst, in_=xo[:]), reads=[xo], writes=[XT])
            S.barrier()
            S.flush()

    def stage_out(self, out_d):
        S = self.S
        with ExitStack() as es:
            xin = [self.sb(es, f"xout{i}", [128, D], F32) for i in range(2)]
            xts = [self.sb(es, f"xtso{i}", [128, 8, 256], F32) for i in range(2)]
            k = 0
            for b in range(self.nb):
                for tt in range(2, T // 128):
                    xi, xo = xin[k % 2], xts[(tt // 2) % 2]
                    k += 1
                    jo = (tt % 2) * 128
                    if tt % 2 == 0:
                        self.load_x(xo, b, tt * 128, 256)
                    for half in range(2):
                        p = self.ps()
                        for j in range(4):
                            c = half * 4 + j
                            S.op("pe", lambda e, p=p, xo=xo, c=c, j=j, jo=jo: e.transpose(out=p[:, j * 128:(j + 1) * 128], in_=xo[:, c, jo:jo + 128], identity=self.ident_f[:]),
                                 reads=[xo, self.ident_f], writes=[p])
                        self.evac(half, xi[:, half * 512:(half + 1) * 512], p[:], [p], [xi])
                    S.dma("sp", lambda e, xi=xi, b=b, tt=tt: e.dma_start(out=out_d[b, (tt - 2) * 128:(tt - 1) * 128, :], in_=xi[:]), reads=[xi])
            S.barrier()
            S.flush()

    def stage_mlp(self, layer):
        S = self.S
        W = 256
        with ExitStack() as es:
            w1 = self.sb(es, "w1", [128, 8, HID], BF16)
            w2 = self.sb(es, "w2", [128, 32, D], BF16)
            for kc in range(8):
                self.load_w(w1, w1[:, kc, :], self.w1_d[layer, :, kc, :])
            for j4 in range(8):
                self.load_w(w2, w2[:, j4 * 4:(j4 + 1) * 4, :], self.w2_d[layer, :, j4 * 4:(j4 + 1) * 4, :])
            xts = [self.sb(es, f"mx{i}", [128, 8, W], F32) for i in range(2)]
            hT = self.sb(es, "mh", [128, 8, W], BF16)
            u = self.sb(es, "mu", [128, 32, W], BF16)
            f = self.sb(es, "mf", [128, 8, W], F32)
            rstd = self.sb(es, "mrstd", [128, W], F32)
            rl = [self.sb(es, f"mrl{i}", [128, W], F32) for i in range(2)]
            k = 0
            for b in range(self.nb):
                for t0 in range(0, T, W):
                    if t0 < TC and layer == self.last_layer:
                        continue
                    col = 4 if t0 < TC else b
                    xt = xts[k % 2]
                    k += 1
                    self.load_x(xt, b, t0, W)
                    self.rstd_of(xt, W, rstd)
                    self.modulate(xt, W, rstd, layer, 3, 4, col, hT)
                    for j in range(32):
                        p = self.ps()
                        for kc in range(8):
                            S.op("pe", lambda e, p=p, kc=kc, j=j: e.matmul(p[:, :W], lhsT=w1[:, kc, j * 128:(j + 1) * 128], rhs=hT[:, kc, :], start=(kc == 0), stop=(kc == 7)),
                                 reads=[w1, hT], writes=[p])
                        r = rl[j % 2]
                        S.op("act", lambda e, p=p, r=r: e.activation(out=r[:], in_=p[:, :W], func=AF.Relu), reads=[p], writes=[r])
                        S.op("dve" if j % 2 else "pool", lambda e, r=r, j=j: e.tensor_tensor(out=u[:, j, :], in0=r[:], in1=r[:], op=ALU.mult), reads=[r], writes=[u])
                    for c in range(8):
                        p = self.ps()
                        for j in range(32):
                            S.op("pe", lambda e, p=p, c=c, j=j: e.matmul(p[:, :W], lhsT=w2[:, j, c * 128:(c + 1) * 128], rhs=u[:, j, :], start=(j == 0), stop=(j == 31)),
                                 reads=[w2, u], writes=[p])
                        self.evac(c, f[:, c, :], p[:, :W], [p], [f])
                    self.rstd_of(f, W, rstd)
                    self.resid(xt, W, f, rstd, layer, 5, col)
                    self.store_x(xt, b, t0, W)
            S.barrier()
            S.flush()


def host_common(inp):
    vp = VecPack()
    vp.add("b_mod", np.concatenate([fm_vec(inp["b_mod"][l]) for l in range(4)], axis=1))
    for n in ("g_pre_mix", "g_post_mix", "g_pre_mlp", "g_post_mlp"):
        vp.add(n, np.concatenate([fm_vec(inp[n][l]) for l in range(4)], axis=1))
    com = {
        "w_mod": np.stack([wlay(np.asarray(inp["w_mod"][l], np.float32)) for l in range(4)]),
        "w_mlp_in": np.stack([wlay(np.asarray(inp["w_mlp_in"][l], np.float32)) for l in range(4)]),
        "w_mlp_out": np.stack([wlay(np.asarray(inp["w_mlp_out"][l], np.float32)) for l in range(4)]),
    }
    host_mixers(inp, vp, com)
    com["vecs"] = vp.pack()
    return com, vp.off, vp.n


def host_mixers(inp, vp, com):
    pass


def host_core(inp, b0, nb):
    c = np.asarray(inp["c"], np.float32)[b0:b0 + nb]
    cc = np.zeros((5, D), np.float32)
    cc[:nb] = c
    cc[4] = np.asarray(inp["c_ctx"], np.float32)
    cT = np.ascontiguousarray(cc.reshape(5, 8, 128).transpose(2, 1, 0))
    return {
        "x": np.ascontiguousarray(np.asarray(inp["x"], np.float32)[b0:b0 + nb]),
        "ctx": np.ascontiguousarray(np.asarray(inp["ctx"], np.float32)[b0:b0 + nb]),
        "cT": cT,
    }


def run(inp, nb, ncores, layers=(0, 1, 2, 3), prog_cls=None, last_layer=3, do_mlp=True, trace=False):
    com, voff, nv = host_common(inp)
    prog = (prog_cls or Prog)(nb, list(layers), voff, nv, last_layer=last_layer, do_mlp=do_mlp)
    nc = prog.build()
    in_maps = []
    for i in range(ncores):
        d = dict(com)
        d.update(host_core(inp, i * nb, nb))
        d = {k: v for k, v in d.items() if k in prog.dram}
        in_maps.append(d)
    res = run_bass_kernel_spmd(nc, in_maps, core_ids=list(range(ncores)), trace=trace)
    out = np.concatenate([np.asarray(r["out"]) for r in res.results], axis=0)
    return out.astype(np.float32), res, prog


def kernel(**inputs):
    out, _, _ = run(inputs, 4, NCORES)
    return out


def _partner():
    d = np.arange(64)
    axis, half, f = d // 32, (d % 32) // 16, d % 16
    return axis * 32 + (1 - half) * 16 + f


def host_attn(inp, vp, com):
    wqkv = np.asarray(inp["attn_w_qkv"][0], np.float32)
    wo = np.asarray(inp["attn_w_o"][0], np.float32)
    pr = _partner()
    qcols = np.arange(1024)
    qperm = (qcols // 64) * 64 + pr[qcols % 64]
    com["attn_wq"] = wlay(wqkv[:, :1024])
    com["attn_wqp"] = wlay(wqkv[:, qperm])
    kd = np.concatenate([1024 + g * 64 + np.concatenate([np.arange(64), np.arange(64)]) for g in range(4)])
    kdp = np.concatenate([1024 + g * 64 + np.concatenate([pr, pr]) for g in range(4)])
    com["attn_wk"] = wlay(wqkv[:, kd])
    com["attn_wkp"] = wlay(wqkv[:, kdp])
    com["attn_wv"] = wlay(wqkv[:, 1280:1536])
    com["attn_wo"] = np.ascontiguousarray(wo.reshape(16, 64, 1024).transpose(1, 0, 2))
    inv_freq = (np.float32(10000.0) ** (-np.arange(16, dtype=np.float32) * np.float32(2.0) / np.float32(32))).astype(np.float32)
    pos = np.arange(TL)
    row = (pos // 64).astype(np.float32)
    colp = (pos % 64).astype(np.float32)
    p = np.arange(128) % 64
    axis, half, f = p // 32, (p % 32) // 16, p % 16
    base = np.where(axis[:, None] == 0, row[None, :], colp[None, :]).astype(np.float32)
    ang = (base * inv_freq[f][:, None]).astype(np.float32)
    sgn = np.where(half == 0, -1.0, 1.0).astype(np.float32)[:, None]
    com["attn_cos"] = np.cos(ang).astype(np.float32)
    com["attn_sin"] = (np.sin(ang) * sgn).astype(np.float32)
    com["attn_sinkrow"] = np.ascontiguousarray(np.repeat(np.asarray(inp["attn_sink"][0], np.float32), 128)[None, :])
    kk = np.arange(128)[:, None]
    qq = np.arange(128)[None, :]
    com["attn_maskL"] = np.tile((qq <= kk).astype(np.float32), (1, 4))
    com["attn_maskU"] = np.tile((kk <= qq).astype(np.float32), (1, 4))


class FullProg(Prog):
    def declare_mixer_inputs(self):
        if 0 in self.layers:
            self.din("attn_wq", [128, 8, 1024])
            self.din("attn_wqp", [128, 8, 1024])
            self.din("attn_wk", [128, 8, 512])
            self.din("attn_wkp", [128, 8, 512])
            self.din("attn_wv", [128, 8, 256])
            self.din("attn_wo", [64, 16, 1024])
            self.din("attn_cos", [128, TL])
            self.din("attn_sin", [128, TL])
            self.din("attn_sinkrow", [1, 2048])
            self.din("attn_maskL", [128, 512])
            self.din("attn_maskU", [128, 512])

    def stage_mixer(self, layer):
        if layer == 0:
            self.stage_attn(layer)

    def stage_attn(self, layer):
        S = self.S
        dr = self.dram
        W = 256
        need_ctx = layer != self.last_layer
        with ExitStack() as es:
            wq = self.sb(es, "wq", [128, 8, 1024], BF16)
            wqp = self.sb(es, "wqp", [128, 8, 1024], BF16)
            wk = self.sb(es, "wk", [128, 8, 512], BF16)
            wkp = self.sb(es, "wkp", [128, 8, 512], BF16)
            wv = self.sb(es, "wv", [128, 8, 256], BF16)
            wo = self.sb(es, "wo", [64, 16, 1024], BF16)
            cos = self.sb(es, "cos", [128, TL], F32)
            sin = self.sb(es, "sin", [128, TL], F32)
            esink = self.sb(es, "esink", [1, 2048], BF16)
            sinkf = self.sb(es, "sinkf", [1, 2048], F32)
            mL = self.sb(es, "mL", [128, 512], BF16)
            mU = self.sb(es, "mU", [128, 512], BF16)
            for w, n in ((wq, "attn_wq"), (wqp, "attn_wqp"), (wk, "attn_wk"), (wkp, "attn_wkp"), (wv, "attn_wv"), (wo, "attn_wo"), (mL, "attn_maskL"), (mU, "attn_maskU")):
                self.load_w(w, w[:], dr[n])
            S.dma("sp", lambda e: e.dma_start(out=cos[:], in_=dr["attn_cos"][:, :]), writes=[cos])
            S.dma("sp", lambda e: e.dma_start(out=sin[:], in_=dr["attn_sin"][:, :]), writes=[sin])
            S.dma("sp", lambda e: e.dma_start(out=sinkf[:], in_=dr["attn_sinkrow"][:, :]), writes=[sinkf])
            S.op("act", lambda e: e.activation(out=esink[:], in_=sinkf[:], func=AF.Exp), reads=[sinkf], writes=[esink])
            kT = self.sb(es, "kT", [128, 4, T], BF16)
            V = self.sb(es, "V", [128, T // 128, 256], BF16)
            xts = [self.sb(es, f"ax{i}", [128, 8, W], F32) for i in range(2)]
            hT = self.sb(es, "ah", [128, 8, W], BF16)
            qT = self.sb(es, "aq", [128, 8, W], BF16)
            OT = self.sb(es, "aO", [64, 16, W], BF16)
            y = self.sb(es, "ay", [128, 8, W], F32)
            rstd = self.sb(es, "arstd", [128, W], F32)
            ta = [self.sb(es, f"ata{i}", [128, W], F32) for i in range(2)]
            tb = [self.sb(es, f"atb{i}", [128, W], F32) for i in range(2)]
            Et = [self.sb(es, f"aE{i}", [128, 512], BF16) for i in range(3)]
            rden = [self.sb(es, f"ard{i}", [64, 512], F32) for i in range(2)]
            kx = 0
            ei = 0
            ri = 0

            def rope(i, p1, p2, tl, out_ap, outtile):
                a, b2 = ta[i % 2], tb[i % 2]
                S.op("dve", lambda e: e.tensor_tensor(out=a[:], in0=p1[:, :W], in1=cos[:, tl:tl + W], op=ALU.mult), reads=[p1, cos], writes=[a])
                S.op("dve", lambda e: e.tensor_tensor(out=b2[:], in0=p2[:, :W], in1=sin[:, tl:tl + W], op=ALU.mult), reads=[p2, sin], writes=[b2])
                S.op("pool", lambda e: e.tensor_tensor(out=out_ap, in0=a[:], in1=b2[:], op=ALU.add), reads=[a, b2], writes=[outtile])

            for b in range(self.nb):
                for t0 in range(0, T, W):
                    if ATT_STOP <= 0:
                        continue
                    lat = t0 >= TC
                    col = b if lat else 4
                    xt = xts[kx % 2]
                    kx += 1
                    self.load_x(xt, b, t0, W)
                    self.rstd_of(xt, W, rstd)
                    self.modulate(xt, W, rstd, layer, 0, 1, col, hT)
                    for g in range(4):
                        p1 = self.ps()
                        for kc in range(8):
                            S.op("pe", lambda e, p1=p1, kc=kc, g=g: e.matmul(p1[:, :W], lhsT=wk[:, kc, g * 128:(g + 1) * 128], rhs=hT[:, kc, :], start=(kc == 0), stop=(kc == 7)),
                                 reads=[wk, hT], writes=[p1])
                        if lat:
                            p2 = self.ps()
                            for kc in range(8):
                                S.op("pe", lambda e, p2=p2, kc=kc, g=g: e.matmul(p2[:, :W], lhsT=wkp[:, kc, g * 128:(g + 1) * 128], rhs=hT[:, kc, :], start=(kc == 0), stop=(kc == 7)),
                                     reads=[wkp, hT], writes=[p2])
                            rope(g, p1, p2, t0 - TC, kT[:, g, t0:t0 + W], kT)
                        else:
                            self.evac(g, kT[:, g, t0:t0 + W], p1[:, :W], [p1], [kT])
                    for tt in range(W // 128):
                        p = self.ps()
                        for kc in range(8):
                            S.op("pe", lambda e, p=p, kc=kc, tt=tt: e.matmul(p[:, :256], lhsT=hT[:, kc, tt * 128:(tt + 1) * 128], rhs=wv[:, kc, :], start=(kc == 0), stop=(kc == 7)),
                                 reads=[wv, hT], writes=[p])
                        self.evac(tt, V[:, t0 // 128 + tt, :], p[:, :256], [p], [V])
                for t0 in range(0, T, W):
                    lat = t0 >= TC
                    if not lat and not need_ctx:
                        continue
                    if ATT_STOP <= 1:
                        continue
                    col = b if lat else 4
                    xt = xts[kx % 2]
                    kx += 1
                    self.load_x(xt, b, t0, W)
                    self.rstd_of(xt, W, rstd)
                    self.modulate(xt, W, rstd, layer, 0, 1, col, hT)
                    for c in range(8):
                        p1 = self.ps()
                        for kc in range(8):
                            S.op("pe", lambda e, p1=p1, kc=kc, c=c: e.matmul(p1[:, :W], lhsT=wq[:, kc, c * 128:(c + 1) * 128], rhs=hT[:, kc, :], start=(kc == 0), stop=(kc == 7)),
                                 reads=[wq, hT], writes=[p1])
                        if lat:
                            p2 = self.ps()
                            for kc in range(8):
                                S.op("pe", lambda e, p2=p2, kc=kc, c=c: e.matmul(p2[:, :W], lhsT=wqp[:, kc, c * 128:(c + 1) * 128], rhs=hT[:, kc, :], start=(kc == 0), stop=(kc == 7)),
                                     reads=[wqp, hT], writes=[p2])
                            rope(c, p1, p2, t0 - TC, qT[:, c, :], qT)
                        else:
                            self.evac(c, qT[:, c, :], p1[:, :W], [p1], [qT])
                    for qb in range(W // 128):
                        if ATT_STOP <= 2:
                            continue
                        q0 = qb * 128
                        if lat:
                            bi = (t0 - TC) // 128 + qb
                            chunks = []
                            if bi > 0:
                                chunks.append((2 + bi - 1, mL))
                            chunks.append((2 + bi, None))
                            if bi < 15:
                                chunks.append((2 + bi + 1, mU))
                            chunks += [(0, None), (1, None)]
                        else:
                            chunks = [(0, None), (1, None)]
                        for g in range(4):
                            pO = self.ps()
                            pD = self.ps()
                            for ci, (kt, mask) in enumerate(chunks):
                                pSs = [self.ps_ex((pO, pD)), self.ps_ex((pO, pD))]
                                for hh in range(4):
                                    h = 4 * g + hh
                                    c, s = h // 2, h % 2
                                    pS = pSs[s]
                                    S.op("pe", lambda e, pS=pS, hh=hh, c=c, s=s, kt=kt, g=g, q0=q0: e.matmul(
                                        pS[:, (hh // 2) * 128:(hh // 2 + 1) * 128], lhsT=kT[s * 64:(s + 1) * 64, g, kt * 128:(kt + 1) * 128], rhs=qT[s * 64:(s + 1) * 64, c, q0:q0 + 128], start=True, stop=True),
                                        reads=[kT, qT], writes=[pS])
                                E = Et[ei % 3]
                                ei += 1
                                for s in range(2):
                                    S.op("act", lambda e, E=E, pS=pSs[s], s=s: e.activation(out=E[:].rearrange("p (c s q) -> p c s q", c=2, s=2)[:, :, s, :], in_=pS[:, 0:256].rearrange("p (c q) -> p c q", c=2),
                                                                                          func=AF.Exp, scale=0.125), reads=[pSs[s]], writes=[E])
                                if mask is not None:
                                    S.op("pool", lambda e, E=E, mask=mask: e.tensor_tensor(out=E[:], in0=E[:], in1=mask[:], op=ALU.mult), reads=[E, mask], writes=[E])
                                last = ci == len(chunks) - 1
                                if ATT_SUB == "a":
                                    continue
                                S.op("pe", lambda e, pO=pO, E=E, kt=kt, g=g, ci=ci, last=last: e.matmul(pO[0:64, :], lhsT=V[:, kt, g * 64:(g + 1) * 64], rhs=E[:], start=(ci == 0), stop=last),
                                     reads=[V, E], writes=[pO])
                                S.op("pe", lambda e, pD=pD, E=E, ci=ci, last=last: e.matmul(pD[0:64, :], lhsT=self.ones_b[:, 0:64], rhs=E[:], start=(ci == 0), stop=(last and ATT_SUB == "b")),
                                     reads=[self.ones_b, E], writes=[pD])
                            if ATT_SUB == "a":
                                continue
                            if ATT_SUB != "b":
                                S.op("pe", lambda e, pD=pD, g=g: e.matmul(pD[0:64, :], lhsT=self.ones_b[0:1, 0:64], rhs=esink[0:1, g * 512:(g + 1) * 512], start=False, stop=True),
                                     reads=[self.ones_b, esink], writes=[pD])
                            if ATT_SUB in ("b", "c"):
                                continue
                            rd = rden[ri % 2]
                            ri += 1
                            S.op("dve", lambda e, rd=rd, pD=pD: e.reciprocal(out=rd[:], in_=pD[0:64, :]), reads=[pD], writes=[rd])
                            S.op("dve", lambda e, rd=rd, pO=pO, g=g, q0=q0: e.tensor_tensor(out=OT[:, 4 * g:4 * g + 4, q0:q0 + 128], in0=pO[0:64, :].rearrange("p (h q) -> p h q", h=4),
                                                                                          in1=rd[:].rearrange("p (h q) -> p h q", h=4), op=ALU.mult), reads=[pO, rd], writes=[OT])
                    if ATT_STOP <= 3:
                        continue
                    for c in range(8):
                        p = self.ps()
                        for h in range(16):
                            S.op("pe", lambda e, p=p, h=h, c=c: e.matmul(p[:, :W], lhsT=wo[:, h, c * 128:(c + 1) * 128], rhs=OT[:, h, :], start=(h == 0), stop=(h == 15)),
                                 reads=[wo, OT], writes=[p])
                        self.evac(c, y[:, c, :], p[:, :W], [p], [y])
                    self.rstd_of(y, W, rstd)
                    self.resid(xt, W, y, rstd, layer, 2, col)
                    self.store_x(xt, b, t0, W)
            S.barrier()
            S.flush()


def host_mixers(inp, vp, com):
    host_attn(inp, vp, com)


def scan_consts(gscale):
    s = np.arange(128)[:, None]
    t = np.arange(128)[None, :]
    same = (s // 64) == (t // 64)
    triF = (same & (s <= t)).astype(np.float32)
    triB = (same & (s >= t)).astype(np.float32)
    trisF = (same & (s > t)).astype(np.float32)
    trisB = (same & (s < t)).astype(np.float32)
    return np.ascontiguousarray(np.concatenate([triF * gscale, triB * gscale, trisF * gscale, trisB * gscale, np.tile(triF, (1, 4)), np.tile(triB, (1, 4))], axis=1).astype(np.float32))


def host_gla(inp, vp, com):
    com["gla_win"] = wlay(np.asarray(inp["gla_w_in"][0], np.float32))
    wd = np.asarray(inp["gla_w_gate_down"][0], np.float32)
    com["gla_wgd"] = wlay(np.concatenate([wd[0], wd[1]], axis=1))
    com["gla_wgu"] = np.ascontiguousarray(np.asarray(inp["gla_w_gate_up"][0], np.float32).transpose(1, 0, 2))
    com["gla_gbias"] = np.ascontiguousarray(np.asarray(inp["gla_gate_bias"][0], np.float32).reshape(1, 1024))
    com["gla_wo"] = wlay(np.asarray(inp["gla_w_o"][0], np.float32))
    com["gla_consts"] = scan_consts(1.0 / 16.0)
    vp.add("gla_gn", fm_vec(np.tile(np.asarray(inp["gla_g_norm"][0], np.float32), 4)))


def host_hgrn(inp, vp, com):
    com["hgrn_win"] = wlay(np.asarray(inp["hgrn_w_in"][0], np.float32))
    wf = np.asarray(inp["hgrn_w_f"][0], np.float32)
    com["hgrn_wf"] = np.stack([wlay(wf[0]), wlay(wf[1])])
    com["hgrn_wo"] = wlay(np.asarray(inp["hgrn_w_o"][0], np.float32))
    com["hgrn_consts"] = scan_consts(1.0)
    lb = np.asarray(inp["hgrn_lb"], np.float32)
    vp.add("hgrn_lb", np.concatenate([fm_vec(lb[l]) for l in range(4)], axis=1))
    com["hgrn_lbrow"] = np.ascontiguousarray(np.broadcast_to(lb.reshape(1, 4096), (128, 4096)))
    vp.add("hgrn_gn", fm_vec(np.tile(np.asarray(inp["hgrn_g_norm"][0], np.float32), 8)))


class ScanMixin:
    def scan_setup(self, es, consts_d, H, dv):
        S = self.S
        c = self.sb(es, "sconst", [128, 4 * 128 + 2 * 512], F32)
        S.dma("sp", lambda e: e.dma_start(out=c[:], in_=consts_d[:, :]), writes=[c])
        self.sc = c
        self.sH, self.sdv = H, dv
        G = H // 4
        self.sG = G
        mk = lambda n, shp, dt: self.sb(es, n, shp, dt)
        self.s_eb = [mk(f"s_eb{g}", [128, 4, 128], F32) for g in range(G)]
        self.s_emb = mk("s_emb", [128, 4, 128], F32)
        self.s_qd = [mk(f"s_qd{g}", [128, 4, 128], BF16) for g in range(G)]
        self.s_ki = mk("s_ki", [128, 4, 128], BF16)
        self.s_esuf = mk("s_esuf", [128, 512], F32)
        self.s_kend = [mk(f"s_kend{g}", [128, 512], BF16) for g in range(G)]
        self.s_AmT = [mk(f"s_AmT{g}", [128, 4, 128], BF16) for g in range(G)]
        self.s_S = mk("s_S", [128, H, dv], F32)
        self.s_Sb = mk("s_Sb", [128, H, dv], BF16)
        self.s_ob = [mk(f"s_ob{i}", [128, 8, 128], F32) for i in range(2)]
        self.s_ol = mk("s_ol", [128, 8, 128], F32)
        self._obi = 0

    def scan_reset(self):
        self.S.op("pool", lambda e: e.memset(self.s_S[:], 0.0), writes=[self.s_S])
        self.S.op("pool", lambda e: e.memset(self.s_Sb[:], 0.0), writes=[self.s_Sb])

    def scan_block(self, d, qf, kf, ktm, vtm, gtm, OACC, blk, first_dir):
        S = self.S
        sc = self.sc
        H, dv, G = self.sH, self.sdv, self.sG
        dvc = dv // 128
        tri = sc[:, d * 128:(d + 1) * 128]
        tris = sc[:, 256 + d * 128:256 + (d + 1) * 128]
        mask4 = sc[:, 512 + d * 512:512 + (d + 1) * 512]
        for g in range(G):
            eb, emb, qd, ki, kend, AmT = self.s_eb[g], self.s_emb, self.s_qd[g], self.s_ki, self.s_kend[g], self.s_AmT[g]
            pb = self.ps()
            for hh in range(4):
                h = g * 4 + hh
                S.op("pe", lambda e, pb=pb, hh=hh, h=h: e.matmul(pb[:, hh * 128:(hh + 1) * 128], lhsT=gtm[:, h * 128:(h + 1) * 128], rhs=tri, start=True, stop=True),
                     reads=[gtm, sc], writes=[pb])
            S.op("act", lambda e, pb=pb, eb=eb: e.activation(out=eb[:].rearrange("p h t -> p (h t)"), in_=pb[:], func=AF.Exp), reads=[pb], writes=[eb])
            S.op("act", lambda e, pb=pb, emb=emb: e.activation(out=emb[:].rearrange("p h t -> p (h t)"), in_=pb[:], func=AF.Exp, scale=-1.0), reads=[pb], writes=[emb])
            S.op("dve", lambda e, g=g, qd=qd, eb=eb: e.tensor_tensor(out=qd[:], in0=qf[:, g * 4:(g + 1) * 4, :], in1=eb[:], op=ALU.mult), reads=[qf, eb], writes=[qd])
            S.op("pool", lambda e, g=g, ki=ki, emb=emb: e.tensor_tensor(out=ki[:], in0=kf[:, g * 4:(g + 1) * 4, :], in1=emb[:], op=ALU.mult), reads=[kf, emb], writes=[ki])
            psf = self.ps()
            S.op("pe", lambda e, psf=psf, g=g: e.matmul(psf[:], lhsT=tris, rhs=gtm[:, g * 512:(g + 1) * 512], start=True, stop=True), reads=[gtm, sc], writes=[psf])
            S.op("act", lambda e, psf=psf: e.activation(out=self.s_esuf[:], in_=psf[:], func=AF.Exp), reads=[psf], writes=[self.s_esuf])
            S.op("dve", lambda e, kend=kend, g=g: e.tensor_tensor(out=kend[:], in0=ktm[:, g * 512:(g + 1) * 512], in1=self.s_esuf[:], op=ALU.mult), reads=[ktm, self.s_esuf], writes=[kend])
            pA = self.ps()
            for hh in range(4):
                S.op("pe", lambda e, pA=pA, hh=hh, ki=ki, qd=qd: e.matmul(pA[:, hh * 128:(hh + 1) * 128], lhsT=ki[:, hh, :], rhs=qd[:, hh, :], start=True, stop=True), reads=[ki, qd], writes=[pA])
            S.op("dve", lambda e, pA=pA, AmT=AmT: e.tensor_tensor(out=AmT[:].rearrange("p h t -> p (h t)"), in0=pA[:], in1=mask4, op=ALU.mult), reads=[pA, sc], writes=[AmT])
        ob = self.s_ob[self._obi % 2]
        self._obi += 1
        po = [self.ps(), self.ps()]

        def po_ap(oc, lo, hi):
            return po[oc // 4][:, (oc % 4) * 128 + lo:(oc % 4) * 128 + hi]

        for h in range(H):
            for vc in range(dvc):
                oc = h * dvc + vc
                S.op("pe", lambda e, h=h, vc=vc, oc=oc: e.matmul(po_ap(oc, 0, 128), lhsT=vtm[:, h * dv + vc * 128:h * dv + (vc + 1) * 128], rhs=self.s_AmT[h // 4][:, h % 4, :], start=(oc % 4 == 0), stop=False, skip_group_check=True),
                     reads=[vtm, self.s_AmT[h // 4]], writes=[po[oc // 4]])
        order = (0, 1) if d == 0 else (1, 0)
        for cc in order:
            lo, hi = cc * 64, (cc + 1) * 64
            for h in range(H):
                for vc in range(dvc):
                    oc = h * dvc + vc
                    S.op("pe", lambda e, h=h, vc=vc, oc=oc, lo=lo, hi=hi: e.matmul(po_ap(oc, lo, hi), lhsT=self.s_Sb[:, h, vc * 128:(vc + 1) * 128], rhs=self.s_qd[h // 4][:, h % 4, lo:hi], start=False, stop=True, skip_group_check=True),
                         reads=[self.s_Sb, self.s_qd[h // 4]], writes=[po[oc // 4]])
            colb = (63 if cc == 0 else 127) if d == 0 else (0 if cc == 0 else 64)
            hpb = 512 // dv
            for h0 in range(0, H, hpb):
                pS = self.ps()
                for h in range(h0, h0 + hpb):
                    S.op("pe", lambda e, pS=pS, h=h, h0=h0, lo=lo, hi=hi: e.matmul(pS[:, (h - h0) * dv:(h - h0 + 1) * dv], lhsT=self.s_kend[h // 4][lo:hi, (h % 4) * 128:(h % 4 + 1) * 128],
                                                                                rhs=vtm[lo:hi, h * dv:(h + 1) * dv], start=True, stop=True), reads=[self.s_kend[h // 4], vtm], writes=[pS])
                for h in range(h0, h0 + hpb):
                    S.op("dve", lambda e, pS=pS, h=h, h0=h0, colb=colb: e.scalar_tensor_tensor(out=self.s_S[:, h, :], in0=self.s_S[:, h, :], scalar=self.s_eb[h // 4][:, h % 4, colb:colb + 1],
                                                                                             in1=pS[:, (h - h0) * dv:(h - h0 + 1) * dv], op0=ALU.mult, op1=ALU.add),
                         reads=[pS, self.s_eb[h // 4], self.s_S], writes=[self.s_S])
            S.op("pool", lambda e: e.tensor_copy(out=self.s_Sb[:], in_=self.s_S[:]), reads=[self.s_S], writes=[self.s_Sb])
        for i in range(2):
            self.evac(i, ob[:, i * 4:(i + 1) * 4, :], po[i][:].rearrange("p (c t) -> p c t", c=4), [po[i]], [ob])
        t0 = blk * 128
        if not first_dir:
            ol = self.s_ol
            S.dma("sp", lambda e: e.dma_start(out=ol[:], in_=OACC.t[blk]), reads=[OACC], writes=[ol])
            S.op("pool", lambda e: e.tensor_tensor(out=ob[:], in0=ob[:], in1=ol[:], op=ALU.add), reads=[ob, ol], writes=[ob])
        S.dma("sp", lambda e: e.dma_start(out=OACC.t[blk], in_=ob[:]), reads=[ob], writes=[OACC])

    def fill_hTall(self, hTall, xts, rstd, b, layer, W=256):
        k = 0
        for t0 in range(0, T, W):
            col = b if t0 >= TC else 4
            xt = xts[k % 2]
            k += 1
            self.load_x(xt, b, t0, W)
            self.rstd_of(xt, W, rstd)
            self.modulate(xt, W, rstd, layer, 0, 1, col, hTall, hlo=t0)

    def phase_out(self, layer, b, OACC, hTall, w_gate, gate_c0, wo, gn_name, H, xts, rstd, og, y, oin, sg):
        S = self.S
        W = 256
        need_ctx = layer != self.last_layer
        cph = 8 // H if H <= 8 else 1
        k = 0
        for t0 in range(0, T, W):
            lat = t0 >= TC
            if not lat and not need_ctx:
                continue
            col = b if lat else 4
            xt = xts[k % 2]
            k += 1
            self.load_x(xt, b, t0, W)
            raise NotImplementedError("phase_out is unused (see stage_out_part)")
            for c in range(8):
                p = self.ps()
                for kc in range(8):
                    S.op("pe", lambda e, p=p, kc=kc, c=c, t0=t0: e.matmul(p[:, :W], lhsT=w_gate[:, kc, gate_c0 + c * 128:gate_c0 + (c + 1) * 128], rhs=hTall[:, kc, t0:t0 + W], start=(kc == 0), stop=(kc == 7)),
                         reads=[w_gate, hTall], writes=[p])
                S.op("act", lambda e, p=p, c=c: e.activation(out=sg[:, c, :], in_=p[:, :W], func=AF.Silu), reads=[p], writes=[sg])
            nchh = 8 // H
            for h in range(H):
                self.rstd_of(oin, W, rstd, nch=nchh, scale=1.0 / (128 * nchh), c0=h * nchh)
                for cc in range(nchh):
                    c = h * nchh + cc
                    tmp = self.tmpf[c % 2]
                    S.op("dve", lambda e, tmp=tmp, c=c: e.scalar_tensor_tensor(out=tmp[:, :W], in0=oin[:, c, :], scalar=self.vcol(gn_name, c), in1=rstd[:, :W], op0=ALU.mult, op1=ALU.mult),
                         reads=[oin, rstd, self.vecs], writes=[tmp])
                    S.op("pool", lambda e, tmp=tmp, c=c: e.tensor_tensor(out=og[:, c, :], in0=tmp[:, :W], in1=sg[:, c, :], op=ALU.mult), reads=[tmp, sg], writes=[og])
            for c in range(8):
                p = self.ps()
                for kc in range(8):
                    S.op("pe", lambda e, p=p, kc=kc, c=c: e.matmul(p[:, :W], lhsT=wo[:, kc, c * 128:(c + 1) * 128], rhs=og[:, kc, :], start=(kc == 0), stop=(kc == 7)), reads=[wo, og], writes=[p])
                self.evac(c, y[:, c, :], p[:, :W], [p], [y])
            self.rstd_of(y, W, rstd)
            self.resid(xt, W, y, rstd, layer, 2, col)
            self.store_x(xt, b, t0, W)


SCAN_ORDER = {0: list(range(18)), 1: [1, 0] + list(range(17, 1, -1))}


class FullProg2(ScanMixin, FullProg):
    def declare_mixer_inputs(self):
        FullProg.declare_mixer_inputs(self)
        if 1 in self.layers:
            self.din("gla_win", [128, 8, 3072])
            self.din("gla_wgd", [128, 8, 32])
            self.din("gla_wgu", [16, 2, 512])
            self.din("gla_gbias", [1, 1024])
            self.din("gla_wo", [128, 8, 1024])
            self.din("gla_consts", [128, 1536])
        if 3 in self.layers:
            self.din("hgrn_win", [128, 8, 3072])
            self.din("hgrn_wf", [2, 128, 8, 1024])
            self.din("hgrn_wo", [128, 8, 1024])
            self.din("hgrn_consts", [128, 1536])
            self.din("hgrn_lbrow", [128, 4096])
        if 1 in self.layers or 3 in self.layers or 2 in self.layers:
            self.OACC = [Tile(self.dscratch(f"OACC{b}", [T // 128, 128, 8, 128]), f"OACC{b}") for b in range(self.nb)]

    def stage_mixer(self, layer):
        if layer == 0:
            self.stage_attn(layer)
        elif layer == 1:
            self.stage_gla(layer)
        elif layer == 3:
            self.stage_hgrn(layer)
        elif layer == 2:
            self.stage_rwkv(layer)

    def stage_out_part(self, layer, w_gate_d, gate_c0, wo_d, gn_name, H):
        S = self.S
        W = 256
        with ExitStack() as es:
            wg = self.sb(es, "po_wg", [128, 8, 1024], BF16)
            wo = self.sb(es, "po_wo", [128, 8, 1024], BF16)
            self.load_w(wg, wg[:], w_gate_d[:, :, gate_c0:gate_c0 + 1024])
            self.load_w(wo, wo[:], wo_d)
            xts = [self.sb(es, f"po_x{i}", [128, 8, W], F32) for i in range(2)]
            rstd = self.sb(es, "po_rstd", [128, W], F32)
            hT = self.sb(es, "po_h", [128, 8, W], BF16)
            og = self.sb(es, "po_og", [128, 8, W], BF16)
            y = self.sb(es, "po_y", [128, 8, W], F32)
            oin = self.sb(es, "po_oin", [128, 2, 8, 128], F32)
            oinv = oin[:].rearrange("p j c t -> p c j t")
            v3 = lambda ap: ap.rearrange("p (j t) -> p j t", j=2)
            sg = self.sb(es, "po_sg", [128, 8, W], F32)
            need_ctx = layer != self.last_layer
            nchh = 8 // H
            k = 0
            for b in range(self.nb):
                OACC = self.OACC[b]
                for t0 in range(0, T, W):
                    lat = t0 >= TC
                    if not lat and not need_ctx:
                        continue
                    col = b if lat else 4
                    xt = xts[k % 2]
                    k += 1
                    self.load_x(xt, b, t0, W)
                    self.rstd_of(xt, W, rstd)
                    self.modulate(xt, W, rstd, layer, 0, 1, col, hT)
                    for j in range(2):
                        S.dma("sp", lambda e, j=j, t0=t0, OACC=OACC: e.dma_start(out=oin[:, j], in_=OACC.t[t0 // 128 + j]), reads=[OACC], writes=[oin])
                    for c in range(8):
                        p = self.ps()
                        for kc in range(8):
                            S.op("pe", lambda e, p=p, kc=kc, c=c: e.matmul(p[:, :W], lhsT=wg[:, kc, c * 128:(c + 1) * 128], rhs=hT[:, kc, :], start=(kc == 0), stop=(kc == 7)), reads=[wg, hT], writes=[p])
                        S.op("act", lambda e, p=p, c=c: e.activation(out=sg[:, c, :], in_=p[:, :W], func=AF.Silu), reads=[p], writes=[sg])
                    for h in range(H):
                        pr = self.ps()
                        for cc in range(nchh):
                            c = h * nchh + cc
                            sq = self.sqb[cc % 2]
                            S.op("act", lambda e, sq=sq, c=c: e.activation(out=v3(sq[:, :W]), in_=oinv[:, c], func=AF.Square), reads=[oin], writes=[sq])
                            S.op("pe", lambda e, sq=sq, cc=cc, pr=pr: e.matmul(pr[:, :W], lhsT=self.ones_b[:, :], rhs=sq[:, :W], start=(cc == 0), stop=(cc == nchh - 1)), reads=[sq, self.ones_b], writes=[pr])
                        S.op("act", lambda e, pr=pr: e.activation(out=rstd[:, :W], in_=pr[:, :W], func=AF.Sqrt, bias=self.epsb[:, 0:1], scale=1.0 / (128 * nchh)), reads=[pr, self.epsb], writes=[rstd])
                        S.op("dve", lambda e: e.reciprocal(out=rstd[:, :W], in_=rstd[:, :W]), reads=[rstd], writes=[rstd])
                        for cc in range(nchh):
                            c = h * nchh + cc
                            tmp = self.tmpf[c % 2]
                            S.op("dve", lambda e, tmp=tmp, c=c: e.scalar_tensor_tensor(out=v3(tmp[:, :W]), in0=oinv[:, c], scalar=self.vcol(gn_name, c), in1=v3(rstd[:, :W]), op0=ALU.mult, op1=ALU.mult),
                                 reads=[oin, rstd, self.vecs], writes=[tmp])
                            S.op("pool", lambda e, tmp=tmp, c=c: e.tensor_tensor(out=og[:, c, :], in0=tmp[:, :W], in1=sg[:, c, :], op=ALU.mult), reads=[tmp, sg], writes=[og])
                    for c in range(8):
                        p = self.ps()
                        for kc in range(8):
                            S.op("pe", lambda e, p=p, kc=kc, c=c: e.matmul(p[:, :W], lhsT=wo[:, kc, c * 128:(c + 1) * 128], rhs=og[:, kc, :], start=(kc == 0), stop=(kc == 7)), reads=[wo, og], writes=[p])
                        self.evac(c, y[:, c, :], p[:, :W], [p], [y])
                    self.rstd_of(y, W, rstd)
                    self.resid(xt, W, y, rstd, layer, 2, col)
                    self.store_x(xt, b, t0, W)
            S.barrier()
            S.flush()

    def stage_gla(self, layer):
        S = self.S
        dr = self.dram
        with ExitStack() as es:
            win = self.sb(es, "g_win", [128, 8, 2048], BF16)
            wgd = self.sb(es, "g_wgd", [128, 8, 32], BF16)
            wgu = self.sb(es, "g_wgu", [16, 2, 512], BF16)
            gbias = self.sb(es, "g_gb", [1, 1024], BF16)
            self.load_w(win, win[:], dr["gla_win"][:, :, 0:2048])
            self.load_w(wgd, wgd[:], dr["gla_wgd"])
            self.load_w(wgu, wgu[:], dr["gla_wgu"])
            self.load_w(gbias, gbias[:], dr["gla_gbias"])
            self.scan_setup(es, dr["gla_consts"], 4, 256)
            hTall = self.sb(es, "g_hT", [128, 8, T], BF16)
            xts = [self.sb(es, f"g_x{i}", [128, 8, 256], F32) for i in range(2)]
            rstd = self.sb(es, "g_rstd", [128, 256], F32)
            qf = self.sb(es, "g_qf", [128, 4, 128], F32)
            kf = self.sb(es, "g_kf", [128, 4, 128], F32)
            ktm = self.sb(es, "g_ktm", [128, 512], BF16)
            vtm = self.sb(es, "g_vtm", [128, 1024], BF16)
            gtm = self.sb(es, "g_gtm", [128, 512], F32)
            sig = self.sb(es, "g_sig", [128, 512], F32)
            hdT = self.sb(es, "g_hdT", [16, 128], BF16)
            for b in range(self.nb):
                self.fill_hTall(hTall, xts, rstd, b, layer)
                for d in (0, 1):
                    self.scan_reset()
                    for blk in SCAN_ORDER[d]:
                        ts = blk * 128
                        pq = self.ps()
                        pk = self.ps()
                        for hh in range(4):
                            for kc in range(8):
                                S.op("pe", lambda e, pq=pq, hh=hh, kc=kc, ts=ts: e.matmul(pq[:, hh * 128:(hh + 1) * 128], lhsT=win[:, kc, hh * 128:(hh + 1) * 128], rhs=hTall[:, kc, ts:ts + 128], start=(kc == 0), stop=(kc == 7)),
                                     reads=[win, hTall], writes=[pq])
                        for hh in range(4):
                            for kc in range(8):
                                S.op("pe", lambda e, pk=pk, hh=hh, kc=kc, ts=ts: e.matmul(pk[:, hh * 128:(hh + 1) * 128], lhsT=win[:, kc, 512 + hh * 128:512 + (hh + 1) * 128], rhs=hTall[:, kc, ts:ts + 128], start=(kc == 0), stop=(kc == 7)),
                                     reads=[win, hTall], writes=[pk])
                        S.op("act", lambda e, pq=pq: e.activation(out=qf[:].rearrange("p h t -> p (h t)"), in_=pq[:], func=AF.Copy, scale=float(128 ** -0.5)), reads=[pq], writes=[qf])
                        S.op("dve", lambda e, pk=pk: e.tensor_copy(out=kf[:].rearrange("p h t -> p (h t)"), in_=pk[:]), reads=[pk], writes=[kf])
                        pkt = self.ps()
                        for kc in range(8):
                            S.op("pe", lambda e, pkt=pkt, kc=kc, ts=ts: e.matmul(pkt[:], lhsT=hTall[:, kc, ts:ts + 128], rhs=win[:, kc, 512:1024], start=(kc == 0), stop=(kc == 7)), reads=[win, hTall], writes=[pkt])
                        S.op("act", lambda e, pkt=pkt: e.activation(out=ktm[:], in_=pkt[:], func=AF.Copy), reads=[pkt], writes=[ktm])
                        for half in range(2):
                            pv = self.ps()
                            for kc in range(8):
                                S.op("pe", lambda e, pv=pv, kc=kc, ts=ts, half=half: e.matmul(pv[:], lhsT=hTall[:, kc, ts:ts + 128], rhs=win[:, kc, 1024 + half * 512:1024 + (half + 1) * 512], start=(kc == 0), stop=(kc == 7)),
                                     reads=[win, hTall], writes=[pv])
                            self.evac(half, vtm[:, half * 512:(half + 1) * 512], pv[:], [pv], [vtm])
                        phd = self.ps()
                        for kc in range(8):
                            S.op("pe", lambda e, phd=phd, kc=kc, ts=ts, d=d: e.matmul(phd[0:16, 0:128], lhsT=wgd[:, kc, d * 16:(d + 1) * 16], rhs=hTall[:, kc, ts:ts + 128], start=(kc == 0), stop=(kc == 7)),
                                 reads=[wgd, hTall], writes=[phd])
                        S.op("dve", lambda e, phd=phd: e.tensor_copy(out=hdT[:], in_=phd[0:16, 0:128]), reads=[phd], writes=[hdT])
                        pz = self.ps()
                        S.op("pe", lambda e, pz=pz, d=d: e.matmul(pz[:], lhsT=hdT[:, :], rhs=wgu[:, d, :], start=True, stop=False), reads=[hdT, wgu], writes=[pz])
                        S.op("pe", lambda e, pz=pz, d=d: e.matmul(pz[:], lhsT=self.ones_b[0:1, :], rhs=gbias[0:1, d * 512:(d + 1) * 512], start=False, stop=True), reads=[self.ones_b, gbias], writes=[pz])
                        S.op("act", lambda e, pz=pz: e.activation(out=sig[:], in_=pz[:], func=AF.Sigmoid), reads=[pz], writes=[sig])
                        S.op("act", lambda e: e.activation(out=gtm[:], in_=sig[:], func=AF.Ln), reads=[sig], writes=[gtm])
                        self.scan_block(d, qf, kf, ktm, vtm, gtm, self.OACC[b], blk, d == 0)
            S.barrier()
            S.flush()
        self.stage_out_part(layer, dr["gla_win"], 2048, dr["gla_wo"], "gla_gn", 4)

    def stage_hgrn(self, layer):
        S = self.S
        dr = self.dram
        with ExitStack() as es:
            win = self.sb(es, "h_win", [128, 8, 2048], BF16)
            wf = [self.sb(es, f"h_wf{d}", [128, 8, 1024], BF16) for d in range(2)]
            self.load_w(win, win[:], dr["hgrn_win"][:, :, 0:2048])
            for d in range(2):
                self.load_w(wf[d], wf[d][:], dr["hgrn_wf"][d])
            self.scan_setup(es, dr["hgrn_consts"], 8, 128)
            lb_row = self.sb(es, "h_lbrow", [128, 1024], F32)
            oml_row = self.sb(es, "h_omlrow", [128, 1024], F32)
            lb_f = self.sb(es, "h_lbfm", [128, 8], F32)
            oml_f = self.sb(es, "h_omlfm", [128, 8], F32)
            es_outer = es
            es = es_tmp = ExitStack()
            lbr = self.sb(es, "h_lbr", [128, 4096], F32)
            S.dma("sp", lambda e: e.dma_start(out=lbr[:], in_=dr["hgrn_lbrow"][:, :]), writes=[lbr])
            S.op("act", lambda e: e.activation(out=lbr[:], in_=lbr[:], func=AF.Exp), reads=[lbr], writes=[lbr])
            lbf = self.sb(es, "h_lbf", [128, 32], F32)
            o = self.voff["hgrn_lb"][0]
            S.op("act", lambda e: e.activation(out=lbf[:], in_=self.vecs[:, o:o + 32], func=AF.Exp), reads=[self.vecs], writes=[lbf])
            den_row = self.sb(es, "h_denrow", [128, 1024], F32)
            den_f = self.sb(es, "h_denfm", [128, 8], F32)
            for (src, n, num, oml, den) in ((lbr, 1024, lb_row, oml_row, den_row), (lbf, 8, lb_f, oml_f, den_f)):
                if layer == 0:
                    S.op("dve", lambda e, num=num: e.memset(num[:], 0.0), writes=[num])
                else:
                    S.op("dve", lambda e, num=num, src=src, n=n: e.tensor_copy(out=num[:], in_=src[:, n:2 * n]), reads=[src], writes=[num])
                    for j in range(2, layer + 1):
                        S.op("dve", lambda e, num=num, src=src, n=n, j=j: e.tensor_tensor(out=num[:], in0=num[:], in1=src[:, j * n:(j + 1) * n], op=ALU.add), reads=[src, num], writes=[num])
                S.op("dve", lambda e, num=num, src=src, n=n, den=den: e.tensor_tensor(out=den[:], in0=num[:], in1=src[:, 0:n], op=ALU.add), reads=[src, num], writes=[den])
                S.op("dve", lambda e, den=den: e.reciprocal(out=den[:], in_=den[:]), reads=[den], writes=[den])
                S.op("dve", lambda e, num=num, den=den: e.tensor_tensor(out=num[:], in0=num[:], in1=den[:], op=ALU.mult), reads=[num, den], writes=[num])
                S.op("dve", lambda e, oml=oml, src=src, n=n, den=den: e.tensor_tensor(out=oml[:], in0=src[:, 0:n], in1=den[:], op=ALU.mult), reads=[src, den], writes=[oml])
            S.barrier()
            S.flush()
            es_tmp.close()
            es = es_outer
            hTall = self.sb(es, "h_hT", [128, 8, T], BF16)
            xts = [self.sb(es, f"h_x{i}", [128, 8, 256], F32) for i in range(2)]
            rstd = self.sb(es, "h_rstd", [128, 256], F32)
            qf = self.sb(es, "h_qf", [128, 8, 128], F32)
            kf = self.sb(es, "h_kf", [128, 8, 128], F32)
            sgn = self.sb(es, "h_sgn", [128, 8, 128], F32)
            ktm = self.sb(es, "h_ktm", [128, 1024], BF16)
            vtm = self.sb(es, "h_vtm", [128, 1024], BF16)
            gtm = self.sb(es, "h_gtm", [128, 1024], F32)
            ftm = self.sb(es, "h_ftm", [128, 1024], F32)
            for b in range(self.nb):
                self.fill_hTall(hTall, xts, rstd, b, layer)
                for d in (0, 1):
                    self.scan_reset()
                    for blk in SCAN_ORDER[d]:
                        ts = blk * 128
                        for g in range(2):
                            pq = self.ps()
                            pz = self.ps()
                            for hh in range(4):
                                c = g * 4 + hh
                                for kc in range(8):
                                    S.op("pe", lambda e, pq=pq, hh=hh, c=c, kc=kc, ts=ts: e.matmul(pq[:, hh * 128:(hh + 1) * 128], lhsT=win[:, kc, c * 128:(c + 1) * 128], rhs=hTall[:, kc, ts:ts + 128], start=(kc == 0), stop=(kc == 7)),
                                         reads=[win, hTall], writes=[pq])
                            for hh in range(4):
                                c = g * 4 + hh
                                for kc in range(8):
                                    S.op("pe", lambda e, pz=pz, hh=hh, c=c, kc=kc, ts=ts, d=d: e.matmul(pz[:, hh * 128:(hh + 1) * 128], lhsT=wf[d][:, kc, c * 128:(c + 1) * 128], rhs=hTall[:, kc, ts:ts + 128], start=(kc == 0), stop=(kc == 7)),
                                         reads=[wf[d], hTall], writes=[pz])
                            S.op("act", lambda e, pq=pq, g=g: e.activation(out=qf[:, g * 4:(g + 1) * 4, :].rearrange("p h t -> p (h t)"), in_=pq[:], func=AF.Silu), reads=[pq], writes=[qf])
                            S.op("act", lambda e, pz=pz, g=g: e.activation(out=sgn[:, g * 4:(g + 1) * 4, :].rearrange("p h t -> p (h t)"), in_=pz[:], func=AF.Sigmoid, scale=-1.0), reads=[pz], writes=[sgn])
                            for hh in range(4):
                                c = g * 4 + hh
                                S.op("pool", lambda e, c=c: e.tensor_scalar(out=kf[:, c, :], in0=sgn[:, c, :], scalar1=oml_f[:, c:c + 1], scalar2=None, op0=ALU.mult), reads=[sgn, oml_f], writes=[kf])
                        for half in range(2):
                            pf = self.ps()
                            for kc in range(8):
                                S.op("pe", lambda e, pf=pf, kc=kc, ts=ts, half=half, d=d: e.matmul(pf[:], lhsT=hTall[:, kc, ts:ts + 128], rhs=wf[d][:, kc, half * 512:(half + 1) * 512], start=(kc == 0), stop=(kc == 7)),
                                     reads=[wf[d], hTall], writes=[pf])
                            S.op("act", lambda e, pf=pf, half=half: e.activation(out=ftm[:, half * 512:(half + 1) * 512], in_=pf[:], func=AF.Sigmoid), reads=[pf], writes=[ftm])
                            pv = self.ps()
                            for kc in range(8):
                                S.op("pe", lambda e, pv=pv, kc=kc, ts=ts, half=half: e.matmul(pv[:], lhsT=hTall[:, kc, ts:ts + 128], rhs=win[:, kc, 1024 + half * 512:1024 + (half + 1) * 512], start=(kc == 0), stop=(kc == 7)),
                                     reads=[win, hTall], writes=[pv])
                            self.evac(half + 1, vtm[:, half * 512:(half + 1) * 512], pv[:], [pv], [vtm])
                        S.op("dve", lambda e: e.tensor_tensor(out=ftm[:], in0=ftm[:], in1=oml_row[:], op=ALU.mult), reads=[ftm, oml_row], writes=[ftm])
                        S.op("dve", lambda e: e.tensor_tensor(out=ftm[:], in0=ftm[:], in1=lb_row[:], op=ALU.add), reads=[ftm, lb_row], writes=[ftm])
                        S.op("act", lambda e: e.activation(out=gtm[:], in_=ftm[:], func=AF.Ln), reads=[ftm], writes=[gtm])
                        S.op("pool", lambda e: e.tensor_scalar(out=ktm[:], in0=ftm[:], scalar1=-1.0, scalar2=1.0, op0=ALU.mult, op1=ALU.add), reads=[ftm], writes=[ktm])
                        self.scan_block(d, qf, kf, ktm, vtm, gtm, self.OACC[b], blk, d == 0)
            S.barrier()
            S.flush()
        self.stage_out_part(layer, dr["hgrn_win"], 2048, dr["hgrn_wo"], "hgrn_gn", 8)


def host_mixers(inp, vp, com):
    host_attn(inp, vp, com)
    host_gla(inp, vp, com)
    host_hgrn(inp, vp, com)


def host_rwkv(inp, vp, com):
    g = lambda n: np.asarray(inp[n][0], np.float32)
    wrkv = g("rwkv_w_rkv")
    com["rw_wrkv"] = np.concatenate([wlay(wrkv[i]) for i in range(3)], axis=2)
    com["rw_wo"] = wlay(g("rwkv_w_o"))
    com["rw_wdn"] = wlay(np.concatenate([g("rwkv_w_down")[0], g("rwkv_w_down")[1], g("rwkv_a_down")[0], g("rwkv_a_down")[1]], axis=1))
    com["rw_gdn"] = wlay(g("rwkv_g_down"))
    com["rw_wup"] = np.ascontiguousarray(np.concatenate([g("rwkv_w_up").transpose(1, 0, 2), g("rwkv_w0")[None, :, :]], axis=0))
    com["rw_aup"] = np.ascontiguousarray(g("rwkv_a_up").transpose(1, 0, 2))
    com["rw_gup"] = g("rwkv_g_up")
    com["rw_consts"] = scan_consts(1.0)
    blk = np.zeros((128, 128), np.float32)
    blk[:64, :64] = 1.0
    blk[64:, 64:] = 1.0
    com["rw_blk"] = blk
    mix = g("rwkv_mix")
    vp.add("rw_mix", np.concatenate([fm_vec(mix[n]) for n in range(6)], axis=1))
    vp.add("rw_a0", np.concatenate([fm_vec(g("rwkv_a0")[d]) for d in range(2)], axis=1))
    vp.add("rw_kk", fm_vec(g("rwkv_k_k")))
    vp.add("rw_ka", fm_vec(g("rwkv_k_a")))
    vp.add("rw_rk", fm_vec(g("rwkv_r_k").reshape(-1)))
    vp.add("rw_lnw", fm_vec(g("rwkv_ln_w")))
    vp.add("rw_lnb", fm_vec(g("rwkv_ln_b")))


RW_DIRS = (0, 1)


class FullProg3(FullProg2):
    def declare_mixer_inputs(self):
        FullProg2.declare_mixer_inputs(self)
        if 2 in self.layers:
            self.din("rw_wrkv", [128, 8, 3072])
            self.din("rw_wo", [128, 8, 1024])
            self.din("rw_wdn", [128, 8, 256])
            self.din("rw_gdn", [128, 8, 128])
            self.din("rw_wup", [65, 2, 1024])
            self.din("rw_aup", [64, 2, 1024])
            self.din("rw_gup", [128, 1024])
            self.din("rw_consts", [128, 1536])
            self.din("rw_blk", [128, 128])

    def mm(self, out_ap, lhsT, rhs, reads, writes, start=True, stop=True, skip=False):
        if skip:
            self.S.op("pe", lambda e: e.matmul(out_ap, lhsT=lhsT, rhs=rhs, start=start, stop=stop, skip_group_check=True), reads=reads, writes=writes)
        else:
            self.S.op("pe", lambda e: e.matmul(out_ap, lhsT=lhsT, rhs=rhs, start=start, stop=stop), reads=reads, writes=writes)

    def stage_rwkv(self, layer):
        S = self.S
        dr = self.dram
        need_ctx = layer != self.last_layer
        f3 = lambda t: t[:].rearrange("p c t -> p (c t)")
        with ExitStack() as es:
            mk = lambda n, shp, dt: self.sb(es, "r_" + n, shp, dt)
            wrkv = mk("wrkv", [128, 8, 3072], BF16)
            wo = mk("wo", [128, 8, 1024], BF16)
            wdn = mk("wdn", [128, 8, 256], BF16)
            gdn = mk("gdn", [128, 8, 128], BF16)
            wup = mk("wup", [65, 2, 1024], BF16)
            aup = mk("aup", [64, 2, 1024], BF16)
            gup = mk("gup", [128, 1024], BF16)
            for kc in range(8):
                self.load_w(wrkv, wrkv[:, kc, :], dr["rw_wrkv"][:, kc, :])
            for w, n in ((wo, "rw_wo"), (wdn, "rw_wdn"), (gdn, "rw_gdn"), (wup, "rw_wup"), (aup, "rw_aup"), (gup, "rw_gup")):
                self.load_w(w, w[:], dr[n])
            sc = mk("sc", [128, 512], F32)
            blkf = mk("blkf", [128, 128], F32)
            blkb = mk("blkb", [128, 128], BF16)
            S.dma("sp", lambda e: e.dma_start(out=sc[:], in_=dr["rw_consts"][:, 0:512]), writes=[sc])
            S.dma("sp", lambda e: e.dma_start(out=blkf[:], in_=dr["rw_blk"][:, :]), writes=[blkf])
            S.op("dve", lambda e: e.tensor_copy(out=blkb[:], in_=blkf[:]), reads=[blkf], writes=[blkb])
            hTall = mk("hT", [128, 8, T], BF16)
            xts = [mk(f"x{i}", [128, 8, 128], F32) for i in range(2)]
            rstd = mk("rstd", [128, 128], F32)
            F = lambda n: mk(n, [128, 8, 128], F32)
            B = lambda n: mk(n, [128, 8, 128], BF16)
            dx = B("dx")
            hm0 = B("hm0")
            hm = [hm0, hm0, hm0]
            rT, kT, kk, ap_, bb, kd = B("rT"), B("kT"), B("kk"), B("ap"), B("bb"), B("kd")
            t1 = mk("t1", [65, 128], BF16)
            S.op("pool", lambda e: e.memset(t1[64:65, :], 1.0), writes=[t1])
            vtm = mk("vtm", [128, 1024], BF16)
            lwt = mk("lwt", [128, 1024], F32)
            e1, e2 = F("e1"), F("e2")
            ARt = mk("ARt", [128, 8, 256], BF16)
            Rt = B("Rt")
            Bt, Kt, Bet, Ket, Atb = B("Bt"), B("Kt"), B("Bet"), B("Ket"), B("Atb")
            gC = mk("gC", [128, 8, 2], F32)
            Betm, Ketm, Atm = mk("Betm", [128, 1024], BF16), mk("Ketm", [128, 1024], BF16), mk("Atm", [128, 1024], BF16)
            pads = {n: [mk(f"{n}pad{p}", [128, 128], BF16) for p in range(2)] for n in ("Be", "Ke", "V", "A", "U")}
            for n in pads:
                for p in range(2):
                    S.op("pool", lambda e, t=pads[n][p]: e.memset(t[:], 0.0), writes=[pads[n][p]])
            sq = lambda n: [mk(f"{n}{i}", [128, 128], BF16) for i in range(2)]
            sqf = lambda n: [mk(f"{n}{i}", [128, 128], F32) for i in range(2)]
            Pm, Qm, Wm = sqf("Pm"), sqf("Qm"), sqf("Wm")
            MrbT, LakT, MrkT = sq("MrbT"), sq("LakT"), sq("MrkT")
            AX = sqf("AX")
            PhiT = mk("PhiT", [128, 128], BF16)
            RhT = mk("RhT", [128, 128], BF16)
            Yh = mk("Yh", [128, 128], F32)
            Sb = mk("Sb", [128, 8, 128], BF16)
            obs = [F("ob0")] * 2
            ol = e2
            mix = lambda n, c: self.vcol("rw_mix", n * 8 + c)

            def make_hm(blk, n, dst):
                ts = blk * 128
                for c in range(8):
                    S.op("dve", lambda e, c=c: e.scalar_tensor_tensor(out=dst[:, c, :], in0=dx[:, c, :], scalar=mix(n, c), in1=hTall[:, c, ts:ts + 128], op0=ALU.mult, op1=ALU.add),
                         reads=[dx, hTall, self.vecs], writes=[dst])

            def make_dx(blk):
                ts = blk * 128
                first = blk in (0, 2)
                last = blk in (1, 17)
                lo = 1 if first else 0
                hi = 127 if last else 128
                S.op("pool", lambda e: e.tensor_tensor(out=dx[:, :, lo:hi], in0=hTall[:, :, ts + lo - 1:ts + hi - 1], in1=hTall[:, :, ts + lo + 1:ts + hi + 1], op=ALU.add), reads=[hTall], writes=[dx])
                if first:
                    S.op("pool", lambda e: e.tensor_copy(out=dx[:, :, 0:1], in_=hTall[:, :, ts + 1:ts + 2]), reads=[hTall], writes=[dx])
                if last:
                    S.op("pool", lambda e: e.tensor_copy(out=dx[:, :, 127:128], in_=hTall[:, :, ts + 126:ts + 127]), reads=[hTall], writes=[dx])
                S.op("dve", lambda e: e.scalar_tensor_tensor(out=dx[:], in0=dx[:], scalar=0.5, in1=hTall[:, :, ts:ts + 128], op0=ALU.mult, op1=ALU.subtract), reads=[dx, hTall], writes=[dx])

            def proj_fm(src, col0, dst, func=AF.Copy):
                for g in range(2):
                    p = self.ps()
                    for hh in range(4):
                        c = g * 4 + hh
                        for kc in range(8):
                            self.mm(p[:, hh * 128:(hh + 1) * 128], wrkv[:, kc, col0 + c * 128:col0 + (c + 1) * 128], src[:, kc, :], [wrkv, src], [p], start=(kc == 0), stop=(kc == 7))
                    self.evac(g, dst[:, g * 4:(g + 1) * 4, :].rearrange("p c t -> p (c t)"), p[:], [p], [dst])

            def lora_dn(src, col0, dst, func):
                p = self.ps()
                for kc in range(8):
                    self.mm(p[0:64, 0:128], wdn[:, kc, col0:col0 + 64], src[:, kc, :], [wdn, src], [p], start=(kc == 0), stop=(kc == 7))
                S.op("act", lambda e: e.activation(out=dst[0:64, :], in_=p[0:64, 0:128], func=func), reads=[p], writes=[dst])

            def a_prime(blk, d, dst):
                make_hm(blk, 4, hm[2])
                lora_dn(hm[2], 128 + d * 64, t1, AF.Copy)
                for g in range(2):
                    p = self.ps()
                    for hh in range(4):
                        c = g * 4 + hh
                        self.mm(p[:, hh * 128:(hh + 1) * 128], aup[:, d, c * 128:(c + 1) * 128], t1[0:64, :], [aup, t1], [p])
                    for hh in range(4):
                        c = g * 4 + hh
                        S.op("act", lambda e, c=c, hh=hh, p=p: e.activation(out=dst[:, c, :], in_=p[:, hh * 128:(hh + 1) * 128], func=AF.Sigmoid, bias=self.vcol("rw_a0", d * 8 + c), scale=1.0),
                             reads=[p, self.vecs], writes=[dst])

            def blocksum(src, dst_ps_tiles, fp32=True):
                for g in range(2):
                    p = dst_ps_tiles[g]
                    for hh in range(4):
                        c = g * 4 + hh
                        self.mm(p[:, hh * 128:(hh + 1) * 128], blkf[:] if fp32 else blkb[:], src[:, c, :], [blkf, blkb, src], [p])

            def common_inputs(blk):
                make_dx(blk)
                make_hm(blk, 0, hm[0])
                proj_fm(hm[0], 0, rT)
                make_hm(blk, 1, hm[1])
                proj_fm(hm[1], 1024, kT)
                for c in range(8):
                    S.op("dve" if c % 2 else "pool", lambda e, c=c: e.tensor_scalar(out=kk[:, c, :], in0=kT[:, c, :], scalar1=self.vcol("rw_kk", c), scalar2=None, op0=ALU.mult), reads=[kT, self.vecs], writes=[kk])
                S.op("pool", lambda e: e.tensor_tensor(out=e1[:], in0=kk[:], in1=kk[:], op=ALU.mult), reads=[kk], writes=[e1])
                pp = [self.ps(), self.ps()]
                blocksum(e1, pp)
                for g in range(2):
                    S.op("act", lambda e, g=g: e.activation(out=e2[:, g * 4:(g + 1) * 4, :].rearrange("p c t -> p (c t)"), in_=pp[g][:], func=AF.Sqrt), reads=[pp[g]], writes=[e2])
                S.op("dve", lambda e: e.tensor_scalar(out=e2[:], in0=e2[:], scalar1=1e-12, scalar2=None, op0=ALU.max), reads=[e2], writes=[e2])
                S.op("dve", lambda e: e.reciprocal(out=e2[:], in_=e2[:]), reads=[e2], writes=[e2])
                S.op("pool", lambda e: e.tensor_tensor(out=kk[:], in0=kk[:], in1=e2[:], op=ALU.mult), reads=[kk, e2], writes=[kk])

            def kfac(asrc, dst):
                for c in range(8):
                    S.op("dve", lambda e, c=c: e.tensor_scalar(out=dst[:, c, :], in0=asrc[:, c, :], scalar1=self.vcol("rw_ka", c), scalar2=oka[:, c:c + 1], op0=ALU.mult, op1=ALU.add),
                         reads=[asrc, self.vecs, oka], writes=[dst])
                S.op("pool", lambda e: e.tensor_tensor(out=dst[:], in0=dst[:], in1=kT[:], op=ALU.mult), reads=[dst, kT], writes=[dst])

            oka = mk("oka", [128, 8], F32)
            o_ka = self.voff["rw_ka"][0]
            S.op("dve", lambda e: e.tensor_scalar(out=oka[:], in0=self.vecs[:, o_ka:o_ka + 8], scalar1=-1.0, scalar2=1.0, op0=ALU.mult, op1=ALU.add), reads=[self.vecs], writes=[oka])

            obi = 0
            for b in range(self.nb):
                self.fill_hTall(hTall, xts, rstd, b, layer, W=128)
                OACC = self.OACC[b]
                for d in RW_DIRS:
                    S.op("pool", lambda e: e.memset(Sb[:], 0.0), writes=[Sb])
                    M_incl = sc[:, d * 128:(d + 1) * 128]
                    M_st = sc[:, 384:512] if d == 0 else sc[:, 256:384]
                    M_ts = sc[:, 256:384] if d == 0 else sc[:, 384:512]
                    for blk in SCAN_ORDER[d]:
                        ts = blk * 128
                        common_inputs(blk)
                        DB = (b == 0 and d == 0 and blk == 0)
                        if DB:
                            self.dbg("hT", hTall, hTall[:, :, 0:128], [128, 8, 128])
                            self.dbg("dx", dx, dx[:], [128, 8, 128])
                            self.dbg("rT", rT, rT[:], [128, 8, 128])
                            self.dbg("kT", kT, kT[:], [128, 8, 128])
                            self.dbg("kk", kk, kk[:], [128, 8, 128])
                        make_hm(blk, 2, hm[2])
                        for half in range(2):
                            p = self.ps()
                            for kc in range(8):
                                self.mm(p[:], hm[2][:, kc, :], wrkv[:, kc, 2048 + half * 512:2048 + (half + 1) * 512], [wrkv, hm[2]], [p], start=(kc == 0), stop=(kc == 7))
                            self.evac(half, vtm[:, half * 512:(half + 1) * 512], p[:], [p], [vtm])
                        make_hm(blk, 3, hm[2])
                        lora_dn(hm[2], d * 64, t1, AF.Tanh)
                        for half in range(2):
                            p = self.ps()
                            self.mm(p[:], t1[:], wup[:, d, half * 512:(half + 1) * 512], [t1, wup], [p])
                            S.op("act", lambda e, p=p, half=half: e.activation(out=lwt[:, half * 512:(half + 1) * 512], in_=p[:], func=AF.Sigmoid), reads=[p], writes=[lwt])
                        S.op("dve", lambda e: e.tensor_scalar(out=lwt[:], in0=lwt[:], scalar1=-0.6065306597126334, scalar2=None, op0=ALU.mult), reads=[lwt], writes=[lwt])
                        a_prime(blk, d, ap_)
                        kfac(ap_, kd)
                        S.op("pool", lambda e: e.tensor_tensor(out=bb[:], in0=kk[:], in1=ap_[:], op=ALU.mult), reads=[kk, ap_], writes=[bb])
                        if DB:
                            self.dbg("vtm", vtm, vtm[:], [128, 1024])
                            self.dbg("lwt", lwt, lwt[:], [128, 1024])
                            self.dbg("ap", ap_, ap_[:], [128, 8, 128])
                            self.dbg("kd", kd, kd[:], [128, 8, 128])
                            self.dbg("bb", bb, bb[:], [128, 8, 128])
                        pi = [self.ps(), self.ps()]
                        for c in range(8):
                            self.mm(pi[c // 4][:, (c % 4) * 128:(c % 4 + 1) * 128], lwt[:, c * 128:(c + 1) * 128], M_incl, [lwt, sc], [pi[c // 4]])
                        for g in range(2):
                            S.op("act", lambda e, g=g, pi=pi: e.activation(out=e1[:, g * 4:(g + 1) * 4, :].rearrange("p c t -> p (c t)"), in_=pi[g][:], func=AF.Exp), reads=[pi[g]], writes=[e1])
                            S.op("act", lambda e, g=g, pi=pi: e.activation(out=e2[:, g * 4:(g + 1) * 4, :].rearrange("p c t -> p (c t)"), in_=pi[g][:], func=AF.Exp, scale=-1.0), reads=[pi[g]], writes=[e2])
                        cols = (63, 127) if d == 0 else (0, 64)
                        for cc in range(2):
                            S.op("dve", lambda e, cc=cc, cols=cols: e.tensor_copy(out=gC[:, :, cc], in_=e1[:, :, cols[cc]]), reads=[e1], writes=[gC])
                        S.op("dve", lambda e: e.tensor_tensor(out=Rt[:], in0=rT[:], in1=e1[:], op=ALU.mult), reads=[rT, e1], writes=[Rt])
                        S.op("pool", lambda e: e.tensor_copy(out=ARt[:, :, 128:256], in_=Rt[:]), reads=[Rt], writes=[ARt])
                        S.op("dve", lambda e: e.tensor_tensor(out=Bt[:], in0=bb[:], in1=e2[:], op=ALU.mult), reads=[bb, e2], writes=[Bt])
                        S.op("pool", lambda e: e.tensor_tensor(out=Kt[:], in0=kd[:], in1=e2[:], op=ALU.mult), reads=[kd, e2], writes=[Kt])
                        pi = [self.ps(), self.ps()]
                        for c in range(8):
                            self.mm(pi[c // 4][:, (c % 4) * 128:(c % 4 + 1) * 128], lwt[:, c * 128:(c + 1) * 128], M_st, [lwt, sc], [pi[c // 4]])
                        for g in range(2):
                            S.op("act", lambda e, g=g, pi=pi: e.activation(out=e1[:, g * 4:(g + 1) * 4, :].rearrange("p c t -> p (c t)"), in_=pi[g][:], func=AF.Exp), reads=[pi[g]], writes=[e1])
                        S.op("dve", lambda e: e.scalar_tensor_tensor(out=Atb[:], in0=kk[:], scalar=-1.0, in1=e1[:], op0=ALU.mult, op1=ALU.mult), reads=[kk, e1], writes=[Atb])
                        S.op("pool", lambda e: e.tensor_copy(out=ARt[:, :, 0:128], in_=Atb[:]), reads=[Atb], writes=[ARt])
                        pi = [self.ps(), self.ps()]
                        for c in range(8):
                            self.mm(pi[c // 4][:, (c % 4) * 128:(c % 4 + 1) * 128], lwt[:, c * 128:(c + 1) * 128], M_ts, [lwt, sc], [pi[c // 4]])
                        for g in range(2):
                            S.op("act", lambda e, g=g, pi=pi: e.activation(out=e1[:, g * 4:(g + 1) * 4, :].rearrange("p c t -> p (c t)"), in_=pi[g][:], func=AF.Exp), reads=[pi[g]], writes=[e1])
                        S.op("dve", lambda e: e.tensor_tensor(out=Bet[:], in0=bb[:], in1=e1[:], op=ALU.mult), reads=[bb, e1], writes=[Bet])
                        S.op("pool", lambda e: e.tensor_tensor(out=Ket[:], in0=kd[:], in1=e1[:], op=ALU.mult), reads=[kd, e1], writes=[Ket])
                        for (src, dst) in ((Bet, Betm), (Ket, Ketm), (Atb, Atm)):
                            for c in range(8):
                                S.op("pe", lambda e, src=src, c=c: e.transpose(out=self.PSB[:, c * 128:(c + 1) * 128], in_=src[:, c, :], identity=self.ident_b[:]), reads=[src, self.ident_b], writes=[self.PSB])
                            S.op("act", lambda e, dst=dst: e.activation(out=dst[:], in_=self.PSB[:], func=AF.Copy), reads=[self.PSB], writes=[dst])
                        ob = obs[obi % 2]
                        obi += 1
                        for c in range(8):
                            for par in range(2):
                                hp = par * 64
                                for (nm, src) in (("Be", Betm), ("Ke", Ketm), ("V", vtm)):
                                    S.op("pool", lambda e, nm=nm, src=src, par=par, hp=hp, c=c: e.tensor_copy(out=pads[nm][par][:, hp:hp + 64], in_=src[:, c * 128 + hp:c * 128 + hp + 64]),
                                         reads=[src], writes=[pads[nm][par]])
                            for par in range(2):
                                hp = par * 64
                                i2 = par
                                p1, p2, p3 = self.ps(), self.ps(), self.ps()
                                self.mm(p1[:, 0:256], Bt[hp:hp + 64, c, :], ARt[hp:hp + 64, c, :], [Bt, ARt], [p1])
                                self.mm(p2[:, 0:256], Kt[hp:hp + 64, c, :], ARt[hp:hp + 64, c, :], [Kt, ARt], [p2])
                                self.mm(p3[:, 0:128], ARt[hp:hp + 64, c, 0:128], Bt[hp:hp + 64, c, :], [Bt, ARt], [p3])
                                P, Q, Wc = Pm[0], Qm[0], Wm[0]
                                S.op("dve", lambda e, p1=p1, P=P, M_st=M_st: e.tensor_tensor(out=P[:], in0=p1[:, 0:128], in1=M_st, op=ALU.mult), reads=[p1, sc], writes=[P])
                                S.op("dve", lambda e, p1=p1, t=MrbT[i2], M_incl=M_incl: e.tensor_tensor(out=t[:], in0=p1[:, 128:256], in1=M_incl, op=ALU.mult), reads=[p1, sc], writes=[MrbT[i2]])
                                S.op("dve", lambda e, p2=p2, t=LakT[i2], M_st=M_st: e.tensor_tensor(out=t[:], in0=p2[:, 0:128], in1=M_st, op=ALU.mult), reads=[p2, sc], writes=[LakT[i2]])
                                S.op("dve", lambda e, p2=p2, t=MrkT[i2], M_incl=M_incl: e.tensor_tensor(out=t[:], in0=p2[:, 128:256], in1=M_incl, op=ALU.mult), reads=[p2, sc], writes=[MrkT[i2]])
                                S.op("dve", lambda e, p3=p3, Q=Q, M_ts=M_ts: e.tensor_tensor(out=Q[:], in0=p3[:, 0:128], in1=M_ts, op=ALU.mult), reads=[p3, sc], writes=[Q])
                                S.op("pool", lambda e, Wc=Wc, P=P: e.tensor_tensor(out=Wc[:], in0=self.ident_f[:], in1=P[:], op=ALU.add), reads=[self.ident_f, P], writes=[Wc])
                                for lv in range(1, 6):
                                    Pn, Qn, Wn = Pm[lv % 2], Qm[lv % 2], Wm[lv % 2]
                                    pq = self.ps()
                                    self.mm(pq[:, 0:128], P[:], Q[:], [P, Q], [pq])
                                    if lv < 5:
                                        pp_ = self.ps()
                                        self.mm(pp_[:, 0:128], Q[:], P[:], [P, Q], [pp_])
                                        S.op("act", lambda e, pp_=pp_, Pn=Pn: e.activation(out=Pn[:], in_=pp_[:, 0:128], func=AF.Copy), reads=[pp_], writes=[Pn])
                                    S.op("dve", lambda e, pq=pq, Qn=Qn: e.tensor_copy(out=Qn[:], in_=pq[:, 0:128]), reads=[pq], writes=[Qn])
                                    pw = self.ps()
                                    self.mm(pw[:, 0:128], Qn[:], Wc[:], [Qn, Wc], [pw])
                                    S.op("dve", lambda e, pw=pw, Wn=Wn, Wc=Wc: e.tensor_tensor(out=Wn[:], in0=pw[:, 0:128], in1=Wc[:], op=ALU.add), reads=[pw, Wc], writes=[Wn])
                                    P, Q, Wc = Pn, Qn, Wn
                                ax = AX[par]
                                px = self.ps()
                                self.mm(px[:, 0:64], LakT[i2][:], pads["V"][par][:, hp:hp + 64], [LakT[i2], pads["V"][par]], [px])
                                S.op("act", lambda e, px=px, ax=ax: e.activation(out=ax[:, 64:128], in_=px[:, 0:64], func=AF.Copy), reads=[px], writes=[ax])
                                S.op("pool", lambda e, ax=ax, c=c, hp=hp: e.tensor_copy(out=ax[:, 0:64], in_=Atm[:, c * 128 + hp:c * 128 + hp + 64]), reads=[Atm], writes=[ax])
                                pau = self.ps()
                                self.mm(pau[:, 0:128], Wc[:], ax[:], [Wc, ax], [pau])
                                S.op("act", lambda e, pau=pau, par=par, c=c, hp=hp: e.activation(out=pads["A"][par][:, hp:hp + 64], in_=pau[:, 0:64], func=AF.Copy), reads=[pau], writes=[pads["A"][par]])
                                S.op("dve", lambda e, pau=pau, par=par, c=c, hp=hp: e.tensor_copy(out=pads["U"][par][:, hp:hp + 64], in_=pau[:, 64:128]), reads=[pau], writes=[pads["U"][par]])
                            pR = self.ps()
                            pYh = self.ps()
                            for par in range(2):
                                self.mm(pR[:, 0:128], pads["A"][par][:, :], MrbT[par][:], [pads["A"][par], MrbT[par]], [pR], start=(par == 0), stop=(par == 1))
                            for par in range(2):
                                self.mm(pYh[:, 0:128], pads["U"][par][:, :], MrbT[par][:], [pads["U"][par], MrbT[par]], [pYh], start=(par == 0), stop=False)
                                self.mm(pYh[:, 0:128], pads["V"][par][:, :], MrkT[par][:], [pads["V"][par], MrkT[par]], [pYh], start=False, stop=(par == 1))
                            S.op("dve", lambda e, pR=pR, c=c: e.tensor_tensor(out=RhT[:], in0=pR[:, 0:128], in1=Rt[:, c, :], op=ALU.add), reads=[pR, Rt], writes=[RhT])
                            S.op("act", lambda e, pYh=pYh: e.activation(out=Yh[:], in_=pYh[:, 0:128], func=AF.Copy), reads=[pYh], writes=[Yh])
                            pY = self.ps()
                            order = (0, 1) if d == 0 else (1, 0)
                            for cc in order:
                                lo, hi = cc * 64, (cc + 1) * 64
                                self.mm(pY[:, lo:hi], Sb[:, c, :], RhT[:, lo:hi], [Sb, RhT], [pY])
                                pPhi = self.ps()
                                for par in range(2):
                                    self.mm(pPhi[:, 0:128], pads["A"][par][lo:hi, :], pads["Be"][par][lo:hi, :], [pads["A"][par], pads["Be"][par]], [pPhi], start=(par == 0), stop=(par == 1))
                                S.op("dve", lambda e, pPhi=pPhi, c=c, cc=cc: e.scalar_tensor_tensor(out=PhiT[:], in0=self.ident_f[:], scalar=gC[:, c, cc:cc + 1], in1=pPhi[:, 0:128], op0=ALU.mult, op1=ALU.add),
                                     reads=[pPhi, gC, self.ident_f], writes=[PhiT])
                                pS = self.ps()
                                for par in range(2):
                                    self.mm(pS[:, 0:128], pads["Be"][par][lo:hi, :], pads["U"][par][lo:hi, :], [pads["Be"][par], pads["U"][par]], [pS], start=(par == 0), stop=False)
                                    self.mm(pS[:, 0:128], pads["Ke"][par][lo:hi, :], pads["V"][par][lo:hi, :], [pads["Ke"][par], pads["V"][par]], [pS], start=False, stop=False)
                                self.mm(pS[:, 0:128], PhiT[:], Sb[:, c, :], [PhiT, Sb], [pS], start=False, stop=True)
                                S.op("act", lambda e, pS=pS, c=c: e.activation(out=Sb[:, c, :], in_=pS[:, 0:128], func=AF.Copy), reads=[pS], writes=[Sb])
                            S.op("dve", lambda e, pY=pY, c=c, ob=ob: e.tensor_tensor(out=ob[:, c, :], in0=pY[:, 0:128], in1=Yh[:], op=ALU.add), reads=[pY, Yh], writes=[ob])
                        if DB:
                            self.dbg("ob", ob, ob[:], [128, 8, 128])
                            self.dbg("Rt", Rt, Rt[:], [128, 8, 128])
                            self.dbg("Bt", Bt, Bt[:], [128, 8, 128])
                            self.dbg("Kt", Kt, Kt[:], [128, 8, 128])
                            self.dbg("Atb", Atb, Atb[:], [128, 8, 128])
                            self.dbg("Bet", Bet, Bet[:], [128, 8, 128])
                            self.dbg("Sb", Sb, Sb[:], [128, 8, 128])
                        if d == 1 and len(RW_DIRS) == 2:
                            S.dma("sp", lambda e, blk=blk, OACC=OACC: e.dma_start(out=ol[:], in_=OACC.t[blk]), reads=[OACC], writes=[ol])
                            S.op("pool", lambda e, ob=ob: e.tensor_tensor(out=ob[:], in0=ob[:], in1=ol[:], op=ALU.add), reads=[ob, ol], writes=[ob])
                        S.dma("sp", lambda e, blk=blk, ob=ob, OACC=OACC: e.dma_start(out=OACC.t[blk], in_=ob[:]), reads=[ob], writes=[OACC])
                for blk in range(18):
                    lat = blk >= 2
                    if not lat and not need_ctx:
                        continue
                    col = b if lat else 4
                    ts = blk * 128
                    xt = xts[blk % 2]
                    self.load_x(xt, b, ts, 128)
                    common_inputs(blk)
                    a_prime(blk, 0, ap_)
                    a_prime(blk, 1, bb)
                    S.op("dve", lambda e: e.tensor_tensor(out=ap_[:], in0=ap_[:], in1=bb[:], op=ALU.add), reads=[ap_, bb], writes=[ap_])
                    S.op("dve", lambda e: e.tensor_scalar(out=ap_[:], in0=ap_[:], scalar1=0.5, scalar2=None, op0=ALU.mult), reads=[ap_], writes=[ap_])
                    kfac(ap_, kd)
                    S.op("pool", lambda e: e.tensor_tensor(out=kd[:], in0=kd[:], in1=rT[:], op=ALU.mult), reads=[kd, rT], writes=[kd])
                    for c in range(8):
                        S.op("dve", lambda e, c=c: e.tensor_scalar(out=kd[:, c, :], in0=kd[:, c, :], scalar1=self.vcol("rw_rk", c), scalar2=None, op0=ALU.mult), reads=[kd, self.vecs], writes=[kd])
                    pp = [self.ps(), self.ps()]
                    blocksum(kd, pp, fp32=False)
                    make_hm(blk, 2, hm[2])
                    proj_fm(hm[2], 2048, Rt)
                    for g in range(2):
                        S.op("dve", lambda e, g=g, pp=pp: e.tensor_tensor(out=bb[:, g * 4:(g + 1) * 4, :].rearrange("p c t -> p (c t)"), in0=pp[g][:], in1=Rt[:, g * 4:(g + 1) * 4, :].rearrange("p c t -> p (c t)"), op=ALU.mult),
                             reads=[pp[g], Rt], writes=[bb])
                    S.dma("sp", lambda e, blk=blk, OACC=OACC: e.dma_start(out=ol[:], in_=OACC.t[blk]), reads=[OACC], writes=[ol])
                    pm = [self.ps(), self.ps()]
                    blocksum(ol, pm)
                    for g in range(2):
                        S.op("dve", lambda e, g=g, pm=pm: e.scalar_tensor_tensor(out=e1[:, g * 4:(g + 1) * 4, :].rearrange("p c t -> p (c t)"), in0=pm[g][:], scalar=-1.0 / 64.0, in1=ol[:, g * 4:(g + 1) * 4, :].rearrange("p c t -> p (c t)"),
                                                                         op0=ALU.mult, op1=ALU.add), reads=[pm[g], ol], writes=[e1])
                    S.op("pool", lambda e: e.tensor_tensor(out=e2[:], in0=e1[:], in1=e1[:], op=ALU.mult), reads=[e1], writes=[e2])
                    pv = [self.ps(), self.ps()]
                    blocksum(e2, pv)
                    for g in range(2):
                        S.op("act", lambda e, g=g, pv=pv: e.activation(out=e2[:, g * 4:(g + 1) * 4, :].rearrange("p c t -> p (c t)"), in_=pv[g][:], func=AF.Sqrt, bias=self.epsb2[:, 0:1], scale=1.0 / 64.0), reads=[pv[g], self.epsb2], writes=[e2])
                    S.op("dve", lambda e: e.reciprocal(out=e2[:], in_=e2[:]), reads=[e2], writes=[e2])
                    S.op("pool", lambda e: e.tensor_tensor(out=e1[:], in0=e1[:], in1=e2[:], op=ALU.mult), reads=[e1, e2], writes=[e1])
                    for c in range(8):
                        S.op("dve", lambda e, c=c: e.tensor_scalar(out=e1[:, c, :], in0=e1[:, c, :], scalar1=self.vcol("rw_lnw", c), scalar2=self.vcol("rw_lnb", c), op0=ALU.mult, op1=ALU.add), reads=[e1, self.vecs], writes=[e1])
                    S.op("pool", lambda e: e.tensor_tensor(out=e1[:], in0=e1[:], in1=bb[:], op=ALU.add), reads=[e1, bb], writes=[e1])
                    make_hm(blk, 5, hm[2])
                    pg = self.ps()
                    for kc in range(8):
                        self.mm(pg[:, 0:128], gdn[:, kc, :], hm[2][:, kc, :], [gdn, hm[2]], [pg], start=(kc == 0), stop=(kc == 7))
                    S.op("act", lambda e, pg=pg: e.activation(out=PhiT[:], in_=pg[:, 0:128], func=AF.Sigmoid), reads=[pg], writes=[PhiT])
                    for g in range(2):
                        p = self.ps()
                        for hh in range(4):
                            c = g * 4 + hh
                            self.mm(p[:, hh * 128:(hh + 1) * 128], gup[:, c * 128:(c + 1) * 128], PhiT[:], [gup, PhiT], [p])
                        S.op("dve", lambda e, g=g, p=p: e.tensor_tensor(out=Bt[:, g * 4:(g + 1) * 4, :].rearrange("p c t -> p (c t)"), in0=p[:], in1=e1[:, g * 4:(g + 1) * 4, :].rearrange("p c t -> p (c t)"), op=ALU.mult),
                             reads=[p, e1], writes=[Bt])
                    for g in range(2):
                        p = self.ps()
                        for hh in range(4):
                            c = g * 4 + hh
                            for kc in range(8):
                                self.mm(p[:, hh * 128:(hh + 1) * 128], wo[:, kc, c * 128:(c + 1) * 128], Bt[:, kc, :], [wo, Bt], [p], start=(kc == 0), stop=(kc == 7))
                        self.evac(g, e2[:, g * 4:(g + 1) * 4, :].rearrange("p c t -> p (c t)"), p[:], [p], [e2])
                    self.rstd_of(e2, 128, rstd)
                    self.resid(xt, 128, e2, rstd, layer, 2, col)
                    self.store_x(xt, b, ts, 128)
            S.barrier()
            S.flush()


def host_mixers(inp, vp, com):
    host_attn(inp, vp, com)
    host_gla(inp, vp, com)
    host_hgrn(inp, vp, com)
    host_rwkv(inp, vp, com)


def kernel(**inputs):
    out, _, _ = run(inputs, 4, NCORES, prog_cls=FullProg3)
    return out
```
